# Optimizing a Trainium2 kernel written in Bass

```python
import math
import jax, jax.numpy as jnp
from jax import lax
import numpy as np

D_MODEL = 1024
BATCH = 2
SEQ = 16384
DEPTH = 4
DEC_BATCH = 32
DEC_SEQ = 2048
PAST_LEN = 128

N_MIXERS = 3
N_A = (DEPTH + 2) // 3
N_B = (DEPTH + 1) // 3
N_C = DEPTH // 3
FNET_GROUPS = 4
FNET_GROUP_DIM = D_MODEL // FNET_GROUPS
HEAD_DIM = 64
SWA_Q_HEADS = D_MODEL // HEAD_DIM
SWA_KV_HEADS = 4
SWA_GROUP = SWA_Q_HEADS // SWA_KV_HEADS
WINDOW = 128
BLOCK = 128
DIFF_HEADS = D_MODEL // (2 * HEAD_DIM)
D_DIFF = DIFF_HEADS * 2 * HEAD_DIM
D_FF = 2816
CONV_WIDTH = 3
ROPE_THETA = 10000.0
EPS = 1e-6
NEG = -1e30

kernel_name = "hybrid_fnet_swa_diffattn_encoder"


def rms_norm(x, g):
    xf = x.astype(jnp.float32)
    y = xf * lax.rsqrt(jnp.mean(xf * xf, axis=-1, keepdims=True) + EPS)
    return (y * g.astype(jnp.float32)).astype(x.dtype)


def rope_tables(seq):
    inv = 1.0 / (ROPE_THETA ** (jnp.arange(0, HEAD_DIM, 2, dtype=jnp.float32) / HEAD_DIM))
    ang = jnp.arange(seq, dtype=jnp.float32)[:, None] * inv[None, :]
    return jnp.cos(ang), jnp.sin(ang)


def apply_rope(x, cos, sin):
    half = HEAD_DIM // 2
    shp = (1, cos.shape[0]) + (1,) * (x.ndim - 3) + (half,)
    c = cos.reshape(shp)
    s = sin.reshape(shp)
    xf = x.astype(jnp.float32)
    x1, x2 = xf[..., :half], xf[..., half:]
    return jnp.concatenate([x1 * c - x2 * s, x2 * c + x1 * s], axis=-1).astype(x.dtype)


def fourier_mixer(h, w_o, b_o):
    B, S, D = h.shape
    hg = h.astype(jnp.float32).reshape(B, S, FNET_GROUPS, FNET_GROUP_DIM)
    f = jnp.fft.fftn(hg, axes=(1, 3), norm="ortho").real
    return f.reshape(B, S, D).astype(h.dtype) @ w_o + b_o


def windowed_gqa(h, w_qkv, q_g, k_g, sink, w_o, cos, sin):
    B, S, _ = h.shape
    nq = SWA_Q_HEADS * HEAD_DIM
    nkv = SWA_KV_HEADS * HEAD_DIM
    qkv = h @ w_qkv
    q = qkv[..., :nq].reshape(B, S, SWA_KV_HEADS, SWA_GROUP, HEAD_DIM)
    k = qkv[..., nq:nq + nkv].reshape(B, S, SWA_KV_HEADS, HEAD_DIM)
    v = qkv[..., nq + nkv:].reshape(B, S, SWA_KV_HEADS, HEAD_DIM)
    q = apply_rope(rms_norm(q, q_g), cos, sin)
    k = apply_rope(rms_norm(k, k_g), cos, sin)
    pad = ((0, 0), (BLOCK, BLOCK), (0, 0), (0, 0))
    kp = jnp.pad(k, pad)
    vp = jnp.pad(v, pad)
    nb = S // BLOCK
    qb = jnp.moveaxis(q.reshape(B, nb, BLOCK, SWA_KV_HEADS, SWA_GROUP, HEAD_DIM), 1, 0)
    sink_f = sink.astype(jnp.float32).reshape(SWA_KV_HEADS, SWA_GROUP)[None, :, :, None, None]
    scale = HEAD_DIM ** -0.5

    def block(args):
        n, q_blk = args
        start = n * BLOCK
        k_blk = lax.dynamic_slice_in_dim(kp, start, 3 * BLOCK, axis=1)
        v_blk = lax.dynamic_slice_in_dim(vp, start, 3 * BLOCK, axis=1)
        s = jnp.einsum('bqhgd,bkhd->bhgqk', q_blk, k_blk).astype(jnp.float32) * scale
        qi = start + jnp.arange(BLOCK)
        kj = start - BLOCK + jnp.arange(3 * BLOCK)
        valid = (jnp.abs(qi[:, None] - kj[None, :]) <= WINDOW) & (kj >= 0)[None, :] & (kj < S)[None, :]
        s = jnp.where(valid, s, NEG)
        m = jnp.maximum(jnp.max(s, axis=-1, keepdims=True), sink_f)
        e = jnp.exp(s - m)
        denom = jnp.sum(e, axis=-1, keepdims=True) + jnp.exp(sink_f - m)
        p = (e / denom).astype(v_blk.dtype)
        return jnp.einsum('bhgqk,bkhd->bqhgd', p, v_blk)

    o = lax.map(block, (jnp.arange(nb), qb))
    o = jnp.moveaxis(o, 0, 1).reshape(B, S, nq)
    return o @ w_o


def diff_attention(h, w_qkv, q_g, k_g, lq1, lk1, lq2, lk2, subln_g, w_o, cos, sin, lambda_init):
    B, S, _ = h.shape
    nqk = DIFF_HEADS * 2 * HEAD_DIM
    qkv = h @ w_qkv
    q = qkv[..., :nqk].reshape(B, S, DIFF_HEADS, 2, HEAD_DIM)
    k = qkv[..., nqk:2 * nqk].reshape(B, S, DIFF_HEADS, 2, HEAD_DIM)
    v = qkv[..., 2 * nqk:].reshape(B, S, DIFF_HEADS, 2 * HEAD_DIM)
    q = apply_rope(rms_norm(q, q_g), cos, sin)
    k = apply_rope(rms_norm(k, k_g), cos, sin)
    f32 = jnp.float32
    lam = (jnp.exp(jnp.sum(lq1.astype(f32) * lk1.astype(f32)))
           - jnp.exp(jnp.sum(lq2.astype(f32) * lk2.astype(f32))) + lambda_init)
    nb = S // BLOCK
    qb = jnp.moveaxis(q.reshape(B, nb, BLOCK, DIFF_HEADS, 2, HEAD_DIM), 1, 0)
    scale = HEAD_DIM ** -0.5

    def block(q_blk):
        s = jnp.einsum('bqhcd,bkhcd->bhcqk', q_blk, k).astype(f32) * scale
        p = jax.nn.softmax(s, axis=-1)
        a = (p[:, :, 0] - lam * p[:, :, 1]).astype(v.dtype)
        return jnp.einsum('bhqk,bkhe->bqhe', a, v)

    o = lax.map(block, qb)
    o = jnp.moveaxis(o, 0, 1).reshape(B, S, DIFF_HEADS, 2 * HEAD_DIM)
    o = rms_norm(o, subln_g) * (1.0 - lambda_init)
    return o.reshape(B, S, D_DIFF) @ w_o


def conv_glu_ffn(h, w_gate, w_up, conv_w, conv_b, w_down):
    g = h @ w_gate
    u = h @ w_up
    gp = jnp.pad(g, ((0, 0), (1, 1), (0, 0)))
    g = gp[:, :-2] * conv_w[0] + gp[:, 1:-1] * conv_w[1] + gp[:, 2:] * conv_w[2] + conv_b
    return (jax.nn.silu(g) * u) @ w_down


def _trunk(x, c, p):
    S = x.shape[1]
    cos, sin = rope_tables(S)
    c_act = jax.nn.silu(c)
    for i in range(DEPTH):
        mod = c_act @ p["ada_w"][i] + p["ada_b"][i]
        sh1, sc1, g1, sh2, sc2, g2 = jnp.split(mod[:, None, :], 6, axis=-1)
        h = rms_norm(x, p["norm1_g"][i]) * (1 + sc1) + sh1
        kind, j = i % N_MIXERS, i // N_MIXERS
        if kind == 0:
            y = fourier_mixer(h, p["fnet_w"][j], p["fnet_b"][j])
        elif kind == 1:
            y = windowed_gqa(h, p["swa_w_qkv"][j], p["swa_q_g"][j], p["swa_k_g"][j],
                             p["swa_sink"][j], p["swa_w_o"][j], cos, sin)
        else:
            lambda_init = 0.8 - 0.6 * math.exp(-0.3 * i)
            y = diff_attention(h, p["diff_w_qkv"][j], p["diff_q_g"][j], p["diff_k_g"][j],
                               p["diff_lq1"][j], p["diff_lk1"][j], p["diff_lq2"][j], p["diff_lk2"][j],
                               p["diff_subln_g"][j], p["diff_w_o"][j], cos, sin, lambda_init)
        x = x + g1 * y
        h = rms_norm(x, p["norm2_g"][i]) * (1 + sc2) + sh2
        x = x + g2 * conv_glu_ffn(h, p["ffn_w_gate"][i], p["ffn_w_up"][i], p["ffn_conv_w"][i],
                                  p["ffn_conv_b"][i], p["ffn_w_down"][i])
    return x


def setup_inputs(seed: int = 0) -> dict:
    key = jax.random.key(seed)
    ks = jax.random.split(key, 40)
    cnt = [0]

    def nrm(shape, scale):
        k = ks[cnt[0]]
        cnt[0] += 1
        return jax.random.normal(k, shape, jnp.float32) * scale

    D = D_MODEL
    qkv_swa = (SWA_Q_HEADS + 2 * SWA_KV_HEADS) * HEAD_DIM
    qkv_diff = 3 * D_DIFF
    nq = SWA_Q_HEADS * HEAD_DIM
    return {
        "x_prompt": nrm((BATCH, SEQ, D), 1.0),
        "x_sample": nrm((DEC_BATCH, DEC_SEQ, D), 1.0),
        "c_prompt": nrm((BATCH, D), 1.0),
        "c_sample": nrm((DEC_BATCH, D), 1.0),
        "ada_w": nrm((DEPTH, D, 6 * D), 0.5 * D ** -0.5),
        "ada_b": nrm((DEPTH, 6 * D), 0.02),
        "norm1_g": 1.0 + nrm((DEPTH, D), 0.02),
        "norm2_g": 1.0 + nrm((DEPTH, D), 0.02),
        "fnet_w": nrm((N_A, D, D), D ** -0.5),
        "fnet_b": nrm((N_A, D), 0.02),
        "swa_w_qkv": nrm((N_B, D, qkv_swa), D ** -0.5),
        "swa_q_g": 1.0 + nrm((N_B, HEAD_DIM), 0.02),
        "swa_k_g": 1.0 + nrm((N_B, HEAD_DIM), 0.02),
        "swa_sink": nrm((N_B, SWA_Q_HEADS), 0.5),
        "swa_w_o": nrm((N_B, nq, D), nq ** -0.5),
        "diff_w_qkv": nrm((N_C, D, qkv_diff), D ** -0.5),
        "diff_q_g": 1.0 + nrm((N_C, HEAD_DIM), 0.02),
        "diff_k_g": 1.0 + nrm((N_C, HEAD_DIM), 0.02),
        "diff_lq1": nrm((N_C, HEAD_DIM), 0.1),
        "diff_lk1": nrm((N_C, HEAD_DIM), 0.1),
        "diff_lq2": nrm((N_C, HEAD_DIM), 0.1),
        "diff_lk2": nrm((N_C, HEAD_DIM), 0.1),
        "diff_subln_g": 1.0 + nrm((N_C, 2 * HEAD_DIM), 0.02),
        "diff_w_o": nrm((N_C, D_DIFF, D), D_DIFF ** -0.5),
        "ffn_w_gate": nrm((DEPTH, D, D_FF), D ** -0.5),
        "ffn_w_up": nrm((DEPTH, D, D_FF), D ** -0.5),
        "ffn_conv_w": nrm((DEPTH, CONV_WIDTH, D_FF), CONV_WIDTH ** -0.5),
        "ffn_conv_b": nrm((DEPTH, D_FF), 0.02),
        "ffn_w_down": nrm((DEPTH, D_FF, D), D_FF ** -0.5),
    }


def reference(x_prompt, x_sample, c_prompt, c_sample, ada_w, ada_b, norm1_g, norm2_g,
              fnet_w, fnet_b, swa_w_qkv, swa_q_g, swa_k_g, swa_sink, swa_w_o,
              diff_w_qkv, diff_q_g, diff_k_g, diff_lq1, diff_lk1, diff_lq2, diff_lk2,
              diff_subln_g, diff_w_o, ffn_w_gate, ffn_w_up, ffn_conv_w, ffn_conv_b, ffn_w_down):
    p = {
        "ada_w": ada_w, "ada_b": ada_b, "norm1_g": norm1_g, "norm2_g": norm2_g,
        "fnet_w": fnet_w, "fnet_b": fnet_b,
        "swa_w_qkv": swa_w_qkv, "swa_q_g": swa_q_g, "swa_k_g": swa_k_g,
        "swa_sink": swa_sink, "swa_w_o": swa_w_o,
        "diff_w_qkv": diff_w_qkv, "diff_q_g": diff_q_g, "diff_k_g": diff_k_g,
        "diff_lq1": diff_lq1, "diff_lk1": diff_lk1, "diff_lq2": diff_lq2, "diff_lk2": diff_lk2,
        "diff_subln_g": diff_subln_g, "diff_w_o": diff_w_o,
        "ffn_w_gate": ffn_w_gate, "ffn_w_up": ffn_w_up, "ffn_conv_w": ffn_conv_w,
        "ffn_conv_b": ffn_conv_b, "ffn_w_down": ffn_w_down,
    }
    y_prompt = _trunk(x_prompt, c_prompt, p)
    y_sample = _trunk(x_sample, c_sample, p)
    return (y_prompt, y_sample)
```

```python
import math
from contextlib import ExitStack
import numpy as np
import ml_dtypes
import concourse.bass as bass
import concourse.mybir as mybir
from concourse.bass_utils import run_bass_kernel_spmd

F32 = mybir.dt.float32
BF16 = mybir.dt.bfloat16
AF = mybir.ActivationFunctionType
ALU = mybir.AluOpType
D = 1024
DFF = 2816
NFC = 22
EPS = 1e-6
EPOCH = 30000
RING = 12
SLAB32 = 49152


class Buf:
    __slots__ = ("name", "w", "r")

    def __init__(self, name=""):
        self.name = name
        self.w = None
        self.r = {}


class T:
    __slots__ = ("ap", "b")

    def __init__(self, ap, b):
        self.ap = ap
        self.b = b


class Sched:
    ENG = ("pe", "act", "dve", "pool", "sp")

    def __init__(self, nc):
        self.nc = nc
        self.ops = {e: [] for e in self.ENG}
        self.cnt = {e: 0 for e in self.ENG}
        self.seen = {e: {} for e in self.ENG}
        self.ring_pos = {}
        self.ring_tok = {}
        self.keys = set()
        self.stack = ExitStack()

    def _waits(self, eng, reads, writes, extra=()):
        seen = self.seen[eng]
        waits = {}

        def need(tok):
            if tok is None:
                return
            k, v = tok
            if eng == "pe" and k[0] == "pe":
                return
            if seen.get(k, 0) >= v:
                return
            if waits.get(k, 0) < v:
                waits[k] = v

        for b in reads:
            need(b.w)
        for b in writes:
            need(b.w)
            for tok in b.r.values():
                need(tok)
        for tok in extra:
            need(tok)
        for k, v in waits.items():
            seen[k] = v
        return list(waits.items())

    def op(self, eng, fn, reads=(), writes=()):
        waits = self._waits(eng, reads, writes)
        cnt = self.cnt[eng]
        self.cnt[eng] = cnt + 1
        epoch, idx = divmod(cnt, EPOCH)
        key = (eng, epoch)
        self.keys.add(key)
        tok = (key, idx + 1)
        self.ops[eng].append((fn, waits, key, 1))
        for b in reads:
            b.r[eng] = tok
        for b in writes:
            b.w = tok
            b.r = {}
        return tok

    def dma(self, q, fn, reads=(), writes=()):
        pos = self.ring_pos.get(q, 0)
        self.ring_pos[q] = (pos + 1) % RING
        key = ("dma", q, pos)
        self.keys.add(key)
        prev = self.ring_tok.get(key)
        waits = self._waits(q, reads, writes, extra=(prev,) if prev else ())
        tok = (key, (prev[1] if prev else 0) + 16)
        self.ring_tok[key] = tok
        self.ops[q].append((fn, waits, key, 16))
        for b in reads:
            b.r[key] = tok
        for b in writes:
            b.w = tok
            b.r = {}
        return tok

    def last_tokens(self):
        toks = list(self.ring_tok.values())
        for e in self.ENG:
            c = self.cnt[e]
            if c:
                epoch, idx = divmod(c - 1, EPOCH)
                toks.append(((e, epoch), idx + 1))
        return toks

    def barrier(self):
        toks = self.last_tokens()
        for e in self.ENG:
            waits = self._waits(e, (), (), extra=toks)
            if waits:
                self.ops[e].append((None, waits, None, 0))

    def emit(self):
        nc = self.nc
        sems = {}
        for i, k in enumerate(sorted(self.keys, key=str)):
            sems[k] = self.stack.enter_context(nc.semaphore(f"s{i}"))

        def replay(e, name):
            for fn, waits, key, inc in self.ops[name]:
                for k, v in waits:
                    e.wait_ge(sems[k], v)
                if fn is not None:
                    fn(e).then_inc(sems[key], inc)

        with nc.Block() as block:
            @block.sync
            def _(e):
                replay(e, "sp")

            @block.tensor
            def _(e):
                replay(e, "pe")

            @block.scalar
            def _(e):
                replay(e, "act")

            @block.vector
            def _(e):
                replay(e, "dve")

            @block.gpsimd
            def _(e):
                replay(e, "pool")


def I_mm(out, lhsT, rhs, start, stop):
    return lambda e: e.matmul(out, lhsT=lhsT, rhs=rhs, start=start, stop=stop)


def I_tr(out, in_, ident):
    return lambda e: e.transpose(out, in_, ident)


def I_act(out, in_, func, scale=None, bias=None):
    kw = {}
    if scale is not None:
        kw["scale"] = scale
    if bias is not None:
        kw["bias"] = bias
    return lambda e: e.activation(out=out, in_=in_, func=func, **kw)


def I_ts(out, in0, s1, op0, s2=None, op1=None):
    if op1 is None:
        return lambda e: e.tensor_scalar(out=out, in0=in0, scalar1=s1, scalar2=None, op0=op0)
    return lambda e: e.tensor_scalar(out=out, in0=in0, scalar1=s1, scalar2=s2, op0=op0, op1=op1)


def I_tt(out, in0, in1, op):
    return lambda e: e.tensor_tensor(out=out, in0=in0, in1=in1, op=op)


def I_stt(out, in0, scalar, in1, op0, op1):
    return lambda e: e.scalar_tensor_tensor(out=out, in0=in0, scalar=scalar, in1=in1, op0=op0, op1=op1)


def I_cp(out, in_):
    return lambda e: e.tensor_copy(out=out, in_=in_)


def I_rec(out, in_):
    return lambda e: e.reciprocal(out=out, in_=in_)


def I_ms(ap, v):
    return lambda e: e.memset(ap, v)


def I_dma(out, in_, slow=False):
    if slow:
        return lambda e: e.dma_start(out=out, in_=in_, allow_slow_non_contiguous=True)
    return lambda e: e.dma_start(out=out, in_=in_)


class _Stop(Exception):
    pass


def build(cfg):
    SP, SS, NSS, NL = cfg["SP"], cfg["SS"], cfg["NSS"], cfg["NL"]
    NLW = cfg.get("NLW", 4)
    STOP = cfg.get("STOP", None)
    QP = SP // 4
    N2 = SP // 128
    NT = QP + NSS * SS
    NS = 1 + NSS
    NTS = SS // 128
    NKG = SS // 512
    segs = [(0, QP, 0)] + [(QP + i * SS, SS, 1 + i) for i in range(NSS)]
    groups = [(s0 + g * 512, s0, sl, sq) for (s0, sl, sq) in segs for g in range(sl // 512)]
    nc = bass.Bass("TRN2", target_bir_lowering=False)

    def din(name, shape, dt=F32):
        return nc.dram_tensor(name, list(shape), dt, kind="ExternalInput").ap()

    def dsc(name, shape, dt):
        return nc.dram_tensor(name, list(shape), dt).ap()

    xin = din("xin", [NT, D])
    cT_in = din("cT", [128, 8 * NS])
    ada_w = din("ada_w", [NLW, D, 6 * D])
    ada_bT = din("ada_bT", [128, 4 * 48])
    ng_in = din("ng", [128, 64])
    fnet_w = din("fnet_w", [2, D, D])
    fnet_b = din("fnet_b", [2, D])
    swa_w_qkv = din("swa_w_qkv", [1, D, 1536])
    swa_g = din("swa_g", [128, 2])
    swa_sink = din("swa_sink", [128, 16])
    swa_w_o = din("swa_w_o", [1, D, D])
    diff_w_qkv = din("diff_w_qkv", [1, D, 3072])
    diff_g = din("diff_g", [128, 2])
    diff_l = din("diff_l", [64, 4])
    diff_subln = din("diff_subln", [128, 1])
    diff_w_o = din("diff_w_o", [1, D, D])
    ffn_w_gate = din("ffn_w_gate", [NLW, D, DFF])
    ffn_w_up = din("ffn_w_up", [NLW, D, DFF])
    ffn_convT = din("ffn_convT", [128, 4 * NFC * 4])
    ffn_w_down = din("ffn_w_down", [NLW, DFF, D])
    ropeC = din("ropeC", [128, NT])
    ropeS = din("ropeS", [128, NT])
    chCS = din("chCS", [128, 2 * 512], BF16)
    tabS = din("tabS", [NKG, 128, NTS * 2 * 512], BF16)
    tabP1 = din("tabP1", [128, N2, 3 * N2], BF16)
    tabP2 = din("tabP2", [128, 64], BF16)
    swamask = din("swamask", [128, 2 * 256], BF16)
    sel_in = din("sel", [128, 12])
    ident_in = din("ident", [128, 128])
    c16_in = din("c16", [128, 1152], BF16)
    yout = nc.dram_tensor("yout", [NT, D], F32, kind="ExternalOutput").ap()

    XTa = dsc("XTa", [8, 128, NT], F32)
    XTb = dsc("XTb", [8, 128, NT], F32)
    MTd = dsc("MTd", [8, 128, NT], BF16)
    gu16 = dsc("gu16", [max(NL, 1), NFC, 128, 2048], BF16)
    wd16 = dsc("wd16", [max(NL, 1), DFF, D], BF16)
    MQ = QP // 128
    zin_c = [dsc(f"zin{i}", [8 * MQ, 2048], BF16) for i in range(16)]
    zall_c = [dsc(f"zall{i}", [4 * 8 * MQ, 2048], BF16) for i in range(16)]
    zs = dsc("zs", [max(NSS * SS, 128), 2048], BF16)
    vbuf = dsc("vbuf", [128, N2, 2048], BF16)
    QT = dsc("QT", [8, 128, NT], BF16)
    KT = dsc("KT", [8, 128, NT], BF16)
    kin_h = [dsc(f"kin{i}", [128, QP], BF16) for i in range(8)]
    kall_h = [dsc(f"kall{i}", [4 * 128, QP], BF16) for i in range(8)]
    Vt = dsc("Vt", [NT, 1024], BF16)
    vin_h = [dsc(f"vin{i}", [QP, 128], BF16) for i in range(8)]
    vall_h = [dsc(f"vall{i}", [SP, 128], BF16) for i in range(8)]
    sbin = dsc("sbin", [2 * 128, 1024], BF16)
    sball = dsc("sball", [4 * 2 * 128, 1024], BF16)
    bnd_in = dsc("bnd_in", [128, 16], F32)
    bnd_all = dsc("bnd_all", [4 * 128, 16], F32)
    RG = [[0, 1, 2, 3], [4, 5, 6, 7]]

    S = Sched(nc)
    slab = S.stack.enter_context(nc.sbuf_tensor("slab", [128, SLAB32], F32))
    PSB = [T(S.stack.enter_context(nc.psum_tensor(f"ps{i}", [128, 512], F32))[:], Buf()) for i in range(8)]
    psq = [T(PSB[7].ap[:, i * 128:(i + 1) * 128], Buf()) for i in range(4)]
    st = {"top": 0, "ps": 0, "psq": 0}

    def alloc(free, dt, name=""):
        n = int(np.prod(free))
        n32 = n if dt == F32 else (n + 1) // 2
        o = st["top"]
        st["top"] = o + (n32 + 15) // 16 * 16
        assert st["top"] <= SLAB32, ("SBUF arena overflow", name, st["top"])
        ap = slab[:, o:o + n32]
        if dt == BF16:
            ap = ap.bitcast(BF16)[:, 0:n]
        if len(free) == 2:
            ap = ap.rearrange("p (a b) -> p a b", b=free[1])
        elif len(free) == 3:
            ap = ap.rearrange("p (a b c) -> p a b c", b=free[1], c=free[2])
        return T(ap, Buf(name))

    def nextps():
        i = st["ps"]
        st["ps"] = (i + 1) % 7
        return PSB[i]

    def nextpsq():
        i = st["psq"]
        st["psq"] = (i + 1) % 4
        return psq[i]

    def mm(out, lhsT, rhs, start, stop, reads, wr):
        S.op("pe", I_mm(out, lhsT, rhs, start, stop), reads, [wr])

    C16 = alloc((1152,), BF16)
    IDN = alloc((128,), F32)
    ONE32 = alloc((128,), F32)
    MH = alloc((512,), F32)
    SEL = alloc((12,), F32)
    MASK = alloc((2, 256), BF16)
    MASKE = alloc((2, 256), BF16)
    CONV = alloc((4 * NFC * 4,), F32)
    NG = alloc((64,), F32)
    ADAB = alloc((4 * 48,), F32)
    CHCS = alloc((2, 512), BF16)
    modv = [[alloc((8, NS), F32) for j in range(6)] for l in range(NL)]
    s1 = [alloc((8, NS), F32) for l in range(NL)]
    s2 = [alloc((8, NS), F32) for l in range(NL)]
    BSEL = alloc((8, 2), F32)
    persist_top = st["top"]
    OM1024 = C16.ap[:, 0:128]
    BLK64 = C16.ap[:, 128:256]
    OM128 = C16.ap[:, 256:384]
    ONES = C16.ap[:, 384:512]
    SWAP = C16.ap[:, 512:640]
    ONEROW = C16.ap[0:1, 640:1152]

    for t, src in ((C16, c16_in), (IDN, ident_in), (SEL, sel_in), (CONV, ffn_convT), (NG, ng_in),
                   (ADAB, ada_bT), (CHCS, chCS.rearrange("p (a b) -> p a b", b=512)),
                   (MASK, swamask.rearrange("p (a b) -> p a b", b=256))):
        S.dma("sp", I_dma(t.ap, src), writes=[t.b])
    S.op("pool", I_ms(MH.ap, -0.5), writes=[MH.b])
    S.op("pool", I_ms(ONE32.ap, 1.0), writes=[ONE32.b])
    S.op("dve", I_ts(MASKE.ap[:, 0, :], MASK.ap[:, 0, :], SEL.ap[:, 8:9], ALU.mult), [MASK.b, SEL.b], [MASKE.b])
    S.op("dve", I_ts(MASKE.ap[:, 1, :], MASK.ap[:, 1, :], SEL.ap[:, 9:10], ALU.mult), [MASK.b, SEL.b], [MASKE.b])

    def stage_begin():
        S.barrier()
        st["top"] = persist_top

    def ck(name):
        if STOP == name:
            raise _Stop()

    def xt_view(X, t0, w):
        return X[:, :, t0:t0 + w].rearrange("c p t -> p c t")

    for l in range(NL):
        for (wsrc, j) in ((ffn_w_gate, 0), (ffn_w_up, 1)):
            src = wsrc[l].rearrange("(kc p) (fc j) -> fc p kc j", p=128, j=128)
            dst = gu16[l].rearrange("fc p (two kc j) -> fc p two kc j", two=2, j=128)
            for fc in range(NFC):
                S.dma("pool", I_dma(dst[fc, :, j], src[fc]))
        for h in range(2):
            S.dma("pool", I_dma(wd16[l, h * 1408:(h + 1) * 1408, :], ffn_w_down[l, h * 1408:(h + 1) * 1408, :]))

    def stage_mod():
        stage_begin()
        cTt = alloc((8, NS), F32)
        cact = alloc((8, NS), BF16)
        S.dma("sp", I_dma(cTt.ap, cT_in.rearrange("p (a b) -> p a b", b=NS)), writes=[cTt.b])
        S.op("act", I_act(cact.ap, cTt.ap, AF.Silu), [cTt.b], [cact.b])
        wbuf = [alloc((8, 1024), BF16) for _ in range(2)]
        i = 0
        for l in range(NL):
            for j in range(6):
                w = wbuf[i % 2]
                i += 1
                S.dma("pool", I_dma(w.ap, ada_w[l, :, j * 1024:(j + 1) * 1024].rearrange("(kc p) n -> p kc n", p=128)),
                      writes=[w.b])
                for dc in range(8):
                    ps = nextps()
                    for kc in range(8):
                        mm(ps.ap[:, 0:NS], w.ap[:, kc, dc * 128:(dc + 1) * 128], cact.ap[:, kc, :], kc == 0, kc == 7,
                           [w.b, cact.b], ps.b)
                    c0 = l * 48 + j * 8 + dc
                    S.op("act", I_act(modv[l][j].ap[:, dc, :], ps.ap[:, 0:NS], AF.Identity, bias=ADAB.ap[:, c0:c0 + 1]),
                         [ps.b, ADAB.b], [modv[l][j].b])
            for kc in range(8):
                S.op("dve", I_ts(s1[l].ap[:, kc, :], modv[l][1].ap[:, kc, :], 1.0, ALU.add,
                                 NG.ap[:, l * 8 + kc:l * 8 + kc + 1], ALU.mult), [modv[l][1].b, NG.b], [s1[l].b])
                S.op("dve", I_ts(s2[l].ap[:, kc, :], modv[l][4].ap[:, kc, :], 1.0, ALU.add,
                                 NG.ap[:, 32 + l * 8 + kc:32 + l * 8 + kc + 1], ALU.mult), [modv[l][4].b, NG.b], [s2[l].b])

    def stage_init():
        stage_begin()
        xt = [alloc((1024,), F32) for _ in range(2)]
        xo = [alloc((8, 128), F32) for _ in range(2)]
        for tt in range(NT // 128):
            x = xt[tt % 2]
            o = xo[tt % 2]
            S.dma("sp", I_dma(x.ap, xin[tt * 128:(tt + 1) * 128, :]), writes=[x.b])
            for hh in range(2):
                ps = nextps()
                for k in range(4):
                    kc = hh * 4 + k
                    S.op("pe", I_tr(ps.ap[:, k * 128:(k + 1) * 128], x.ap[:, kc * 128:(kc + 1) * 128], IDN.ap),
                         [x.b, IDN.b], [ps.b])
                S.op("act" if hh else "dve",
                     (I_act(o.ap[:, 4:8, :], ps.ap.rearrange("p (a b) -> p a b", b=128), AF.Copy) if hh else
                      I_cp(o.ap[:, 0:4, :], ps.ap.rearrange("p (a b) -> p a b", b=128))), [ps.b], [o.b])
            S.dma("sp", I_dma(xt_view(XTb, tt * 128, 128), o.ap), reads=[o.b])

    def stage_final(X):
        stage_begin()
        xi = [alloc((8, 128), F32) for _ in range(2)]
        yo = [alloc((1024,), F32) for _ in range(2)]
        for tt in range(NT // 128):
            x = xi[tt % 2]
            o = yo[tt % 2]
            S.dma("sp", I_dma(x.ap, xt_view(X, tt * 128, 128)), writes=[x.b])
            for hh in range(2):
                ps = nextps()
                for k in range(4):
                    S.op("pe", I_tr(ps.ap[:, k * 128:(k + 1) * 128], x.ap[:, hh * 4 + k, :], IDN.ap), [x.b, IDN.b], [ps.b])
                S.op("act" if hh else "dve",
                     (I_act(o.ap[:, 512:1024], ps.ap, AF.Copy) if hh else I_cp(o.ap[:, 0:512], ps.ap)), [ps.b], [o.b])
            S.dma("sp", I_dma(yout[tt * 128:(tt + 1) * 128, :], o.ap), reads=[o.b])

    def norm_bufs(W):
        return dict(sq=alloc((8, W), BF16), t=alloc((W,), F32), R=alloc((W,), F32),
                    tmp=[alloc((W,), F32) for _ in range(2)])

    def norm(x, W, sT, shT, seq, hT, c0, nb, xap=None):
        xap = x.ap[:, :, 0:W] if xap is None else xap
        S.op("act", I_act(nb["sq"].ap[:, :, 0:W], xap, AF.Square), [x.b], [nb["sq"].b])
        ps = nextps()
        for kc in range(8):
            mm(ps.ap[:, 0:W], OM1024, nb["sq"].ap[:, kc, 0:W], kc == 0, kc == 7, [nb["sq"].b, C16.b], ps.b)
        S.op("dve", I_ts(nb["t"].ap[:, 0:W], ps.ap[:, 0:W], EPS, ALU.add), [ps.b], [nb["t"].b])
        S.op("act", I_act(nb["t"].ap[:, 0:W], nb["t"].ap[:, 0:W], AF.Ln), [nb["t"].b], [nb["t"].b])
        S.op("act", I_act(nb["R"].ap[:, 0:W], nb["t"].ap[:, 0:W], AF.Exp, scale=-0.5), [nb["t"].b], [nb["R"].b])
        for kc in range(8):
            tmp = nb["tmp"][kc % 2]
            S.op("dve", I_tt(tmp.ap[:, 0:W], xap[:, kc, :], nb["R"].ap[:, 0:W], ALU.mult), [x.b, nb["R"].b], [tmp.b])
            S.op("act", I_act(hT.ap[:, kc, c0:c0 + W], tmp.ap[:, 0:W], AF.Identity, scale=sT.ap[:, kc, seq:seq + 1],
                              bias=shT.ap[:, kc, seq:seq + 1]), [tmp.b, sT.b, shT.b], [hT.b])

    def stage_tail(l, w_in, b_in, Xs, Xd):
        stage_begin()
        wo = alloc((8, 1024), BF16)
        S.dma("pool", I_dma(wo.ap, w_in.rearrange("(kc p) n -> p kc n", p=128)), writes=[wo.b])
        if b_in is not None:
            brow = alloc((1024,), BF16)
            S.dma("pool", I_dma(brow.ap[0:1, :], b_in.rearrange("(o n) -> o n", o=1)), writes=[brow.b])
        xg = [alloc((8, 512), F32) for _ in range(2)]
        mt = [alloc((8, 512), BF16) for _ in range(2)]
        for gi, (t0, s0, sl, seq) in enumerate(groups):
            x = xg[gi % 2]
            m = mt[gi % 2]
            S.dma("sp", I_dma(x.ap, xt_view(Xs, t0, 512)), writes=[x.b])
            S.dma("sp", I_dma(m.ap, xt_view(MTd, t0, 512)), writes=[m.b])
            for dc in range(8):
                ps = nextps()
                for kc in range(8):
                    mm(ps.ap, wo.ap[:, kc, dc * 128:(dc + 1) * 128], m.ap[:, kc, :], kc == 0, kc == 7 and b_in is None,
                       [wo.b, m.b], ps.b)
                if b_in is not None:
                    mm(ps.ap, brow.ap[0:1, dc * 128:(dc + 1) * 128], ONEROW, False, True, [brow.b, C16.b], ps.b)
                S.op("dve", I_stt(x.ap[:, dc, :], ps.ap, modv[l][2].ap[:, dc, seq:seq + 1], x.ap[:, dc, :], ALU.mult, ALU.add),
                     [ps.b, x.b, modv[l][2].b], [x.b])
            S.dma("sp", I_dma(xt_view(Xd, t0, 512), x.ap), reads=[x.b])

    def stage_bnd(X):
        stage_begin()
        bv = bnd_in.rearrange("p (c two) -> p c two", two=2)
        S.dma("sp", I_dma(bv[:, :, 0:1], xt_view(X, 0, 1), True))
        S.dma("sp", I_dma(bv[:, :, 1:2], xt_view(X, QP - 1, 1), True))
        S.barrier()
        bg = Buf()
        S.op("pool", lambda e: e.collective_compute("AllGather", ALU.bypass, replica_groups=RG, ins=[bnd_in.opt()],
                                                    outs=[bnd_all.opt()]), (), [bg])
        ba = alloc((4, 8, 2), F32)
        S.dma("pool", I_dma(ba.ap, bnd_all.rearrange("(r p) (c two) -> p r c two", p=128, two=2)), reads=[bg], writes=[ba.b])
        for side, (col, so) in enumerate(((1, 0), (0, 4))):
            S.op("dve", I_ts(BSEL.ap[:, :, side:side + 1], ba.ap[:, 0, :, col:col + 1], SEL.ap[:, so:so + 1], ALU.mult),
                 [ba.b, SEL.b], [BSEL.b])
            for r in range(1, 4):
                S.op("dve", I_stt(BSEL.ap[:, :, side:side + 1], ba.ap[:, r, :, col:col + 1], SEL.ap[:, so + r:so + r + 1],
                                  BSEL.ap[:, :, side:side + 1], ALU.mult, ALU.add), [ba.b, SEL.b, BSEL.b], [BSEL.b])

    def stage_ffn(l, Xs, Xd):
        stage_begin()
        xg = [alloc((8, 514), F32) for _ in range(2)]
        hT = [alloc((8, 514), BF16) for _ in range(2)]
        hh = alloc((8, 2), BF16)
        nb = norm_bufs(512)
        nbh = norm_bufs(2)
        gx = [alloc((514,), F32) for _ in range(2)]
        acc = [alloc((512,), F32) for _ in range(2)]
        sg = [alloc((512,), F32) for _ in range(2)]
        actT = alloc((NFC, 512), BF16)
        gu = [alloc((2, 8, 128), BF16) for _ in range(4)]
        wdh = [alloc((11, 1024), BF16) for _ in range(2)]
        k = 0
        for gi, (t0, s0, sl, seq) in enumerate(groups):
            x = xg[gi % 2]
            h = hT[gi % 2]
            lin = t0 > s0
            rin = t0 + 512 < s0 + sl
            c_lo = 0 if lin else 1
            c_hi = 514 if rin else 513
            S.dma("sp", I_dma(x.ap[:, :, c_lo:c_hi], xt_view(Xs, t0 - 1 + c_lo, c_hi - c_lo)), writes=[x.b])
            flags = []
            for side, col, inside in ((0, 0, lin), (1, 513, rin)):
                if inside:
                    flags.append(1.0)
                elif seq == 0:
                    S.op("dve", I_cp(x.ap[:, :, col:col + 1], BSEL.ap[:, :, side:side + 1]), [BSEL.b], [x.b])
                    flags.append(SEL.ap[:, 8 + side:9 + side])
                else:
                    S.op("dve", I_ms(x.ap[:, :, col:col + 1], 0.0), (), [x.b])
                    flags.append(0.0)
            norm(x, 512, s2[l], modv[l][3], seq, h, 1, nb, xap=x.ap[:, :, 1:513])
            norm(x, 2, s2[l], modv[l][3], seq, hh, 0, nbh, xap=x.ap[:, :, 0:514:513])
            for side, col in ((0, 0), (1, 513)):
                S.op("dve", I_ts(h.ap[:, :, col:col + 1], hh.ap[:, :, side:side + 1], flags[side], ALU.mult),
                     [hh.b, SEL.b], [h.b])
            for hf in range(2):
                S.dma("sp", I_dma(wdh[hf].ap, wd16[l, hf * 1408:(hf + 1) * 1408, :].rearrange("(fc p) d -> p fc d", p=128)),
                      writes=[wdh[hf].b])
            for fc in range(NFC):
                w = gu[k % 4]
                k += 1
                S.dma("sp", I_dma(w.ap, gu16[l, fc].rearrange("p (two kc j) -> p two kc j", two=2, j=128)), writes=[w.b])
                psg = nextps()
                for kc in range(8):
                    mm(psg.ap, w.ap[:, 0, kc, :], h.ap[:, kc, 1:513], kc == 0, kc == 7, [w.b, h.b], psg.b)
                psh = nextps()
                for kc in range(8):
                    mm(psh.ap[:, 0:2], w.ap[:, 0, kc, :], h.ap[:, kc, 0:514:513], kc == 0, kc == 7, [w.b, h.b], psh.b)
                psu = nextps()
                for kc in range(8):
                    mm(psu.ap, w.ap[:, 1, kc, :], h.ap[:, kc, 1:513], kc == 0, kc == 7, [w.b, h.b], psu.b)
                g = gx[fc % 2]
                a = acc[fc % 2]
                sgt = sg[fc % 2]
                S.op("act", I_act(g.ap[:, 1:513], psg.ap, AF.Copy), [psg.b], [g.b])
                S.op("act", I_act(g.ap[:, 0:514:513], psh.ap[:, 0:2], AF.Copy), [psh.b], [g.b])
                cb = (l * NFC + fc) * 4
                cw = CONV.ap
                S.op("dve", I_ts(a.ap, g.ap[:, 1:513], cw[:, cb + 1:cb + 2], ALU.mult, cw[:, cb + 3:cb + 4], ALU.add),
                     [g.b, CONV.b], [a.b])
                S.op("dve", I_stt(a.ap, g.ap[:, 0:512], cw[:, cb:cb + 1], a.ap, ALU.mult, ALU.add), [g.b, CONV.b, a.b], [a.b])
                S.op("dve", I_stt(a.ap, g.ap[:, 2:514], cw[:, cb + 2:cb + 3], a.ap, ALU.mult, ALU.add), [g.b, CONV.b, a.b], [a.b])
                S.op("act", I_act(sgt.ap, a.ap, AF.Silu), [a.b], [sgt.b])
                S.op("dve", I_tt(actT.ap[:, fc, :], sgt.ap, psu.ap, ALU.mult), [sgt.b, psu.b], [actT.b])
            for dc in range(8):
                ps = nextps()
                for fc in range(NFC):
                    mm(ps.ap, wdh[fc // 11].ap[:, fc % 11, dc * 128:(dc + 1) * 128], actT.ap[:, fc, :], fc == 0, fc == NFC - 1,
                       [wdh[fc // 11].b, actT.b], ps.b)
                S.op("dve", I_stt(x.ap[:, dc, 1:513], ps.ap, modv[l][5].ap[:, dc, seq:seq + 1], x.ap[:, dc, 1:513], ALU.mult, ALU.add),
                     [ps.b, x.b, modv[l][5].b], [x.b])
            S.dma("sp", I_dma(xt_view(Xd, t0, 512), x.ap[:, :, 1:513]), reads=[x.b])

    def stage_fourier(l, j, Xs):
        stage_begin()
        xg = [alloc((8, 512), F32) for _ in range(2)]
        hT = [alloc((8, 512), BF16) for _ in range(2)]
        nb = norm_bufs(512)
        zt = [alloc((4, 2048), BF16) for _ in range(2)]
        for gi, (t0, s0, sl, seq) in enumerate(groups):
            x = xg[gi % 2]
            h = hT[gi % 2]
            z = zt[gi % 2]
            S.dma("sp", I_dma(x.ap, xt_view(Xs, t0, 512)), writes=[x.b])
            norm(x, 512, s1[l], modv[l][0], seq, h, 0, nb)
            for tt in range(4):
                for grp in range(4):
                    ps = nextps()
                    for jj in range(2):
                        mm(ps.ap, h.ap[:, 2 * grp + jj, tt * 128:(tt + 1) * 128], CHCS.ap[:, jj, :], jj == 0, jj == 1,
                           [h.b, CHCS.b], ps.b)
                    dst = z.ap[:, tt, :].rearrange("p (two g c) -> p two g c", two=2, c=256)[:, :, grp, :]
                    src = ps.ap.rearrange("p (two c) -> p two c", two=2)
                    if (tt * 4 + grp) % 2:
                        S.op("act", I_act(dst, src, AF.Copy), [ps.b], [z.b])
                    else:
                        S.op("dve", I_cp(dst, src), [ps.b], [z.b])
            if seq == 0:
                m0 = t0 // 128
                for i in range(16):
                    S.dma("sp", I_dma(zin_c[i].rearrange("(a m) c -> a m c", a=8)[:, m0:m0 + 4, :], z.ap[8 * i:8 * i + 8, :, :]),
                          reads=[z.b])
            else:
                drows = zs[t0 - QP:t0 - QP + 512, :]
                S.dma("sp", I_dma(drows.rearrange("(tt p) c -> p tt c", p=128), z.ap), reads=[z.b])
        ck("A1")
        S.barrier()
        bg = Buf()
        for i in range(16):
            S.op("pool", (lambda a, b: lambda e: e.collective_compute("AllGather", ALU.bypass, replica_groups=RG, ins=[a.opt()],
                                                                      outs=[b.opt()]))(zin_c[i], zall_c[i]), (), [bg])
        ck("AG")
        stage_begin()
        zsb = alloc((NTS, 2048), BF16)
        tb = [alloc((NTS, 2, 512), BF16) for _ in range(2)]
        mo = [alloc((8, 512), BF16) for _ in range(2)]
        ti = 0
        for si in range(NSS):
            S.dma("sp", I_dma(zsb.ap, zs[si * SS:(si + 1) * SS, :].rearrange("(nt p) c -> p nt c", p=128)), writes=[zsb.b])
            for kg in range(NKG):
                tbt = tb[ti % 2]
                m = mo[ti % 2]
                ti += 1
                S.dma("sp", I_dma(tbt.ap, tabS[kg].rearrange("p (nt two k) -> p nt two k", two=2, k=512)), writes=[tbt.b])
                for cc in range(8):
                    ps = nextps()
                    for nt in range(NTS):
                        mm(ps.ap, zsb.ap[:, nt, cc * 128:(cc + 1) * 128], tbt.ap[:, nt, 0, :], nt == 0, False, [zsb.b, tbt.b], ps.b)
                        mm(ps.ap, zsb.ap[:, nt, 1024 + cc * 128:1024 + (cc + 1) * 128], tbt.ap[:, nt, 1, :], False, nt == NTS - 1,
                           [zsb.b, tbt.b], ps.b)
                    S.op("act", I_act(m.ap[:, cc, :], ps.ap, AF.Copy), [ps.b], [m.b])
                t0 = QP + si * SS + kg * 512
                S.dma("sp", I_dma(xt_view(MTd, t0, 512), m.ap), reads=[m.b])
        ck("A2s")
        stage_begin()
        zt1 = [alloc((2048,), BF16) for _ in range(3)]
        tb1 = [alloc((3, N2), BF16) for _ in range(3)]
        vt = [alloc((2048,), BF16) for _ in range(3)]
        for n1 in range(128):
            z = zt1[n1 % 3]
            tbl = tb1[n1 % 3]
            v = vt[n1 % 3]
            ci, ca = n1 // 8, n1 % 8
            for r in range(4):
                S.dma("sp", I_dma(z.ap[r * MQ:(r + 1) * MQ, :], zall_c[ci][r * 8 * MQ + ca * MQ:r * 8 * MQ + (ca + 1) * MQ, :]),
                      reads=[bg], writes=[z.b])
            S.dma("pool", I_dma(tbl.ap[0:N2], tabP1[n1].rearrange("p (a b) -> p a b", b=N2)), writes=[tbl.b])
            for hf in range(2):
                A_ = z.ap[0:N2, hf * 512:(hf + 1) * 512]
                B_ = z.ap[0:N2, 1024 + hf * 512:1024 + (hf + 1) * 512]
                psr = nextps()
                mm(psr.ap[0:N2, :], tbl.ap[0:N2, 0, :], A_, True, False, [z.b, tbl.b], psr.b)
                mm(psr.ap[0:N2, :], tbl.ap[0:N2, 1, :], B_, False, True, [z.b, tbl.b], psr.b)
                psi = nextps()
                mm(psi.ap[0:N2, :], tbl.ap[0:N2, 2, :], B_, True, False, [z.b, tbl.b], psi.b)
                mm(psi.ap[0:N2, :], tbl.ap[0:N2, 1, :], A_, False, True, [z.b, tbl.b], psi.b)
                S.op("act", I_act(v.ap[0:N2, hf * 512:(hf + 1) * 512], psr.ap[0:N2, :], AF.Copy), [psr.b], [v.b])
                S.op("dve", I_cp(v.ap[0:N2, 1024 + hf * 512:1024 + (hf + 1) * 512], psi.ap[0:N2, :]), [psi.b], [v.b])
            S.dma("sp", I_dma(vbuf[n1], v.ap[0:N2, :]), reads=[v.b])
        ck("P1")
        stage_begin()
        T2 = alloc((64,), BF16)
        S.dma("sp", I_dma(T2.ap, tabP2), writes=[T2.b])
        vt2 = [alloc((2048,), BF16) for _ in range(4)]
        MT = alloc((8, QP), BF16)
        KB = min(8, N2)
        for kb in range(N2 // KB):
            banks = [nextps() for _ in range(4)]
            for kl in range(KB):
                k2 = kb * KB + kl
                v = vt2[k2 % 4]
                S.dma("sp", I_dma(v.ap, vbuf[:, k2, :]), writes=[v.b])
                for cc in range(8):
                    bk = banks[cc // 2]
                    o = bk.ap[:, ((cc % 2) * KB + kl) * 32:((cc % 2) * KB + kl + 1) * 32]
                    mm(o, v.ap[:, cc * 128:(cc + 1) * 128], T2.ap[:, 0:32], True, False, [v.b, T2.b], bk.b)
                    mm(o, v.ap[:, 1024 + cc * 128:1024 + (cc + 1) * 128], T2.ap[:, 32:64], False, True, [v.b, T2.b], bk.b)
            for b4 in range(4):
                src = banks[b4].ap[:, 0:2 * KB * 32].rearrange("p (c k a) -> p c k a", c=2, a=32)
                dst = MT.ap[:, 2 * b4:2 * b4 + 2, :].rearrange("p c (a k) -> p c k a", k=N2)[:, :, kb * KB:(kb + 1) * KB, :]
                for c in range(2):
                    S.op("dve" if c else "act", (I_cp(dst[:, c], src[:, c]) if c else I_act(dst[:, c], src[:, c], AF.Copy)),
                         [banks[b4].b], [MT.b])
        for g in range(QP // 512):
            S.dma("sp", I_dma(xt_view(MTd, g * 512, 512), MT.ap[:, :, g * 512:(g + 1) * 512]), reads=[MT.b])

    def qk_post(ps, gcol, G_, o, qb, cs, sn):
        sq, t_, R_, qn, t1, t2 = qb
        S.op("act", I_act(sq.ap, ps.ap, AF.Square), [ps.b], [sq.b])
        ps2 = nextps()
        mm(ps2.ap, BLK64, sq.ap, True, True, [sq.b, C16.b], ps2.b)
        S.op("dve", I_ts(t_.ap, ps2.ap, EPS, ALU.add), [ps2.b], [t_.b])
        S.op("act", I_act(t_.ap, t_.ap, AF.Ln), [t_.b], [t_.b])
        S.op("act", I_act(R_.ap, t_.ap, AF.Exp, scale=-0.5), [t_.b], [R_.b])
        S.op("dve", I_stt(qn.ap, ps.ap, G_.ap[:, gcol:gcol + 1], R_.ap, ALU.mult, ALU.mult), [ps.b, G_.b, R_.b], [qn.b])
        ps3 = nextps()
        mm(ps3.ap, SWAP, qn.ap, True, True, [qn.b, C16.b], ps3.b)
        S.op("pool", I_tt(t1.ap, qn.ap, cs.ap, ALU.mult), [qn.b, cs.b], [t1.b])
        S.op("dve", I_tt(t2.ap, ps3.ap, sn.ap, ALU.mult), [ps3.b, sn.b], [t2.b])
        S.op("dve", I_tt(o.ap, t1.ap, t2.ap, ALU.add), [t1.b, t2.b], [o.b])

    def qk_bufs():
        return (alloc((512,), BF16), alloc((512,), F32), alloc((512,), F32), alloc((512,), BF16),
                alloc((512,), F32), alloc((512,), F32))

    def stage_diff(l, Xs):
        lam_init = 0.8 - 0.6 * math.exp(-0.3 * l)
        stage_begin()
        wq = alloc((8, 3072), BF16)
        S.dma("pool", I_dma(wq.ap, diff_w_qkv[0].rearrange("(kc p) n -> p kc n", p=128)), writes=[wq.b])
        G_ = alloc((2,), F32)
        S.dma("sp", I_dma(G_.ap, diff_g), writes=[G_.b])
        xg = [alloc((8, 512), F32) for _ in range(2)]
        hT = [alloc((8, 512), BF16) for _ in range(2)]
        nb = norm_bufs(512)
        qb = qk_bufs()
        cs = [alloc((512,), F32) for _ in range(2)]
        sn = [alloc((512,), F32) for _ in range(2)]
        qo = [alloc((512,), BF16) for _ in range(4)]
        vo = [alloc((4, 1024), BF16) for _ in range(2)]
        qi = 0
        for gi, (t0, s0, sl, seq) in enumerate(groups):
            x = xg[gi % 2]
            h = hT[gi % 2]
            c_ = cs[gi % 2]
            s_ = sn[gi % 2]
            v = vo[gi % 2]
            S.dma("sp", I_dma(x.ap, xt_view(Xs, t0, 512)), writes=[x.b])
            S.dma("sp", I_dma(c_.ap, ropeC[:, t0:t0 + 512]), writes=[c_.b])
            S.dma("sp", I_dma(s_.ap, ropeS[:, t0:t0 + 512]), writes=[s_.b])
            norm(x, 512, s1[l], modv[l][0], seq, h, 0, nb)
            for oc in range(16):
                ps = nextps()
                for kc in range(8):
                    mm(ps.ap, wq.ap[:, kc, oc * 128:(oc + 1) * 128], h.ap[:, kc, :], kc == 0, kc == 7, [wq.b, h.b], ps.b)
                o = qo[qi % 4]
                qi += 1
                qk_post(ps, 0 if oc < 8 else 1, G_, o, qb, c_, s_)
                if oc < 8:
                    dst = QT[oc, :, t0:t0 + 512]
                elif seq == 0:
                    dst = kin_h[oc - 8][:, t0:t0 + 512]
                else:
                    dst = KT[oc - 8, :, t0:t0 + 512]
                S.dma("sp", I_dma(dst, o.ap), reads=[o.b])
            for tt in range(4):
                for nbk in range(2):
                    ps = nextps()
                    for kc in range(8):
                        mm(ps.ap, h.ap[:, kc, tt * 128:(tt + 1) * 128], wq.ap[:, kc, 2048 + nbk * 512:2048 + (nbk + 1) * 512],
                           kc == 0, kc == 7, [wq.b, h.b], ps.b)
                    S.op("act", I_act(v.ap[:, tt, nbk * 512:(nbk + 1) * 512], ps.ap, AF.Copy), [ps.b], [v.b])
            if seq == 0:
                for hd in range(8):
                    S.dma("sp", I_dma(vin_h[hd][t0:t0 + 512, :].rearrange("(tt p) e -> p tt e", p=128), v.ap[:, :, hd * 128:(hd + 1) * 128]),
                          reads=[v.b])
            else:
                S.dma("sp", I_dma(Vt[t0:t0 + 512, :].rearrange("(tt p) c -> p tt c", p=128), v.ap), reads=[v.b])
        S.barrier()
        bgk = Buf()
        bgv = Buf()
        for hd in range(8):
            S.op("pool", (lambda a, b: lambda e: e.collective_compute("AllGather", ALU.bypass, replica_groups=RG, ins=[a.opt()],
                                                                      outs=[b.opt()]))(kin_h[hd], kall_h[hd]), (), [bgk])
            S.op("pool", (lambda a, b: lambda e: e.collective_compute("AllGather", ALU.bypass, replica_groups=RG, ins=[a.opt()],
                                                                      outs=[b.opt()]))(vin_h[hd], vall_h[hd]), (), [bgv])
        stage_begin()
        LKM = max(SP, SS)
        Kh = alloc((LKM,), BF16)
        Vh = alloc((LKM // 128, 128), BF16)
        Qh = alloc((max(QP, SS),), BF16)
        Et = [alloc((512,), BF16) for _ in range(4)]
        r0 = alloc((512,), F32)
        r1 = alloc((512,), F32)
        a0 = alloc((512,), F32)
        a1 = alloc((512,), F32)
        sq = alloc((512,), BF16)
        t_ = alloc((512,), F32)
        R_ = alloc((512,), F32)
        mo = [alloc((512,), BF16) for _ in range(2)]
        L4 = alloc((4,), F32)
        pr = alloc((2,), F32)
        e12 = alloc((2,), F32)
        nlam = alloc((1,), F32)
        gsub = alloc((1,), F32)
        S.dma("sp", I_dma(L4.ap[0:64, :], diff_l), writes=[L4.b])
        S.dma("sp", I_dma(gsub.ap, diff_subln), writes=[gsub.b])
        S.op("dve", I_tt(pr.ap[0:64, :], L4.ap[0:64, 0:4:2], L4.ap[0:64, 1:4:2], ALU.mult), [L4.b], [pr.b])
        psl = PSB[0]
        mm(psl.ap[:, 0:2], ONE32.ap[0:64, :], pr.ap[0:64, :], True, True, [ONE32.b, pr.b], psl.b)
        S.op("act", I_act(e12.ap, psl.ap[:, 0:2], AF.Exp), [psl.b], [e12.b])
        S.op("dve", I_tt(nlam.ap, e12.ap[:, 1:2], e12.ap[:, 0:1], ALU.subtract), [e12.b], [nlam.b])
        S.op("dve", I_ts(nlam.ap, nlam.ap, -lam_init, ALU.add), [nlam.b], [nlam.b])
        S.op("dve", I_ts(gsub.ap, gsub.ap, 1.0 - lam_init, ALU.mult), [gsub.b], [gsub.b])
        O0, O1, D0, D1 = PSB[3], PSB[4], PSB[5], PSB[6]
        sb = [PSB[0], PSB[1], PSB[2], PSB[7]]
        sc = [0]

        def nexts():
            i = sc[0]
            sc[0] = (i + 1) % 4
            return sb[i]

        mi = 0
        for (s0, sl, seq) in segs:
            LK = SP if seq == 0 else SS
            NK = LK // 128
            for hd in range(8):
                if seq == 0:
                    for r in range(4):
                        S.dma("sp", I_dma(Kh.ap[:, r * QP:(r + 1) * QP], kall_h[hd][r * 128:(r + 1) * 128, :]),
                              reads=[bgk], writes=[Kh.b])
                    vv = vall_h[hd].rearrange("(kt p) e -> p kt e", p=128)
                    for c8 in range(0, NK, 16):
                        S.dma("sp", I_dma(Vh.ap[:, c8:c8 + 16, :], vv[:, c8:c8 + 16, :]), reads=[bgv], writes=[Vh.b])
                else:
                    S.dma("sp", I_dma(Kh.ap[:, 0:LK], KT[hd, :, s0:s0 + sl]), writes=[Kh.b])
                    vv = Vt[s0:s0 + sl, :].rearrange("(kt p) (h e) -> p kt h e", p=128, e=128)
                    S.dma("sp", I_dma(Vh.ap[:, 0:NK, :], vv[:, :, hd, :]), writes=[Vh.b])
                S.dma("sp", I_dma(Qh.ap[:, 0:sl], QT[hd, :, s0:s0 + sl]), writes=[Qh.b])
                for qg in range(sl // 512):
                    qs = slice(qg * 512, (qg + 1) * 512)

                    def smm(kt):
                        pa = nexts()
                        pb = nexts()
                        mm(pa.ap, Kh.ap[0:64, kt * 128:(kt + 1) * 128], Qh.ap[0:64, qs], True, True, [Kh.b, Qh.b], pa.b)
                        mm(pb.ap, Kh.ap[64:128, kt * 128:(kt + 1) * 128], Qh.ap[64:128, qs], True, True, [Kh.b, Qh.b], pb.b)
                        return pa, pb

                    cur = smm(0)
                    for kt in range(NK):
                        nxt = smm(kt + 1) if kt + 1 < NK else None
                        e0 = Et[(2 * kt) % 4]
                        e1 = Et[(2 * kt + 1) % 4]
                        S.op("act", I_act(e0.ap, cur[0].ap, AF.Exp, scale=0.125), [cur[0].b], [e0.b])
                        S.op("act", I_act(e1.ap, cur[1].ap, AF.Exp, scale=0.125), [cur[1].b], [e1.b])
                        mm(O0.ap, Vh.ap[:, kt, :], e0.ap, kt == 0, kt == NK - 1, [Vh.b, e0.b], O0.b)
                        mm(D0.ap, ONES, e0.ap, kt == 0, kt == NK - 1, [C16.b, e0.b], D0.b)
                        mm(O1.ap, Vh.ap[:, kt, :], e1.ap, kt == 0, kt == NK - 1, [Vh.b, e1.b], O1.b)
                        mm(D1.ap, ONES, e1.ap, kt == 0, kt == NK - 1, [C16.b, e1.b], D1.b)
                        cur = nxt
                    S.op("dve", I_rec(r0.ap, D0.ap), [D0.b], [r0.b])
                    S.op("dve", I_rec(r1.ap, D1.ap), [D1.b], [r1.b])
                    S.op("dve", I_tt(a0.ap, O0.ap, r0.ap, ALU.mult), [O0.b, r0.b], [a0.b])
                    S.op("dve", I_tt(a1.ap, O1.ap, r1.ap, ALU.mult), [O1.b, r1.b], [a1.b])
                    S.op("dve", I_stt(a0.ap, a1.ap, nlam.ap[:, 0:1], a0.ap, ALU.mult, ALU.add), [a1.b, nlam.b, a0.b], [a0.b])
                    S.op("act", I_act(sq.ap, a0.ap, AF.Square), [a0.b], [sq.b])
                    pm = nexts()
                    mm(pm.ap, OM128, sq.ap, True, True, [sq.b, C16.b], pm.b)
                    S.op("dve", I_ts(t_.ap, pm.ap, EPS, ALU.add), [pm.b], [t_.b])
                    S.op("act", I_act(t_.ap, t_.ap, AF.Ln), [t_.b], [t_.b])
                    S.op("act", I_act(R_.ap, t_.ap, AF.Exp, scale=-0.5), [t_.b], [R_.b])
                    m = mo[mi % 2]
                    mi += 1
                    S.op("dve", I_stt(m.ap, a0.ap, gsub.ap[:, 0:1], R_.ap, ALU.mult, ALU.mult), [a0.b, gsub.b, R_.b], [m.b])
                    S.dma("sp", I_dma(MTd[hd, :, s0 + qg * 512:s0 + (qg + 1) * 512], m.ap), reads=[m.b])

    def stage_swa(l, Xs):
        stage_begin()
        wq = alloc((8, 1024), BF16)
        wk2 = alloc((8, 4, 128), BF16)
        wv2 = alloc((8, 4, 128), BF16)
        S.dma("pool", I_dma(wq.ap, swa_w_qkv[0, :, 0:1024].rearrange("(kc p) n -> p kc n", p=128)), writes=[wq.b])
        for dup in range(2):
            for hk in range(4):
                S.dma("pool", I_dma(wk2.ap[:, :, hk, dup * 64:(dup + 1) * 64],
                                    swa_w_qkv[0, :, 1024 + hk * 64:1024 + (hk + 1) * 64].rearrange("(kc p) d -> p kc d", p=128)), writes=[wk2.b])
                S.dma("pool", I_dma(wv2.ap[:, :, hk, dup * 64:(dup + 1) * 64],
                                    swa_w_qkv[0, :, 1280 + hk * 64:1280 + (hk + 1) * 64].rearrange("(kc p) d -> p kc d", p=128)), writes=[wv2.b])
        G_ = alloc((2,), F32)
        S.dma("sp", I_dma(G_.ap, swa_g), writes=[G_.b])
        xg = [alloc((8, 512), F32) for _ in range(2)]
        hT = [alloc((8, 512), BF16) for _ in range(2)]
        nb = norm_bufs(512)
        qb = qk_bufs()
        cs = [alloc((512,), F32) for _ in range(2)]
        sn = [alloc((512,), F32) for _ in range(2)]
        qo = [alloc((512,), BF16) for _ in range(4)]
        vo = [alloc((4, 512), BF16) for _ in range(2)]
        qi = 0
        for gi, (t0, s0, sl, seq) in enumerate(groups):
            x = xg[gi % 2]
            h = hT[gi % 2]
            c_ = cs[gi % 2]
            s_ = sn[gi % 2]
            v = vo[gi % 2]
            S.dma("sp", I_dma(x.ap, xt_view(Xs, t0, 512)), writes=[x.b])
            S.dma("sp", I_dma(c_.ap, ropeC[:, t0:t0 + 512]), writes=[c_.b])
            S.dma("sp", I_dma(s_.ap, ropeS[:, t0:t0 + 512]), writes=[s_.b])
            norm(x, 512, s1[l], modv[l][0], seq, h, 0, nb)
            for oc in range(12):
                ps = nextps()
                for kc in range(8):
                    lw = wq.ap[:, kc, oc * 128:(oc + 1) * 128] if oc < 8 else wk2.ap[:, kc, oc - 8, :]
                    mm(ps.ap, lw, h.ap[:, kc, :], kc == 0, kc == 7, [wq.b, wk2.b, h.b], ps.b)
                o = qo[qi % 4]
                qi += 1
                qk_post(ps, 0 if oc < 8 else 1, G_, o, qb, c_, s_)
                dst = QT[oc, :, t0:t0 + 512] if oc < 8 else KT[oc - 8, :, t0:t0 + 512]
                S.dma("sp", I_dma(dst, o.ap), reads=[o.b])
                if seq == 0 and oc >= 8 and t0 == 0:
                    S.dma("sp", I_dma(sbin[0:128, (oc - 8) * 128:(oc - 7) * 128], o.ap[:, 0:128]), reads=[o.b])
                if seq == 0 and oc >= 8 and t0 == QP - 512:
                    S.dma("sp", I_dma(sbin[128:256, (oc - 8) * 128:(oc - 7) * 128], o.ap[:, 384:512]), reads=[o.b])
            for tt in range(4):
                ps = nextps()
                for kc in range(8):
                    mm(ps.ap, h.ap[:, kc, tt * 128:(tt + 1) * 128], wv2.ap[:, kc].rearrange("p a b -> p (a b)"),
                       kc == 0, kc == 7, [wv2.b, h.b], ps.b)
                S.op("act", I_act(v.ap[:, tt, :], ps.ap, AF.Copy), [ps.b], [v.b])
            S.dma("sp", I_dma(Vt[t0:t0 + 512, 0:512].rearrange("(tt p) c -> p tt c", p=128), v.ap), reads=[v.b])
            if seq == 0 and t0 == 0:
                S.dma("sp", I_dma(sbin[0:128, 512:1024], v.ap[:, 0, :]), reads=[v.b])
            if seq == 0 and t0 == QP - 512:
                S.dma("sp", I_dma(sbin[128:256, 512:1024], v.ap[:, 3, :]), reads=[v.b])
        S.barrier()
        bg = Buf()
        S.op("pool", lambda e: e.collective_compute("AllGather", ALU.bypass, replica_groups=RG, ins=[sbin.opt()],
                                                    outs=[sball.opt()]), (), [bg])
        stage_begin()
        sk = alloc((16,), F32)
        se = alloc((16,), F32)
        S.dma("sp", I_dma(sk.ap, swa_sink), writes=[sk.b])
        S.op("act", I_act(se.ap, sk.ap, AF.Exp), [sk.b], [se.b])
        hal = alloc((4, 2, 1024), BF16)
        S.dma("sp", I_dma(hal.ap, sball.rearrange("(r e p) c -> p r e c", p=128, e=2)), reads=[bg], writes=[hal.b])
        hsel = alloc((2, 1024), BF16)
        for side, (e_, so) in enumerate(((1, 0), (0, 4))):
            S.op("dve", I_ts(hsel.ap[:, side, :], hal.ap[:, 0, e_, :], SEL.ap[:, so:so + 1], ALU.mult), [hal.b, SEL.b], [hsel.b])
            for r in range(1, 4):
                S.op("dve", I_stt(hsel.ap[:, side, :], hal.ap[:, r, e_, :], SEL.ap[:, so + r:so + r + 1], hsel.ap[:, side, :],
                                  ALU.mult, ALU.add), [hal.b, SEL.b, hsel.b], [hsel.b])
        Qg = [alloc((8, 512), BF16) for _ in range(2)]
        Kg = [alloc((4, 768), BF16) for _ in range(2)]
        Vg = [alloc((6, 512), BF16) for _ in range(2)]
        Mg = [alloc((8, 512), BF16) for _ in range(2)]
        Et = [alloc((256,), BF16) for _ in range(9)]
        rr = [alloc((256,), F32) for _ in range(3)]
        ODB = [PSB[0], PSB[1], PSB[2]]
        SH = [T(PSB[3 + i].ap[:, 0:256], PSB[3 + i].b) for i in range(5)]
        oi = 0
        si = 0
        ei = 0
        ri = 0
        for gi, (t0, s0, sl, seq) in enumerate(groups):
            q = Qg[gi % 2]
            k_ = Kg[gi % 2]
            v = Vg[gi % 2]
            m = Mg[gi % 2]
            S.dma("sp", I_dma(q.ap, xt_view(QT, t0, 512)), writes=[q.b])
            S.dma("sp", I_dma(k_.ap[:, :, 128:640], KT[0:4, :, t0:t0 + 512].rearrange("c p t -> p c t")), writes=[k_.b])
            S.dma("sp", I_dma(v.ap[:, 1:5, :], Vt[t0:t0 + 512, 0:512].rearrange("(tt p) c -> p tt c", p=128)), writes=[v.b])
            lin = t0 > s0
            rin = t0 + 512 < s0 + sl
            if lin:
                S.dma("sp", I_dma(k_.ap[:, :, 0:128], KT[0:4, :, t0 - 128:t0].rearrange("c p t -> p c t")), writes=[k_.b])
                S.dma("sp", I_dma(v.ap[:, 0, :], Vt[t0 - 128:t0, 0:512]), writes=[v.b])
            elif seq == 0:
                S.op("dve", I_cp(k_.ap[:, :, 0:128], hsel.ap[:, 0, 0:512].rearrange("p (a b) -> p a b", b=128)), [hsel.b], [k_.b])
                S.op("dve", I_cp(v.ap[:, 0, :], hsel.ap[:, 0, 512:1024]), [hsel.b], [v.b])
            if rin:
                S.dma("sp", I_dma(k_.ap[:, :, 640:768], KT[0:4, :, t0 + 512:t0 + 640].rearrange("c p t -> p c t")), writes=[k_.b])
                S.dma("sp", I_dma(v.ap[:, 5, :], Vt[t0 + 512:t0 + 640, 0:512]), writes=[v.b])
            elif seq == 0:
                S.op("dve", I_cp(k_.ap[:, :, 640:768], hsel.ap[:, 1, 0:512].rearrange("p (a b) -> p a b", b=128)), [hsel.b], [k_.b])
                S.op("dve", I_cp(v.ap[:, 5, :], hsel.ap[:, 1, 512:1024]), [hsel.b], [v.b])
            for nbk in range(4):
                offs = []
                for o in (-1, 0, 1):
                    if (o == -1 and nbk == 0 and not lin and seq != 0) or (o == 1 and nbk == 3 and not rin and seq != 0):
                        continue
                    edge = (o == -1 and nbk == 0 and not lin) or (o == 1 and nbk == 3 and not rin)
                    offs.append((o, edge))
                for hk in range(4):
                    for half in range(2):
                        pr_ = slice(half * 64, half * 64 + 64)
                        pOD = ODB[oi % 3]
                        oi += 1
                        es = []
                        for ii, (o, edge) in enumerate(offs):
                            pS = SH[si % 5]
                            si += 1
                            kc0 = 128 + (nbk + o) * 128
                            mm(pS.ap, k_.ap[pr_, hk, kc0:kc0 + 128], q.ap[pr_, 2 * hk:2 * hk + 2, nbk * 128:(nbk + 1) * 128],
                               True, True, [k_.b, q.b], pS.b)
                            e = Et[ei % 9]
                            ei += 1
                            S.op("act", I_act(e.ap, pS.ap, AF.Exp, scale=0.125), [pS.b], [e.b])
                            if o != 0:
                                mk = (MASKE if edge else MASK).ap[:, 0 if o == -1 else 1, :]
                                S.op("dve", I_tt(e.ap, e.ap, mk, ALU.mult), [e.b, MASK.b, MASKE.b], [e.b])
                            es.append((e, o))
                        for ii, (e, o) in enumerate(es):
                            mm(pOD.ap[:, 0:256], v.ap[:, 1 + nbk + o, hk * 128:(hk + 1) * 128], e.ap, ii == 0, ii == len(es) - 1,
                               [v.b, e.b], pOD.b)
                        for ii, (e, o) in enumerate(es):
                            mm(pOD.ap[:, 256:512], ONES, e.ap, ii == 0, ii == len(es) - 1, [C16.b, e.b], pOD.b)
                        r_ = rr[ri % 3]
                        ri += 1
                        for jj in range(2):
                            qh = hk * 4 + half + 2 * jj
                            S.op("dve", I_ts(r_.ap[:, jj * 128:(jj + 1) * 128], pOD.ap[:, 256 + jj * 128:256 + (jj + 1) * 128], se.ap[:, qh:qh + 1], ALU.add),
                                 [pOD.b, se.b], [r_.b])
                        S.op("dve", I_rec(r_.ap, r_.ap), [r_.b], [r_.b])
                        S.op("dve", I_tt(m.ap[pr_, 2 * hk:2 * hk + 2, nbk * 128:(nbk + 1) * 128],
                                         pOD.ap[pr_, 0:256].rearrange("p (a b) -> p a b", b=128),
                                         r_.ap[pr_, :].rearrange("p (a b) -> p a b", b=128), ALU.mult), [pOD.b, r_.b], [m.b])
            S.dma("sp", I_dma(xt_view(MTd, t0, 512), m.ap), reads=[m.b])

    stage_mod()
    stage_init()
    try:
        for l in range(NL):
            kind, j = l % 3, l // 3
            if kind == 0:
                stage_fourier(l, j, XTb)
                ck("mix")
                stage_tail(l, fnet_w[j], fnet_b[j], XTb, XTa)
            elif kind == 1:
                stage_swa(l, XTb)
                ck("mix")
                stage_tail(l, swa_w_o[0], None, XTb, XTa)
            else:
                stage_diff(l, XTb)
                ck("mix")
                stage_tail(l, diff_w_o[0], None, XTb, XTa)
            ck("tail")
            stage_bnd(XTa)
            ck("bnd")
            stage_ffn(l, XTa, XTb)
    except _Stop:
        pass
    stage_final(XTa if STOP in ("tail", "bnd") else XTb)
    S.barrier()
    S.emit()
    S.stack.close()
    return nc


def _bf(a):
    return np.ascontiguousarray(a).astype(ml_dtypes.bfloat16)


def make_tables(cfg):
    SP, SS, NSS = cfg["SP"], cfg["SS"], cfg["NSS"]
    QP = SP // 4
    N2 = SP // 128
    NTS = SS // 128
    NKG = SS // 512
    tb = {}
    a = np.arange(256)
    ang = 2 * np.pi * np.outer(a, a) / 256
    cg = np.cos(ang) / 16.0
    sg = np.sin(ang) / 16.0
    ch = np.zeros((128, 2, 512))
    for jj in range(2):
        ch[:, jj, 0:256] = cg[jj * 128:(jj + 1) * 128]
        ch[:, jj, 256:512] = sg[jj * 128:(jj + 1) * 128]
    tb["chCS"] = _bf(ch.reshape(128, 1024))
    n = np.arange(SS)
    angS = 2 * np.pi * (np.outer(n, n) % SS) / SS
    CS_ = np.cos(angS) / np.sqrt(SS)
    SS_ = -np.sin(angS) / np.sqrt(SS)
    t = np.zeros((NKG, 128, NTS, 2, 512))
    for kg in range(NKG):
        for nt in range(NTS):
            t[kg, :, nt, 0, :] = CS_[nt * 128:(nt + 1) * 128, kg * 512:(kg + 1) * 512]
            t[kg, :, nt, 1, :] = SS_[nt * 128:(nt + 1) * 128, kg * 512:(kg + 1) * 512]
    tb["tabS"] = _bf(t.reshape(NKG, 128, NTS * 2 * 512))
    n1 = np.arange(128)[:, None, None]
    n2 = np.arange(N2)[None, :, None]
    k2 = np.arange(N2)[None, None, :]
    angE = 2 * np.pi * (((n1 + 128 * n2) * k2) % SP) / SP
    Ec = np.cos(angE) / np.sqrt(SP)
    Es = np.sin(angE) / np.sqrt(SP)
    t1 = np.stack([Ec, -Es, -Ec], axis=2)
    tb["tabP1"] = _bf(t1.reshape(128, N2, 3 * N2))
    kk = np.arange(128)[:, None]
    qq = np.arange(128)[None, :]
    m0 = (qq <= kk).astype(np.float32)
    m1 = (kk <= qq).astype(np.float32)
    mk = np.stack([np.concatenate([m0, m0], 1), np.concatenate([m1, m1], 1)], 1)
    tb["swamask"] = _bf(mk.reshape(128, 512))
    tb["ident"] = np.eye(128, dtype=np.float32)
    c16 = np.zeros((128, 1152), np.float32)
    c16[:, 0:128] = 1.0 / 1024
    blk = np.zeros((128, 128), np.float32)
    blk[0:64, 0:64] = 1.0 / 64
    blk[64:128, 64:128] = 1.0 / 64
    c16[:, 128:256] = blk
    c16[:, 256:384] = 1.0 / 128
    c16[:, 384:512] = 1.0
    sw = np.zeros((128, 128), np.float32)
    for m_ in range(128):
        partner = m_ + 32 if (m_ % 64) < 32 else m_ - 32
        sw[partner, m_] = 1.0
    c16[:, 512:640] = sw
    c16[:, 640:1152] = 1.0
    tb["c16"] = _bf(c16)
    return tb


def core_tables(cfg, core):
    SP, SS, NSS = cfg["SP"], cfg["SS"], cfg["NSS"]
    QP = SP // 4
    q = core % 4
    tb = {}
    pos = np.concatenate([np.arange(q * QP, (q + 1) * QP)] + [np.arange(SS)] * NSS).astype(np.float32)
    inv = (1.0 / (np.float32(10000.0) ** (np.arange(0, 64, 2, dtype=np.float32) / np.float32(64)))).astype(np.float32)
    ang = (pos[None, :] * inv[:, None]).astype(np.float32)
    c = np.cos(ang).astype(np.float32)
    s = np.sin(ang).astype(np.float32)
    tb["ropeC"] = np.ascontiguousarray(np.tile(c, (4, 1)))
    tb["ropeS"] = np.ascontiguousarray(np.concatenate([-s, s, -s, s], 0))
    n1 = np.arange(128)[:, None]
    k1 = (32 * q + np.arange(32))[None, :]
    a2 = 2 * np.pi * ((n1 * k1) % 128) / 128
    tb["tabP2"] = _bf(np.concatenate([np.cos(a2), np.sin(a2)], 1))
    sel = np.zeros((128, 12), np.float32)
    if q > 0:
        sel[:, q - 1] = 1.0
        sel[:, 8] = 1.0
    if q < 3:
        sel[:, 4 + q + 1] = 1.0
        sel[:, 9] = 1.0
    tb["sel"] = sel
    return tb


def fm(v):
    return np.ascontiguousarray(np.asarray(v).reshape(8, 128).T)


def prep_inputs(cfg, inp):
    SP, SS, NSS = cfg["SP"], cfg["SS"], cfg["NSS"]
    QP = SP // 4
    shared = dict(make_tables(cfg))
    NLW = cfg.get("NLW", 4)
    for k in ("ada_w", "fnet_w", "fnet_b", "swa_w_qkv", "swa_w_o", "diff_w_qkv", "diff_w_o", "ffn_w_gate", "ffn_w_up", "ffn_w_down"):
        a = np.asarray(inp[k], dtype=np.float32)
        if k in ("ada_w", "ffn_w_gate", "ffn_w_up", "ffn_w_down"):
            a = a[:NLW]
        shared[k] = np.ascontiguousarray(a)
    ab = np.asarray(inp["ada_b"], np.float32)
    shared["ada_bT"] = np.ascontiguousarray(ab.reshape(4, 48, 128).transpose(2, 0, 1).reshape(128, 192))
    ng = np.zeros((128, 64), np.float32)
    for l in range(4):
        ng[:, l * 8:(l + 1) * 8] = fm(inp["norm1_g"][l])
        ng[:, 32 + l * 8:32 + (l + 1) * 8] = fm(inp["norm2_g"][l])
    shared["ng"] = ng
    shared["swa_g"] = np.ascontiguousarray(np.stack([np.tile(inp["swa_q_g"][0], 2), np.tile(inp["swa_k_g"][0], 2)], 1), dtype=np.float32)
    shared["swa_sink"] = np.ascontiguousarray(np.tile(np.asarray(inp["swa_sink"][0], np.float32)[None, :], (128, 1)))
    shared["diff_g"] = np.ascontiguousarray(np.stack([np.tile(inp["diff_q_g"][0], 2), np.tile(inp["diff_k_g"][0], 2)], 1), dtype=np.float32)
    shared["diff_l"] = np.ascontiguousarray(np.stack([inp["diff_lq1"][0], inp["diff_lk1"][0], inp["diff_lq2"][0], inp["diff_lk2"][0]], 1),
                                            dtype=np.float32)
    shared["diff_subln"] = np.ascontiguousarray(np.asarray(inp["diff_subln_g"][0], np.float32).reshape(128, 1))
    cv = np.zeros((128, 4 * NFC * 4), np.float32)
    for l in range(4):
        for j in range(3):
            cv[:, (l * NFC * 4 + j)::4][:, 0:NFC] = np.asarray(inp["ffn_conv_w"][l][j], np.float32).reshape(NFC, 128).T
        cv[:, (l * NFC * 4 + 3)::4][:, 0:NFC] = np.asarray(inp["ffn_conv_b"][l], np.float32).reshape(NFC, 128).T
    shared["ffn_convT"] = cv
    maps = []
    for core in range(8):
        b, q = core // 4, core % 4
        m = dict(shared)
        m.update(core_tables(cfg, core))
        xs = [inp["x_prompt"][b, q * QP:(q + 1) * QP]] + [inp["x_sample"][core * NSS + i] for i in range(NSS)]
        m["xin"] = np.ascontiguousarray(np.concatenate(xs, 0), dtype=np.float32)
        cs = np.stack([inp["c_prompt"][b]] + [inp["c_sample"][core * NSS + i] for i in range(NSS)], 0)
        m["cT"] = np.ascontiguousarray(cs.reshape(1 + NSS, 8, 128).transpose(2, 1, 0).reshape(128, 8 * (1 + NSS)), dtype=np.float32)
        maps.append(m)
    return maps


def assemble(cfg, results, B_, SB_):
    SP, SS, NSS = cfg["SP"], cfg["SS"], cfg["NSS"]
    QP = SP // 4
    yp = np.zeros((B_, SP, D), np.float32)
    ys = np.zeros((SB_, SS, D), np.float32)
    for core in range(8):
        y = results[core]["yout"]
        b, q = core // 4, core % 4
        yp[b, q * QP:(q + 1) * QP] = y[0:QP]
        for i in range(NSS):
            ys[core * NSS + i] = y[QP + i * SS:QP + (i + 1) * SS]
    return yp, ys


FULL = dict(SP=16384, SS=2048, NSS=4, NL=4)


def kernel(**inputs):
    cfg = FULL
    nc = build(cfg)
    maps = prep_inputs(cfg, inputs)
    res = run_bass_kernel_spmd(nc, maps, core_ids=list(range(8)))
    return assemble(cfg, res.results, 2, 32)
```

```python
import math
from contextlib import ExitStack
import numpy as np
import ml_dtypes
import concourse.bass as bass
import concourse.mybir as mybir
from concourse.bass_utils import run_bass_kernel_spmd

F32 = mybir.dt.float32
BF16 = mybir.dt.bfloat16
AF = mybir.ActivationFunctionType
ALU = mybir.AluOpType
D = 1024
DFF = 2816
NFC = 22
EPS = 1e-6
EPOCH = 30000
RING = 12
SLAB32 = 49152


class Buf:
    __slots__ = ("name", "w", "r")

    def __init__(self, name=""):
        self.name = name
        self.w = None
        self.r = {}


class T:
    __slots__ = ("ap", "b")

    def __init__(self, ap, b):
        self.ap = ap
        self.b = b


class Sched:
    ENG = ("pe", "act", "dve", "pool", "sp")

    def __init__(self, nc):
        self.nc = nc
        self.ops = {e: [] for e in self.ENG}
        self.cnt = {e: 0 for e in self.ENG}
        self.seen = {e: {} for e in self.ENG}
        self.ring_pos = {}
        self.ring_tok = {}
        self.keys = set()
        self.stack = ExitStack()

    def _waits(self, eng, reads, writes, extra=()):
        seen = self.seen[eng]
        waits = {}

        def need(tok):
            if tok is None:
                return
            k, v = tok
            if eng == "pe" and k[0] == "pe":
                return
            if seen.get(k, 0) >= v:
                return
            if waits.get(k, 0) < v:
                waits[k] = v

        for b in reads:
            need(b.w)
        for b in writes:
            need(b.w)
            for tok in b.r.values():
                need(tok)
        for tok in extra:
            need(tok)
        for k, v in waits.items():
            seen[k] = v
        return list(waits.items())

    def op(self, eng, fn, reads=(), writes=()):
        waits = self._waits(eng, reads, writes)
        cnt = self.cnt[eng]
        self.cnt[eng] = cnt + 1
        epoch, idx = divmod(cnt, EPOCH)
        key = (eng, epoch)
        self.keys.add(key)
        tok = (key, idx + 1)
        self.ops[eng].append((fn, waits, key, 1))
        for b in reads:
            b.r[eng] = tok
        for b in writes:
            b.w = tok
            b.r = {}
        return tok

    def dma(self, q, fn, reads=(), writes=()):
        pos = self.ring_pos.get(q, 0)
        self.ring_pos[q] = (pos + 1) % RING
        key = ("dma", q, pos)
        self.keys.add(key)
        prev = self.ring_tok.get(key)
        waits = self._waits(q, reads, writes, extra=(prev,) if prev else ())
        tok = (key, (prev[1] if prev else 0) + 16)
        self.ring_tok[key] = tok
        self.ops[q].append((fn, waits, key, 16))
        for b in reads:
            b.r[key] = tok
        for b in writes:
            b.w = tok
            b.r = {}
        return tok

    def last_tokens(self):
        toks = list(self.ring_tok.values())
        for e in self.ENG:
            c = self.cnt[e]
            if c:
                epoch, idx = divmod(c - 1, EPOCH)
                toks.append(((e, epoch), idx + 1))
        return toks

    def barrier(self):
        toks = self.last_tokens()
        for e in self.ENG:
            waits = self._waits(e, (), (), extra=toks)
            if waits:
                self.ops[e].append((None, waits, None, 0))

    def emit(self):
        nc = self.nc
        sems = {}
        for i, k in enumerate(sorted(self.keys, key=str)):
            sems[k] = self.stack.enter_context(nc.semaphore(f"s{i}"))

        def replay(e, name):
            for fn, waits, key, inc in self.ops[name]:
                for k, v in waits:
                    e.wait_ge(sems[k], v)
                if fn is not None:
                    fn(e).then_inc(sems[key], inc)

        with nc.Block() as block:
            @block.sync
            def _(e):
                replay(e, "sp")

            @block.tensor
            def _(e):
                replay(e, "pe")

            @block.scalar
            def _(e):
                replay(e, "act")

            @block.vector
            def _(e):
                replay(e, "dve")

            @block.gpsimd
            def _(e):
                replay(e, "pool")


def I_mm(out, lhsT, rhs, start, stop):
    return lambda e: e.matmul(out, lhsT=lhsT, rhs=rhs, start=start, stop=stop)


def I_tr(out, in_, ident):
    return lambda e: e.transpose(out, in_, ident)


def I_act(out, in_, func, scale=None, bias=None):
    kw = {}
    if scale is not None:
        kw["scale"] = scale
    if bias is not None:
        kw["bias"] = bias
    return lambda e: e.activation(out=out, in_=in_, func=func, **kw)


def I_ts(out, in0, s1, op0, s2=None, op1=None):
    if op1 is None:
        return lambda e: e.tensor_scalar(out=out, in0=in0, scalar1=s1, scalar2=None, op0=op0)
    return lambda e: e.tensor_scalar(out=out, in0=in0, scalar1=s1, scalar2=s2, op0=op0, op1=op1)


def I_tt(out, in0, in1, op):
    return lambda e: e.tensor_tensor(out=out, in0=in0, in1=in1, op=op)


def I_stt(out, in0, scalar, in1, op0, op1):
    return lambda e: e.scalar_tensor_tensor(out=out, in0=in0, scalar=scalar, in1=in1, op0=op0, op1=op1)


def I_cp(out, in_):
    return lambda e: e.tensor_copy(out=out, in_=in_)


def I_rec(out, in_):
    return lambda e: e.reciprocal(out=out, in_=in_)


def I_ms(ap, v):
    return lambda e: e.memset(ap, v)


def I_dma(out, in_, slow=False):
    if slow:
        return lambda e: e.dma_start(out=out, in_=in_, allow_slow_non_contiguous=True)
    return lambda e: e.dma_start(out=out, in_=in_)


class _Stop(Exception):
    pass


def build(cfg):
    SP, SS, NSS, NL = cfg["SP"], cfg["SS"], cfg["NSS"], cfg["NL"]
    NLW = cfg.get("NLW", 4)
    STOP = cfg.get("STOP", None)
    QP = SP // 4
    N2 = SP // 128
    NT = QP + NSS * SS
    NS = 1 + NSS
    NTS = SS // 128
    NKG = SS // 512
    segs = [(0, QP, 0)] + [(QP + i * SS, SS, 1 + i) for i in range(NSS)]
    groups = [(s0 + g * 512, s0, sl, sq) for (s0, sl, sq) in segs for g in range(sl // 512)]
    nc = bass.Bass("TRN2", target_bir_lowering=False)

    def din(name, shape, dt=F32):
        return nc.dram_tensor(name, list(shape), dt, kind="ExternalInput").ap()

    def dsc(name, shape, dt):
        return nc.dram_tensor(name, list(shape), dt).ap()

    xin = din("xin", [NT, D])
    cT_in = din("cT", [128, 8 * NS])
    ada_w = din("ada_w", [NLW, D, 6 * D])
    ada_bT = din("ada_bT", [128, 4 * 48])
    ng_in = din("ng", [128, 64])
    fnet_w = din("fnet_w", [2, D, D])
    fnet_b = din("fnet_b", [2, D])
    swa_w_qkv = din("swa_w_qkv", [1, D, 1536])
    swa_g = din("swa_g", [128, 2])
    swa_sink = din("swa_sink", [128, 16])
    swa_w_o = din("swa_w_o", [1, D, D])
    diff_w_qkv = din("diff_w_qkv", [1, D, 3072])
    diff_g = din("diff_g", [128, 2])
    diff_l = din("diff_l", [64, 4])
    diff_subln = din("diff_subln", [128, 1])
    diff_w_o = din("diff_w_o", [1, D, D])
    ffn_w_gate = din("ffn_w_gate", [NLW, D, DFF])
    ffn_w_up = din("ffn_w_up", [NLW, D, DFF])
    ffn_convT = din("ffn_convT", [128, 4 * NFC * 4])
    ffn_w_down = din("ffn_w_down", [NLW, DFF, D])
    ropeC = din("ropeC", [128, NT])
    ropeS = din("ropeS", [128, NT])
    chCS = din("chCS", [128, 2 * 512], BF16)
    tabS = din("tabS", [NKG, 128, NTS * 2 * 512], BF16)
    tabP1 = din("tabP1", [128, N2, 3 * N2], BF16)
    tabP2 = din("tabP2", [128, 64], BF16)
    swamask = din("swamask", [128, 2 * 256], BF16)
    sel_in = din("sel", [128, 12])
    ident_in = din("ident", [128, 128])
    c16_in = din("c16", [128, 1152], BF16)
    yout = nc.dram_tensor("yout", [NT, D], F32, kind="ExternalOutput").ap()

    XTa = dsc("XTa", [8, 128, NT], F32)
    XTb = dsc("XTb", [8, 128, NT], F32)
    MTd = dsc("MTd", [8, 128, NT], BF16)
    gu16 = dsc("gu16", [max(NL, 1), NFC, 128, 2048], BF16)
    wd16 = dsc("wd16", [max(NL, 1), DFF, D], BF16)
    MQ = QP // 128
    zin_c = [dsc(f"zin{i}", [8 * MQ, 2048], BF16) for i in range(16)]
    zall_c = [dsc(f"zall{i}", [4 * 8 * MQ, 2048], BF16) for i in range(16)]
    zs = dsc("zs", [max(NSS * SS, 128), 2048], BF16)
    vbuf = dsc("vbuf", [128, N2, 2048], BF16)
    QT = dsc("QT", [8, 128, NT], BF16)
    KT = dsc("KT", [8, 128, NT], BF16)
    kin_h = [dsc(f"kin{i}", [128, QP], BF16) for i in range(8)]
    kall_h = [dsc(f"kall{i}", [4 * 128, QP], BF16) for i in range(8)]
    Vt = dsc("Vt", [NT, 1024], BF16)
    vin_h = [dsc(f"vin{i}", [QP, 128], BF16) for i in range(8)]
    vall_h = [dsc(f"vall{i}", [SP, 128], BF16) for i in range(8)]
    sbin = dsc("sbin", [2 * 128, 1024], BF16)
    sball = dsc("sball", [4 * 2 * 128, 1024], BF16)
    bnd_in = dsc("bnd_in", [128, 16], F32)
    bnd_all = dsc("bnd_all", [4 * 128, 16], F32)
    RG = [[0, 1, 2, 3], [4, 5, 6, 7]]

    S = Sched(nc)
    slab = S.stack.enter_context(nc.sbuf_tensor("slab", [128, SLAB32], F32))
    PSB = [T(S.stack.enter_context(nc.psum_tensor(f"ps{i}", [128, 512], F32))[:], Buf()) for i in range(8)]
    psq = [T(PSB[7].ap[:, i * 128:(i + 1) * 128], Buf()) for i in range(4)]
    st = {"top": 0, "ps": 0, "psq": 0}

    def alloc(free, dt, name=""):
        n = int(np.prod(free))
        n32 = n if dt == F32 else (n + 1) // 2
        o = st["top"]
        st["top"] = o + (n32 + 15) // 16 * 16
        assert st["top"] <= SLAB32, ("SBUF arena overflow", name, st["top"])
        ap = slab[:, o:o + n32]
        if dt == BF16:
            ap = ap.bitcast(BF16)[:, 0:n]
        if len(free) == 2:
            ap = ap.rearrange("p (a b) -> p a b", b=free[1])
        elif len(free) == 3:
            ap = ap.rearrange("p (a b c) -> p a b c", b=free[1], c=free[2])
        return T(ap, Buf(name))

    def nextps():
        i = st["ps"]
        st["ps"] = (i + 1) % 7
        return PSB[i]

    def nextpsq():
        i = st["psq"]
        st["psq"] = (i + 1) % 4
        return psq[i]

    def mm(out, lhsT, rhs, start, stop, reads, wr):
        S.op("pe", I_mm(out, lhsT, rhs, start, stop), reads, [wr])

    C16 = alloc((1152,), BF16)
    IDN = alloc((128,), F32)
    ONE32 = alloc((128,), F32)
    MH = alloc((512,), F32)
    SEL = alloc((12,), F32)
    MASK = alloc((2, 256), BF16)
    MASKE = alloc((2, 256), BF16)
    CONV = alloc((4 * NFC * 4,), F32)
    NG = alloc((64,), F32)
    ADAB = alloc((4 * 48,), F32)
    CHCS = alloc((2, 512), BF16)
    modv = [[alloc((8, NS), F32) for j in range(6)] for l in range(NL)]
    s1 = [alloc((8, NS), F32) for l in range(NL)]
    s2 = [alloc((8, NS), F32) for l in range(NL)]
    BSEL = alloc((8, 2), F32)
    persist_top = st["top"]
    OM1024 = C16.ap[:, 0:128]
    BLK64 = C16.ap[:, 128:256]
    OM128 = C16.ap[:, 256:384]
    ONES = C16.ap[:, 384:512]
    SWAP = C16.ap[:, 512:640]
    ONEROW = C16.ap[0:1, 640:1152]

    for t, src in ((C16, c16_in), (IDN, ident_in), (SEL, sel_in), (CONV, ffn_convT), (NG, ng_in),
                   (ADAB, ada_bT), (CHCS, chCS.rearrange("p (a b) -> p a b", b=512)),
                   (MASK, swamask.rearrange("p (a b) -> p a b", b=256))):
        S.dma("sp", I_dma(t.ap, src), writes=[t.b])
    S.op("pool", I_ms(MH.ap, -0.5), writes=[MH.b])
    S.op("pool", I_ms(ONE32.ap, 1.0), writes=[ONE32.b])
    S.op("dve", I_ts(MASKE.ap[:, 0, :], MASK.ap[:, 0, :], SEL.ap[:, 8:9], ALU.mult), [MASK.b, SEL.b], [MASKE.b])
    S.op("dve", I_ts(MASKE.ap[:, 1, :], MASK.ap[:, 1, :], SEL.ap[:, 9:10], ALU.mult), [MASK.b, SEL.b], [MASKE.b])

    def stage_begin():
        S.barrier()
        st["top"] = persist_top

    def ck(name):
        if STOP == name:
            raise _Stop()

    def xt_view(X, t0, w):
        return X[:, :, t0:t0 + w].rearrange("c p t -> p c t")

    for l in range(NL):
        for (wsrc, j) in ((ffn_w_gate, 0), (ffn_w_up, 1)):
            src = wsrc[l].rearrange("(kc p) (fc j) -> fc p kc j", p=128, j=128)
            dst = gu16[l].rearrange("fc p (two kc j) -> fc p two kc j", two=2, j=128)
            for fc in range(NFC):
                S.dma("pool", I_dma(dst[fc, :, j], src[fc]))
        for h in range(2):
            S.dma("pool", I_dma(wd16[l, h * 1408:(h + 1) * 1408, :], ffn_w_down[l, h * 1408:(h + 1) * 1408, :]))

    def stage_mod():
        stage_begin()
        cTt = alloc((8, NS), F32)
        cact = alloc((8, NS), BF16)
        S.dma("sp", I_dma(cTt.ap, cT_in.rearrange("p (a b) -> p a b", b=NS)), writes=[cTt.b])
        S.op("act", I_act(cact.ap, cTt.ap, AF.Silu), [cTt.b], [cact.b])
        wbuf = [alloc((8, 1024), BF16) for _ in range(2)]
        i = 0
        for l in range(NL):
            for j in range(6):
                w = wbuf[i % 2]
                i += 1
                S.dma("pool", I_dma(w.ap, ada_w[l, :, j * 1024:(j + 1) * 1024].rearrange("(kc p) n -> p kc n", p=128)),
                      writes=[w.b])
                for dc in range(8):
                    ps = nextps()
                    for kc in range(8):
                        mm(ps.ap[:, 0:NS], w.ap[:, kc, dc * 128:(dc + 1) * 128], cact.ap[:, kc, :], kc == 0, kc == 7,
                           [w.b, cact.b], ps.b)
                    c0 = l * 48 + j * 8 + dc
                    S.op("act", I_act(modv[l][j].ap[:, dc, :], ps.ap[:, 0:NS], AF.Identity, bias=ADAB.ap[:, c0:c0 + 1]),
                         [ps.b, ADAB.b], [modv[l][j].b])
            for kc in range(8):
                S.op("dve", I_ts(s1[l].ap[:, kc, :], modv[l][1].ap[:, kc, :], 1.0, ALU.add,
                                 NG.ap[:, l * 8 + kc:l * 8 + kc + 1], ALU.mult), [modv[l][1].b, NG.b], [s1[l].b])
                S.op("dve", I_ts(s2[l].ap[:, kc, :], modv[l][4].ap[:, kc, :], 1.0, ALU.add,
                                 NG.ap[:, 32 + l * 8 + kc:32 + l * 8 + kc + 1], ALU.mult), [modv[l][4].b, NG.b], [s2[l].b])

    def stage_init():
        stage_begin()
        xt = [alloc((1024,), F32) for _ in range(2)]
        xo = [alloc((8, 128), F32) for _ in range(2)]
        for tt in range(NT // 128):
            x = xt[tt % 2]
            o = xo[tt % 2]
            S.dma("sp", I_dma(x.ap, xin[tt * 128:(tt + 1) * 128, :]), writes=[x.b])
            for hh in range(2):
                ps = nextps()
                for k in range(4):
                    kc = hh * 4 + k
                    S.op("pe", I_tr(ps.ap[:, k * 128:(k + 1) * 128], x.ap[:, kc * 128:(kc + 1) * 128], IDN.ap),
                         [x.b, IDN.b], [ps.b])
                S.op("act" if hh else "dve",
                     (I_act(o.ap[:, 4:8, :], ps.ap.rearrange("p (a b) -> p a b", b=128), AF.Copy) if hh else
                      I_cp(o.ap[:, 0:4, :], ps.ap.rearrange("p (a b) -> p a b", b=128))), [ps.b], [o.b])
            S.dma("sp", I_dma(xt_view(XTb, tt * 128, 128), o.ap), reads=[o.b])

    def stage_final(X):
        stage_begin()
        xi = [alloc((8, 128), F32) for _ in range(2)]
        yo = [alloc((1024,), F32) for _ in range(2)]
        for tt in range(NT // 128):
            x = xi[tt % 2]
            o = yo[tt % 2]
            S.dma("sp", I_dma(x.ap, xt_view(X, tt * 128, 128)), writes=[x.b])
            for hh in range(2):
                ps = nextps()
                for k in range(4):
                    S.op("pe", I_tr(ps.ap[:, k * 128:(k + 1) * 128], x.ap[:, hh * 4 + k, :], IDN.ap), [x.b, IDN.b], [ps.b])
                S.op("act" if hh else "dve",
                     (I_act(o.ap[:, 512:1024], ps.ap, AF.Copy) if hh else I_cp(o.ap[:, 0:512], ps.ap)), [ps.b], [o.b])
            S.dma("sp", I_dma(yout[tt * 128:(tt + 1) * 128, :], o.ap), reads=[o.b])

    def norm_bufs(W):
        return dict(sq=alloc((8, W), BF16), t=alloc((W,), F32), R=alloc((W,), F32),
                    tmp=[alloc((W,), F32) for _ in range(2)])

    def norm(x, W, sT, shT, seq, hT, c0, nb, xap=None):
        xap = x.ap[:, :, 0:W] if xap is None else xap
        S.op("act", I_act(nb["sq"].ap[:, :, 0:W], xap, AF.Square), [x.b], [nb["sq"].b])
        ps = nextps()
        for kc in range(8):
            mm(ps.ap[:, 0:W], OM1024, nb["sq"].ap[:, kc, 0:W], kc == 0, kc == 7, [nb["sq"].b, C16.b], ps.b)
        S.op("dve", I_ts(nb["t"].ap[:, 0:W], ps.ap[:, 0:W], EPS, ALU.add), [ps.b], [nb["t"].b])
        S.op("act", I_act(nb["t"].ap[:, 0:W], nb["t"].ap[:, 0:W], AF.Ln), [nb["t"].b], [nb["t"].b])
        S.op("act", I_act(nb["R"].ap[:, 0:W], nb["t"].ap[:, 0:W], AF.Exp, scale=-0.5), [nb["t"].b], [nb["R"].b])
        for kc in range(8):
            tmp = nb["tmp"][kc % 2]
            S.op("dve", I_tt(tmp.ap[:, 0:W], xap[:, kc, :], nb["R"].ap[:, 0:W], ALU.mult), [x.b, nb["R"].b], [tmp.b])
            S.op("act", I_act(hT.ap[:, kc, c0:c0 + W], tmp.ap[:, 0:W], AF.Identity, scale=sT.ap[:, kc, seq:seq + 1],
                              bias=shT.ap[:, kc, seq:seq + 1]), [tmp.b, sT.b, shT.b], [hT.b])

    def stage_tail(l, w_in, b_in, Xs, Xd):
        stage_begin()
        wo = alloc((8, 1024), BF16)
        S.dma("pool", I_dma(wo.ap, w_in.rearrange("(kc p) n -> p kc n", p=128)), writes=[wo.b])
        if b_in is not None:
            brow = alloc((1024,), BF16)
            S.dma("pool", I_dma(brow.ap[0:1, :], b_in.rearrange("(o n) -> o n", o=1)), writes=[brow.b])
        xg = [alloc((8, 512), F32) for _ in range(2)]
        mt = [alloc((8, 512), BF16) for _ in range(2)]
        for gi, (t0, s0, sl, seq) in enumerate(groups):
            x = xg[gi % 2]
            m = mt[gi % 2]
            S.dma("sp", I_dma(x.ap, xt_view(Xs, t0, 512)), writes=[x.b])
            S.dma("sp", I_dma(m.ap, xt_view(MTd, t0, 512)), writes=[m.b])
            for dc in range(8):
                ps = nextps()
                for kc in range(8):
                    mm(ps.ap, wo.ap[:, kc, dc * 128:(dc + 1) * 128], m.ap[:, kc, :], kc == 0, kc == 7 and b_in is None,
                       [wo.b, m.b], ps.b)
                if b_in is not None:
                    mm(ps.ap, brow.ap[0:1, dc * 128:(dc + 1) * 128], ONEROW, False, True, [brow.b, C16.b], ps.b)
                S.op("dve", I_stt(x.ap[:, dc, :], ps.ap, modv[l][2].ap[:, dc, seq:seq + 1], x.ap[:, dc, :], ALU.mult, ALU.add),
                     [ps.b, x.b, modv[l][2].b], [x.b])
            S.dma("sp", I_dma(xt_view(Xd, t0, 512), x.ap), reads=[x.b])

    def stage_bnd(X):
        stage_begin()
        bv = bnd_in.rearrange("p (c two) -> p c two", two=2)
        S.dma("sp", I_dma(bv[:, :, 0:1], xt_view(X, 0, 1), True))
        S.dma("sp", I_dma(bv[:, :, 1:2], xt_view(X, QP - 1, 1), True))
        S.barrier()
        bg = Buf()
        S.op("pool", lambda e: e.collective_compute("AllGather", ALU.bypass, replica_groups=RG, ins=[bnd_in.opt()],
                                                    outs=[bnd_all.opt()]), (), [bg])
        ba = alloc((4, 8, 2), F32)
        S.dma("pool", I_dma(ba.ap, bnd_all.rearrange("(r p) (c two) -> p r c two", p=128, two=2)), reads=[bg], writes=[ba.b])
        for side, (col, so) in enumerate(((1, 0), (0, 4))):
            S.op("dve", I_ts(BSEL.ap[:, :, side:side + 1], ba.ap[:, 0, :, col:col + 1], SEL.ap[:, so:so + 1], ALU.mult),
                 [ba.b, SEL.b], [BSEL.b])
            for r in range(1, 4):
                S.op("dve", I_stt(BSEL.ap[:, :, side:side + 1], ba.ap[:, r, :, col:col + 1], SEL.ap[:, so + r:so + r + 1],
                                  BSEL.ap[:, :, side:side + 1], ALU.mult, ALU.add), [ba.b, SEL.b, BSEL.b], [BSEL.b])

    def stage_ffn(l, Xs, Xd):
        stage_begin()
        xg = [alloc((8, 514), F32) for _ in range(2)]
        hT = [alloc((8, 514), BF16) for _ in range(2)]
        hh = alloc((8, 2), BF16)
        nb = norm_bufs(512)
        nbh = norm_bufs(2)
        gx = [alloc((514,), F32) for _ in range(2)]
        acc = [alloc((512,), F32) for _ in range(2)]
        sg = [alloc((512,), F32) for _ in range(2)]
        actT = alloc((NFC, 512), BF16)
        gu = [alloc((2, 8, 128), BF16) for _ in range(4)]
        wdh = [alloc((11, 1024), BF16) for _ in range(2)]
        k = 0
        for gi, (t0, s0, sl, seq) in enumerate(groups):
            x = xg[gi % 2]
            h = hT[gi % 2]
            lin = t0 > s0
            rin = t0 + 512 < s0 + sl
            c_lo = 0 if lin else 1
            c_hi = 514 if rin else 513
            S.dma("sp", I_dma(x.ap[:, :, c_lo:c_hi], xt_view(Xs, t0 - 1 + c_lo, c_hi - c_lo)), writes=[x.b])
            flags = []
            for side, col, inside in ((0, 0, lin), (1, 513, rin)):
                if inside:
                    flags.append(1.0)
                elif seq == 0:
                    S.op("dve", I_cp(x.ap[:, :, col:col + 1], BSEL.ap[:, :, side:side + 1]), [BSEL.b], [x.b])
                    flags.append(SEL.ap[:, 8 + side:9 + side])
                else:
                    S.op("dve", I_ms(x.ap[:, :, col:col + 1], 0.0), (), [x.b])
                    flags.append(0.0)
            norm(x, 512, s2[l], modv[l][3], seq, h, 1, nb, xap=x.ap[:, :, 1:513])
            norm(x, 2, s2[l], modv[l][3], seq, hh, 0, nbh, xap=x.ap[:, :, 0:514:513])
            for side, col in ((0, 0), (1, 513)):
                S.op("dve", I_ts(h.ap[:, :, col:col + 1], hh.ap[:, :, side:side + 1], flags[side], ALU.mult),
                     [hh.b, SEL.b], [h.b])
            for hf in range(2):
                S.dma("sp", I_dma(wdh[hf].ap, wd16[l, hf * 1408:(hf + 1) * 1408, :].rearrange("(fc p) d -> p fc d", p=128)),
                      writes=[wdh[hf].b])
            for fc in range(NFC):
                w = gu[k % 4]
                k += 1
                S.dma("sp", I_dma(w.ap, gu16[l, fc].rearrange("p (two kc j) -> p two kc j", two=2, j=128)), writes=[w.b])
                psg = nextps()
                for kc in range(8):
                    mm(psg.ap, w.ap[:, 0, kc, :], h.ap[:, kc, 1:513], kc == 0, kc == 7, [w.b, h.b], psg.b)
                psh = nextps()
                for kc in range(8):
                    mm(psh.ap[:, 0:2], w.ap[:, 0, kc, :], h.ap[:, kc, 0:514:513], kc == 0, kc == 7, [w.b, h.b], psh.b)
                psu = nextps()
                for kc in range(8):
                    mm(psu.ap, w.ap[:, 1, kc, :], h.ap[:, kc, 1:513], kc == 0, kc == 7, [w.b, h.b], psu.b)
                g = gx[fc % 2]
                a = acc[fc % 2]
                sgt = sg[fc % 2]
                S.op("act", I_act(g.ap[:, 1:513], psg.ap, AF.Copy), [psg.b], [g.b])
                S.op("act", I_act(g.ap[:, 0:514:513], psh.ap[:, 0:2], AF.Copy), [psh.b], [g.b])
                cb = (l * NFC + fc) * 4
                cw = CONV.ap
                S.op("dve", I_ts(a.ap, g.ap[:, 1:513], cw[:, cb + 1:cb + 2], ALU.mult, cw[:, cb + 3:cb + 4], ALU.add),
                     [g.b, CONV.b], [a.b])
                S.op("dve", I_stt(a.ap, g.ap[:, 0:512], cw[:, cb:cb + 1], a.ap, ALU.mult, ALU.add), [g.b, CONV.b, a.b], [a.b])
                S.op("dve", I_stt(a.ap, g.ap[:, 2:514], cw[:, cb + 2:cb + 3], a.ap, ALU.mult, ALU.add), [g.b, CONV.b, a.b], [a.b])
                S.op("act", I_act(sgt.ap, a.ap, AF.Silu), [a.b], [sgt.b])
                S.op("dve", I_tt(actT.ap[:, fc, :], sgt.ap, psu.ap, ALU.mult), [sgt.b, psu.b], [actT.b])
            for dc in range(8):
                ps = nextps()
                for fc in range(NFC):
                    mm(ps.ap, wdh[fc // 11].ap[:, fc % 11, dc * 128:(dc + 1) * 128], actT.ap[:, fc, :], fc == 0, fc == NFC - 1,
                       [wdh[fc // 11].b, actT.b], ps.b)
                S.op("dve", I_stt(x.ap[:, dc, 1:513], ps.ap, modv[l][5].ap[:, dc, seq:seq + 1], x.ap[:, dc, 1:513], ALU.mult, ALU.add),
                     [ps.b, x.b, modv[l][5].b], [x.b])
            S.dma("sp", I_dma(xt_view(Xd, t0, 512), x.ap[:, :, 1:513]), reads=[x.b])

    def stage_fourier(l, j, Xs):
        stage_begin()
        xg = [alloc((8, 512), F32) for _ in range(2)]
        hT = [alloc((8, 512), BF16) for _ in range(2)]
        nbs = [norm_bufs(512) for _ in range(2)]
        zt = [alloc((4, 2048), BF16) for _ in range(2)]
        for gi, (t0, s0, sl, seq) in enumerate(groups):
            x = xg[gi % 2]
            h = hT[gi % 2]
            z = zt[gi % 2]
            S.dma("sp", I_dma(x.ap, xt_view(Xs, t0, 512)), writes=[x.b])
            norm(x, 512, s1[l], modv[l][0], seq, h, 0, nbs[gi % 2])
            for tt in range(4):
                for grp in range(4):
                    ps = nextps()
                    for jj in range(2):
                        mm(ps.ap, h.ap[:, 2 * grp + jj, tt * 128:(tt + 1) * 128], CHCS.ap[:, jj, :], jj == 0, jj == 1,
                           [h.b, CHCS.b], ps.b)
                    dst = z.ap[:, tt, :].rearrange("p (two g c) -> p two g c", two=2, c=256)[:, :, grp, :]
                    src = ps.ap.rearrange("p (two c) -> p two c", two=2)
                    if (tt * 4 + grp) % 2:
                        S.op("act", I_act(dst, src, AF.Copy), [ps.b], [z.b])
                    else:
                        S.op("dve", I_cp(dst, src), [ps.b], [z.b])
            if seq == 0:
                m0 = t0 // 128
                for i in range(16):
                    S.dma("sp", I_dma(zin_c[i].rearrange("(a m) c -> a m c", a=8)[:, m0:m0 + 4, :], z.ap[8 * i:8 * i + 8, :, :]),
                          reads=[z.b])
            else:
                drows = zs[t0 - QP:t0 - QP + 512, :]
                S.dma("sp", I_dma(drows.rearrange("(tt p) c -> p tt c", p=128), z.ap), reads=[z.b])
        ck("A1")
        S.barrier()
        bg = Buf()
        for i in range(16):
            S.op("pool", (lambda a, b: lambda e: e.collective_compute("AllGather", ALU.bypass, replica_groups=RG, ins=[a.opt()],
                                                                      outs=[b.opt()]))(zin_c[i], zall_c[i]), (), [bg])
        ck("AG")
        stage_begin()
        zsb = alloc((NTS, 2048), BF16)
        tb = [alloc((NTS, 2, 512), BF16) for _ in range(2)]
        mo = [alloc((8, 512), BF16) for _ in range(2)]
        ti = 0
        for si in range(NSS):
            S.dma("sp", I_dma(zsb.ap, zs[si * SS:(si + 1) * SS, :].rearrange("(nt p) c -> p nt c", p=128)), writes=[zsb.b])
            for kg in range(NKG):
                tbt = tb[ti % 2]
                m = mo[ti % 2]
                ti += 1
                S.dma("sp", I_dma(tbt.ap, tabS[kg].rearrange("p (nt two k) -> p nt two k", two=2, k=512)), writes=[tbt.b])
                for cc in range(8):
                    ps = nextps()
                    for nt in range(NTS):
                        mm(ps.ap, zsb.ap[:, nt, cc * 128:(cc + 1) * 128], tbt.ap[:, nt, 0, :], nt == 0, False, [zsb.b, tbt.b], ps.b)
                        mm(ps.ap, zsb.ap[:, nt, 1024 + cc * 128:1024 + (cc + 1) * 128], tbt.ap[:, nt, 1, :], False, nt == NTS - 1,
                           [zsb.b, tbt.b], ps.b)
                    S.op("act", I_act(m.ap[:, cc, :], ps.ap, AF.Copy), [ps.b], [m.b])
                t0 = QP + si * SS + kg * 512
                S.dma("sp", I_dma(xt_view(MTd, t0, 512), m.ap), reads=[m.b])
        ck("A2s")
        stage_begin()
        zt1 = [alloc((2048,), BF16) for _ in range(3)]
        tb1 = [alloc((3, N2), BF16) for _ in range(3)]
        vt = [alloc((2048,), BF16) for _ in range(3)]
        for n1 in range(128):
            z = zt1[n1 % 3]
            tbl = tb1[n1 % 3]
            v = vt[n1 % 3]
            ci, ca = n1 // 8, n1 % 8
            for r in range(4):
                S.dma("sp", I_dma(z.ap[r * MQ:(r + 1) * MQ, :], zall_c[ci][r * 8 * MQ + ca * MQ:r * 8 * MQ + (ca + 1) * MQ, :]),
                      reads=[bg], writes=[z.b])
            S.dma("pool", I_dma(tbl.ap[0:N2], tabP1[n1].rearrange("p (a b) -> p a b", b=N2)), writes=[tbl.b])
            for hf in range(2):
                A_ = z.ap[0:N2, hf * 512:(hf + 1) * 512]
                B_ = z.ap[0:N2, 1024 + hf * 512:1024 + (hf + 1) * 512]
                psr = nextps()
                mm(psr.ap[0:N2, :], tbl.ap[0:N2, 0, :], A_, True, False, [z.b, tbl.b], psr.b)
                mm(psr.ap[0:N2, :], tbl.ap[0:N2, 1, :], B_, False, True, [z.b, tbl.b], psr.b)
                psi = nextps()
                mm(psi.ap[0:N2, :], tbl.ap[0:N2, 2, :], B_, True, False, [z.b, tbl.b], psi.b)
                mm(psi.ap[0:N2, :], tbl.ap[0:N2, 1, :], A_, False, True, [z.b, tbl.b], psi.b)
                S.op("act", I_act(v.ap[0:N2, hf * 512:(hf + 1) * 512], psr.ap[0:N2, :], AF.Copy), [psr.b], [v.b])
                S.op("dve", I_cp(v.ap[0:N2, 1024 + hf * 512:1024 + (hf + 1) * 512], psi.ap[0:N2, :]), [psi.b], [v.b])
            S.dma("sp", I_dma(vbuf[n1], v.ap[0:N2, :]), reads=[v.b])
        ck("P1")
        stage_begin()
        T2 = alloc((64,), BF16)
        S.dma("sp", I_dma(T2.ap, tabP2), writes=[T2.b])
        vt2 = [alloc((2048,), BF16) for _ in range(4)]
        MT = alloc((8, QP), BF16)
        KB = min(8, N2)
        for kb in range(N2 // KB):
            banks = [nextps() for _ in range(4)]
            for kl in range(KB):
                k2 = kb * KB + kl
                v = vt2[k2 % 4]
                S.dma("sp", I_dma(v.ap, vbuf[:, k2, :]), writes=[v.b])
                for cc in range(8):
                    bk = banks[cc // 2]
                    o = bk.ap[:, ((cc % 2) * KB + kl) * 32:((cc % 2) * KB + kl + 1) * 32]
                    mm(o, v.ap[:, cc * 128:(cc + 1) * 128], T2.ap[:, 0:32], True, False, [v.b, T2.b], bk.b)
                    mm(o, v.ap[:, 1024 + cc * 128:1024 + (cc + 1) * 128], T2.ap[:, 32:64], False, True, [v.b, T2.b], bk.b)
            for b4 in range(4):
                src = banks[b4].ap[:, 0:2 * KB * 32].rearrange("p (c k a) -> p c k a", c=2, a=32)
                dst = MT.ap[:, 2 * b4:2 * b4 + 2, :].rearrange("p c (a k) -> p c k a", k=N2)[:, :, kb * KB:(kb + 1) * KB, :]
                for c in range(2):
                    S.op("dve" if c else "act", (I_cp(dst[:, c], src[:, c]) if c else I_act(dst[:, c], src[:, c], AF.Copy)),
                         [banks[b4].b], [MT.b])
        for g in range(QP // 512):
            S.dma("sp", I_dma(xt_view(MTd, g * 512, 512), MT.ap[:, :, g * 512:(g + 1) * 512]), reads=[MT.b])

    def qk_post(ps, gcol, G_, o, qb, cs, sn):
        sq, t_, R_, qn, t1, t2 = qb
        S.op("act", I_act(sq.ap, ps.ap, AF.Square), [ps.b], [sq.b])
        ps2 = nextps()
        mm(ps2.ap, BLK64, sq.ap, True, True, [sq.b, C16.b], ps2.b)
        S.op("dve", I_ts(t_.ap, ps2.ap, EPS, ALU.add), [ps2.b], [t_.b])
        S.op("act", I_act(t_.ap, t_.ap, AF.Ln), [t_.b], [t_.b])
        S.op("act", I_act(R_.ap, t_.ap, AF.Exp, scale=-0.5), [t_.b], [R_.b])
        S.op("dve", I_stt(qn.ap, ps.ap, G_.ap[:, gcol:gcol + 1], R_.ap, ALU.mult, ALU.mult), [ps.b, G_.b, R_.b], [qn.b])
        ps3 = nextps()
        mm(ps3.ap, SWAP, qn.ap, True, True, [qn.b, C16.b], ps3.b)
        S.op("pool", I_tt(t1.ap, qn.ap, cs.ap, ALU.mult), [qn.b, cs.b], [t1.b])
        S.op("dve", I_tt(t2.ap, ps3.ap, sn.ap, ALU.mult), [ps3.b, sn.b], [t2.b])
        S.op("dve", I_tt(o.ap, t1.ap, t2.ap, ALU.add), [t1.b, t2.b], [o.b])

    def qk_bufs():
        return (alloc((512,), BF16), alloc((512,), F32), alloc((512,), F32), alloc((512,), BF16),
                alloc((512,), F32), alloc((512,), F32))

    def stage_diff(l, Xs):
        lam_init = 0.8 - 0.6 * math.exp(-0.3 * l)
        stage_begin()
        wq = alloc((8, 3072), BF16)
        S.dma("pool", I_dma(wq.ap, diff_w_qkv[0].rearrange("(kc p) n -> p kc n", p=128)), writes=[wq.b])
        G_ = alloc((2,), F32)
        S.dma("sp", I_dma(G_.ap, diff_g), writes=[G_.b])
        xg = [alloc((8, 512), F32) for _ in range(2)]
        hT = [alloc((8, 512), BF16) for _ in range(2)]
        nb = norm_bufs(512)
        qbs = [qk_bufs() for _ in range(2)]
        cs = [alloc((512,), F32) for _ in range(2)]
        sn = [alloc((512,), F32) for _ in range(2)]
        qo = [alloc((512,), BF16) for _ in range(4)]
        vo = [alloc((4, 1024), BF16) for _ in range(2)]
        qi = 0
        for gi, (t0, s0, sl, seq) in enumerate(groups):
            x = xg[gi % 2]
            h = hT[gi % 2]
            c_ = cs[gi % 2]
            s_ = sn[gi % 2]
            v = vo[gi % 2]
            S.dma("sp", I_dma(x.ap, xt_view(Xs, t0, 512)), writes=[x.b])
            S.dma("sp", I_dma(c_.ap, ropeC[:, t0:t0 + 512]), writes=[c_.b])
            S.dma("sp", I_dma(s_.ap, ropeS[:, t0:t0 + 512]), writes=[s_.b])
            norm(x, 512, s1[l], modv[l][0], seq, h, 0, nb)
            for oc in range(16):
                ps = nextps()
                for kc in range(8):
                    mm(ps.ap, wq.ap[:, kc, oc * 128:(oc + 1) * 128], h.ap[:, kc, :], kc == 0, kc == 7, [wq.b, h.b], ps.b)
                o = qo[qi % 4]
                qi += 1
                qk_post(ps, 0 if oc < 8 else 1, G_, o, qbs[qi % 2], c_, s_)
                if oc < 8:
                    dst = QT[oc, :, t0:t0 + 512]
                elif seq == 0:
                    dst = kin_h[oc - 8][:, t0:t0 + 512]
                else:
                    dst = KT[oc - 8, :, t0:t0 + 512]
                S.dma("sp", I_dma(dst, o.ap), reads=[o.b])
            for tt in range(4):
                for nbk in range(2):
                    ps = nextps()
                    for kc in range(8):
                        mm(ps.ap, h.ap[:, kc, tt * 128:(tt + 1) * 128], wq.ap[:, kc, 2048 + nbk * 512:2048 + (nbk + 1) * 512],
                           kc == 0, kc == 7, [wq.b, h.b], ps.b)
                    S.op("act", I_act(v.ap[:, tt, nbk * 512:(nbk + 1) * 512], ps.ap, AF.Copy), [ps.b], [v.b])
            if seq == 0:
                for hd in range(8):
                    S.dma("sp", I_dma(vin_h[hd][t0:t0 + 512, :].rearrange("(tt p) e -> p tt e", p=128), v.ap[:, :, hd * 128:(hd + 1) * 128]),
                          reads=[v.b])
            else:
                S.dma("sp", I_dma(Vt[t0:t0 + 512, :].rearrange("(tt p) c -> p tt c", p=128), v.ap), reads=[v.b])
        S.barrier()
        bgk = Buf()
        bgv = Buf()
        for hd in range(8):
            S.op("pool", (lambda a, b: lambda e: e.collective_compute("AllGather", ALU.bypass, replica_groups=RG, ins=[a.opt()],
                                                                      outs=[b.opt()]))(kin_h[hd], kall_h[hd]), (), [bgk])
            S.op("pool", (lambda a, b: lambda e: e.collective_compute("AllGather", ALU.bypass, replica_groups=RG, ins=[a.opt()],
                                                                      outs=[b.opt()]))(vin_h[hd], vall_h[hd]), (), [bgv])
        stage_begin()
        LKM = max(SP, SS)
        Khs = [alloc((LKM,), BF16) for _ in range(2)]
        Vhs = [alloc((LKM // 128, 128), BF16) for _ in range(2)]
        Qhs = [alloc((max(QP, SS),), BF16) for _ in range(2)]
        hi_ = 0
        Et = [alloc((512,), BF16) for _ in range(4)]
        r0 = alloc((512,), F32)
        r1 = alloc((512,), F32)
        a0 = alloc((512,), F32)
        a1 = alloc((512,), F32)
        sq = alloc((512,), BF16)
        t_ = alloc((512,), F32)
        R_ = alloc((512,), F32)
        mo = [alloc((512,), BF16) for _ in range(2)]
        L4 = alloc((4,), F32)
        pr = alloc((2,), F32)
        e12 = alloc((2,), F32)
        nlam = alloc((1,), F32)
        gsub = alloc((1,), F32)
        S.dma("sp", I_dma(L4.ap[0:64, :], diff_l), writes=[L4.b])
        S.dma("sp", I_dma(gsub.ap, diff_subln), writes=[gsub.b])
        S.op("dve", I_tt(pr.ap[0:64, :], L4.ap[0:64, 0:4:2], L4.ap[0:64, 1:4:2], ALU.mult), [L4.b], [pr.b])
        psl = PSB[0]
        mm(psl.ap[:, 0:2], ONE32.ap[0:64, :], pr.ap[0:64, :], True, True, [ONE32.b, pr.b], psl.b)
        S.op("act", I_act(e12.ap, psl.ap[:, 0:2], AF.Exp), [psl.b], [e12.b])
        S.op("dve", I_tt(nlam.ap, e12.ap[:, 1:2], e12.ap[:, 0:1], ALU.subtract), [e12.b], [nlam.b])
        S.op("dve", I_ts(nlam.ap, nlam.ap, -lam_init, ALU.add), [nlam.b], [nlam.b])
        S.op("dve", I_ts(gsub.ap, gsub.ap, 1.0 - lam_init, ALU.mult), [gsub.b], [gsub.b])
        O0, O1, D0, D1 = PSB[3], PSB[4], PSB[5], PSB[6]
        sb = [PSB[0], PSB[1], PSB[2], PSB[7]]
        sc = [0]

        def nexts():
            i = sc[0]
            sc[0] = (i + 1) % 4
            return sb[i]

        mi = 0
        for (s0, sl, seq) in segs:
            LK = SP if seq == 0 else SS
            NK = LK // 128
            for hd in range(8):
                Kh, Vh, Qh = Khs[hi_ % 2], Vhs[hi_ % 2], Qhs[hi_ % 2]
                hi_ += 1
                if seq == 0:
                    for r in range(4):
                        S.dma("sp", I_dma(Kh.ap[:, r * QP:(r + 1) * QP], kall_h[hd][r * 128:(r + 1) * 128, :]),
                              reads=[bgk], writes=[Kh.b])
                    vv = vall_h[hd].rearrange("(kt p) e -> p kt e", p=128)
                    for c8 in range(0, NK, 16):
                        S.dma("sp", I_dma(Vh.ap[:, c8:c8 + 16, :], vv[:, c8:c8 + 16, :]), reads=[bgv], writes=[Vh.b])
                else:
                    S.dma("sp", I_dma(Kh.ap[:, 0:LK], KT[hd, :, s0:s0 + sl]), writes=[Kh.b])
                    vv = Vt[s0:s0 + sl, :].rearrange("(kt p) (h e) -> p kt h e", p=128, e=128)
                    S.dma("sp", I_dma(Vh.ap[:, 0:NK, :], vv[:, :, hd, :]), writes=[Vh.b])
                S.dma("sp", I_dma(Qh.ap[:, 0:sl], QT[hd, :, s0:s0 + sl]), writes=[Qh.b])
                for qg in range(sl // 512):
                    qs = slice(qg * 512, (qg + 1) * 512)

                    def smm(kt):
                        pa = nexts()
                        pb = nexts()
                        mm(pa.ap, Kh.ap[0:64, kt * 128:(kt + 1) * 128], Qh.ap[0:64, qs], True, True, [Kh.b, Qh.b], pa.b)
                        mm(pb.ap, Kh.ap[64:128, kt * 128:(kt + 1) * 128], Qh.ap[64:128, qs], True, True, [Kh.b, Qh.b], pb.b)
                        return pa, pb

                    cur = smm(0)
                    for kt in range(NK):
                        nxt = smm(kt + 1) if kt + 1 < NK else None
                        e0 = Et[(2 * kt) % 4]
                        e1 = Et[(2 * kt + 1) % 4]
                        S.op("act", I_act(e0.ap, cur[0].ap, AF.Exp, scale=0.125), [cur[0].b], [e0.b])
                        S.op("act", I_act(e1.ap, cur[1].ap, AF.Exp, scale=0.125), [cur[1].b], [e1.b])
                        mm(O0.ap, Vh.ap[:, kt, :], e0.ap, kt == 0, kt == NK - 1, [Vh.b, e0.b], O0.b)
                        mm(D0.ap, ONES, e0.ap, kt == 0, kt == NK - 1, [C16.b, e0.b], D0.b)
                        mm(O1.ap, Vh.ap[:, kt, :], e1.ap, kt == 0, kt == NK - 1, [Vh.b, e1.b], O1.b)
                        mm(D1.ap, ONES, e1.ap, kt == 0, kt == NK - 1, [C16.b, e1.b], D1.b)
                        cur = nxt
                    S.op("dve", I_rec(r0.ap, D0.ap), [D0.b], [r0.b])
                    S.op("dve", I_rec(r1.ap, D1.ap), [D1.b], [r1.b])
                    S.op("dve", I_tt(a0.ap, O0.ap, r0.ap, ALU.mult), [O0.b, r0.b], [a0.b])
                    S.op("dve", I_tt(a1.ap, O1.ap, r1.ap, ALU.mult), [O1.b, r1.b], [a1.b])
                    S.op("dve", I_stt(a0.ap, a1.ap, nlam.ap[:, 0:1], a0.ap, ALU.mult, ALU.add), [a1.b, nlam.b, a0.b], [a0.b])
                    S.op("act", I_act(sq.ap, a0.ap, AF.Square), [a0.b], [sq.b])
                    pm = nexts()
                    mm(pm.ap, OM128, sq.ap, True, True, [sq.b, C16.b], pm.b)
                    S.op("dve", I_ts(t_.ap, pm.ap, EPS, ALU.add), [pm.b], [t_.b])
                    S.op("act", I_act(t_.ap, t_.ap, AF.Ln), [t_.b], [t_.b])
                    S.op("act", I_act(R_.ap, t_.ap, AF.Exp, scale=-0.5), [t_.b], [R_.b])
                    m = mo[mi % 2]
                    mi += 1
                    S.op("dve", I_stt(m.ap, a0.ap, gsub.ap[:, 0:1], R_.ap, ALU.mult, ALU.mult), [a0.b, gsub.b, R_.b], [m.b])
                    S.dma("sp", I_dma(MTd[hd, :, s0 + qg * 512:s0 + (qg + 1) * 512], m.ap), reads=[m.b])

    def stage_swa(l, Xs):
        stage_begin()
        wq = alloc((8, 1024), BF16)
        wk2 = alloc((8, 4, 128), BF16)
        wv2 = alloc((8, 4, 128), BF16)
        S.dma("pool", I_dma(wq.ap, swa_w_qkv[0, :, 0:1024].rearrange("(kc p) n -> p kc n", p=128)), writes=[wq.b])
        for dup in range(2):
            for hk in range(4):
                S.dma("pool", I_dma(wk2.ap[:, :, hk, dup * 64:(dup + 1) * 64],
                                    swa_w_qkv[0, :, 1024 + hk * 64:1024 + (hk + 1) * 64].rearrange("(kc p) d -> p kc d", p=128)), writes=[wk2.b])
                S.dma("pool", I_dma(wv2.ap[:, :, hk, dup * 64:(dup + 1) * 64],
                                    swa_w_qkv[0, :, 1280 + hk * 64:1280 + (hk + 1) * 64].rearrange("(kc p) d -> p kc d", p=128)), writes=[wv2.b])
        G_ = alloc((2,), F32)
        S.dma("sp", I_dma(G_.ap, swa_g), writes=[G_.b])
        xg = [alloc((8, 512), F32) for _ in range(2)]
        hT = [alloc((8, 512), BF16) for _ in range(2)]
        nbs = [norm_bufs(512) for _ in range(2)]
        qbs = [qk_bufs() for _ in range(2)]
        cs = [alloc((512,), F32) for _ in range(2)]
        sn = [alloc((512,), F32) for _ in range(2)]
        qo = [alloc((512,), BF16) for _ in range(4)]
        vo = [alloc((4, 512), BF16) for _ in range(2)]
        qi = 0
        for gi, (t0, s0, sl, seq) in enumerate(groups):
            x = xg[gi % 2]
            h = hT[gi % 2]
            c_ = cs[gi % 2]
            s_ = sn[gi % 2]
            v = vo[gi % 2]
            S.dma("sp", I_dma(x.ap, xt_view(Xs, t0, 512)), writes=[x.b])
            S.dma("sp", I_dma(c_.ap, ropeC[:, t0:t0 + 512]), writes=[c_.b])
            S.dma("sp", I_dma(s_.ap, ropeS[:, t0:t0 + 512]), writes=[s_.b])
            norm(x, 512, s1[l], modv[l][0], seq, h, 0, nbs[gi % 2])
            for oc in range(12):
                ps = nextps()
                for kc in range(8):
                    lw = wq.ap[:, kc, oc * 128:(oc + 1) * 128] if oc < 8 else wk2.ap[:, kc, oc - 8, :]
                    mm(ps.ap, lw, h.ap[:, kc, :], kc == 0, kc == 7, [wq.b, wk2.b, h.b], ps.b)
                o = qo[qi % 4]
                qi += 1
                qk_post(ps, 0 if oc < 8 else 1, G_, o, qbs[qi % 2], c_, s_)
                dst = QT[oc, :, t0:t0 + 512] if oc < 8 else KT[oc - 8, :, t0:t0 + 512]
                S.dma("sp", I_dma(dst, o.ap), reads=[o.b])
                if seq == 0 and oc >= 8 and t0 == 0:
                    S.dma("sp", I_dma(sbin[0:128, (oc - 8) * 128:(oc - 7) * 128], o.ap[:, 0:128]), reads=[o.b])
                if seq == 0 and oc >= 8 and t0 == QP - 512:
                    S.dma("sp", I_dma(sbin[128:256, (oc - 8) * 128:(oc - 7) * 128], o.ap[:, 384:512]), reads=[o.b])
            for tt in range(4):
                ps = nextps()
                for kc in range(8):
                    mm(ps.ap, h.ap[:, kc, tt * 128:(tt + 1) * 128], wv2.ap[:, kc].rearrange("p a b -> p (a b)"),
                       kc == 0, kc == 7, [wv2.b, h.b], ps.b)
                S.op("act", I_act(v.ap[:, tt, :], ps.ap, AF.Copy), [ps.b], [v.b])
            S.dma("sp", I_dma(Vt[t0:t0 + 512, 0:512].rearrange("(tt p) c -> p tt c", p=128), v.ap), reads=[v.b])
            if seq == 0 and t0 == 0:
                S.dma("sp", I_dma(sbin[0:128, 512:1024], v.ap[:, 0, :]), reads=[v.b])
            if seq == 0 and t0 == QP - 512:
                S.dma("sp", I_dma(sbin[128:256, 512:1024], v.ap[:, 3, :]), reads=[v.b])
        S.barrier()
        bg = Buf()
        S.op("pool", lambda e: e.collective_compute("AllGather", ALU.bypass, replica_groups=RG, ins=[sbin.opt()],
                                                    outs=[sball.opt()]), (), [bg])
        stage_begin()
        sk = alloc((16,), F32)
        se = alloc((16,), F32)
        S.dma("sp", I_dma(sk.ap, swa_sink), writes=[sk.b])
        S.op("act", I_act(se.ap, sk.ap, AF.Exp), [sk.b], [se.b])
        hal = alloc((4, 2, 1024), BF16)
        S.dma("sp", I_dma(hal.ap, sball.rearrange("(r e p) c -> p r e c", p=128, e=2)), reads=[bg], writes=[hal.b])
        hsel = alloc((2, 1024), BF16)
        for side, (e_, so) in enumerate(((1, 0), (0, 4))):
            S.op("dve", I_ts(hsel.ap[:, side, :], hal.ap[:, 0, e_, :], SEL.ap[:, so:so + 1], ALU.mult), [hal.b, SEL.b], [hsel.b])
            for r in range(1, 4):
                S.op("dve", I_stt(hsel.ap[:, side, :], hal.ap[:, r, e_, :], SEL.ap[:, so + r:so + r + 1], hsel.ap[:, side, :],
                                  ALU.mult, ALU.add), [hal.b, SEL.b, hsel.b], [hsel.b])
        Qg = [alloc((8, 512), BF16) for _ in range(2)]
        Kg = [alloc((4, 768), BF16) for _ in range(2)]
        Vg = [alloc((6, 512), BF16) for _ in range(2)]
        Mg = [alloc((8, 512), BF16) for _ in range(2)]
        Et = [alloc((256,), BF16) for _ in range(9)]
        rr = [alloc((256,), F32) for _ in range(3)]
        ODB = [PSB[0], PSB[1], PSB[2]]
        SH = [T(PSB[3 + i].ap[:, 0:256], PSB[3 + i].b) for i in range(5)]
        oi = 0
        si = 0
        ei = 0
        ri = 0
        for gi, (t0, s0, sl, seq) in enumerate(groups):
            q = Qg[gi % 2]
            k_ = Kg[gi % 2]
            v = Vg[gi % 2]
            m = Mg[gi % 2]
            S.dma("sp", I_dma(q.ap, xt_view(QT, t0, 512)), writes=[q.b])
            S.dma("sp", I_dma(k_.ap[:, :, 128:640], KT[0:4, :, t0:t0 + 512].rearrange("c p t -> p c t")), writes=[k_.b])
            S.dma("sp", I_dma(v.ap[:, 1:5, :], Vt[t0:t0 + 512, 0:512].rearrange("(tt p) c -> p tt c", p=128)), writes=[v.b])
            lin = t0 > s0
            rin = t0 + 512 < s0 + sl
            if lin:
                S.dma("sp", I_dma(k_.ap[:, :, 0:128], KT[0:4, :, t0 - 128:t0].rearrange("c p t -> p c t")), writes=[k_.b])
                S.dma("sp", I_dma(v.ap[:, 0, :], Vt[t0 - 128:t0, 0:512]), writes=[v.b])
            elif seq == 0:
                S.op("dve", I_cp(k_.ap[:, :, 0:128], hsel.ap[:, 0, 0:512].rearrange("p (a b) -> p a b", b=128)), [hsel.b], [k_.b])
                S.op("dve", I_cp(v.ap[:, 0, :], hsel.ap[:, 0, 512:1024]), [hsel.b], [v.b])
            if rin:
                S.dma("sp", I_dma(k_.ap[:, :, 640:768], KT[0:4, :, t0 + 512:t0 + 640].rearrange("c p t -> p c t")), writes=[k_.b])
                S.dma("sp", I_dma(v.ap[:, 5, :], Vt[t0 + 512:t0 + 640, 0:512]), writes=[v.b])
            elif seq == 0:
                S.op("dve", I_cp(k_.ap[:, :, 640:768], hsel.ap[:, 1, 0:512].rearrange("p (a b) -> p a b", b=128)), [hsel.b], [k_.b])
                S.op("dve", I_cp(v.ap[:, 5, :], hsel.ap[:, 1, 512:1024]), [hsel.b], [v.b])
            for nbk in range(4):
                offs = []
                for o in (-1, 0, 1):
                    if (o == -1 and nbk == 0 and not lin and seq != 0) or (o == 1 and nbk == 3 and not rin and seq != 0):
                        continue
                    edge = (o == -1 and nbk == 0 and not lin) or (o == 1 and nbk == 3 and not rin)
                    offs.append((o, edge))
                for hk in range(4):
                    for half in range(2):
                        pr_ = slice(half * 64, half * 64 + 64)
                        pOD = ODB[oi % 3]
                        oi += 1
                        es = []
                        for ii, (o, edge) in enumerate(offs):
                            pS = SH[si % 5]
                            si += 1
                            kc0 = 128 + (nbk + o) * 128
                            mm(pS.ap, k_.ap[pr_, hk, kc0:kc0 + 128], q.ap[pr_, 2 * hk:2 * hk + 2, nbk * 128:(nbk + 1) * 128],
                               True, True, [k_.b, q.b], pS.b)
                            e = Et[ei % 9]
                            ei += 1
                            S.op("act", I_act(e.ap, pS.ap, AF.Exp, scale=0.125), [pS.b], [e.b])
                            if o != 0:
                                mk = (MASKE if edge else MASK).ap[:, 0 if o == -1 else 1, :]
                                S.op("dve", I_tt(e.ap, e.ap, mk, ALU.mult), [e.b, MASK.b, MASKE.b], [e.b])
                            es.append((e, o))
                        for ii, (e, o) in enumerate(es):
                            mm(pOD.ap[:, 0:256], v.ap[:, 1 + nbk + o, hk * 128:(hk + 1) * 128], e.ap, ii == 0, ii == len(es) - 1,
                               [v.b, e.b], pOD.b)
                        for ii, (e, o) in enumerate(es):
                            mm(pOD.ap[:, 256:512], ONES, e.ap, ii == 0, ii == len(es) - 1, [C16.b, e.b], pOD.b)
                        r_ = rr[ri % 3]
                        ri += 1
                        for jj in range(2):
                            qh = hk * 4 + half + 2 * jj
                            S.op("dve", I_ts(r_.ap[:, jj * 128:(jj + 1) * 128], pOD.ap[:, 256 + jj * 128:256 + (jj + 1) * 128], se.ap[:, qh:qh + 1], ALU.add),
                                 [pOD.b, se.b], [r_.b])
                        S.op("dve", I_rec(r_.ap, r_.ap), [r_.b], [r_.b])
                        S.op("dve", I_tt(m.ap[pr_, 2 * hk:2 * hk + 2, nbk * 128:(nbk + 1) * 128],
                                         pOD.ap[pr_, 0:256].rearrange("p (a b) -> p a b", b=128),
                                         r_.ap[pr_, :].rearrange("p (a b) -> p a b", b=128), ALU.mult), [pOD.b, r_.b], [m.b])
            S.dma("sp", I_dma(xt_view(MTd, t0, 512), m.ap), reads=[m.b])

    stage_mod()
    stage_init()
    try:
        for l in range(NL):
            kind, j = l % 3, l // 3
            if kind == 0:
                stage_fourier(l, j, XTb)
                ck("mix")
                stage_tail(l, fnet_w[j], fnet_b[j], XTb, XTa)
            elif kind == 1:
                stage_swa(l, XTb)
                ck("mix")
                stage_tail(l, swa_w_o[0], None, XTb, XTa)
            else:
                stage_diff(l, XTb)
                ck("mix")
                stage_tail(l, diff_w_o[0], None, XTb, XTa)
            ck("tail")
            stage_bnd(XTa)
            ck("bnd")
            stage_ffn(l, XTa, XTb)
    except _Stop:
        pass
    stage_final(XTa if STOP in ("tail", "bnd") else XTb)
    S.barrier()
    S.emit()
    S.stack.close()
    return nc


def _bf(a):
    return np.ascontiguousarray(a).astype(ml_dtypes.bfloat16)


def make_tables(cfg):
    SP, SS, NSS = cfg["SP"], cfg["SS"], cfg["NSS"]
    QP = SP // 4
    N2 = SP // 128
    NTS = SS // 128
    NKG = SS // 512
    tb = {}
    a = np.arange(256)
    ang = 2 * np.pi * np.outer(a, a) / 256
    cg = np.cos(ang) / 16.0
    sg = np.sin(ang) / 16.0
    ch = np.zeros((128, 2, 512))
    for jj in range(2):
        ch[:, jj, 0:256] = cg[jj * 128:(jj + 1) * 128]
        ch[:, jj, 256:512] = sg[jj * 128:(jj + 1) * 128]
    tb["chCS"] = _bf(ch.reshape(128, 1024))
    n = np.arange(SS)
    angS = 2 * np.pi * (np.outer(n, n) % SS) / SS
    CS_ = np.cos(angS) / np.sqrt(SS)
    SS_ = -np.sin(angS) / np.sqrt(SS)
    t = np.zeros((NKG, 128, NTS, 2, 512))
    for kg in range(NKG):
        for nt in range(NTS):
            t[kg, :, nt, 0, :] = CS_[nt * 128:(nt + 1) * 128, kg * 512:(kg + 1) * 512]
            t[kg, :, nt, 1, :] = SS_[nt * 128:(nt + 1) * 128, kg * 512:(kg + 1) * 512]
    tb["tabS"] = _bf(t.reshape(NKG, 128, NTS * 2 * 512))
    n1 = np.arange(128)[:, None, None]
    n2 = np.arange(N2)[None, :, None]
    k2 = np.arange(N2)[None, None, :]
    angE = 2 * np.pi * (((n1 + 128 * n2) * k2) % SP) / SP
    Ec = np.cos(angE) / np.sqrt(SP)
    Es = np.sin(angE) / np.sqrt(SP)
    t1 = np.stack([Ec, -Es, -Ec], axis=2)
    tb["tabP1"] = _bf(t1.reshape(128, N2, 3 * N2))
    kk = np.arange(128)[:, None]
    qq = np.arange(128)[None, :]
    m0 = (qq <= kk).astype(np.float32)
    m1 = (kk <= qq).astype(np.float32)
    mk = np.stack([np.concatenate([m0, m0], 1), np.concatenate([m1, m1], 1)], 1)
    tb["swamask"] = _bf(mk.reshape(128, 512))
    tb["ident"] = np.eye(128, dtype=np.float32)
    c16 = np.zeros((128, 1152), np.float32)
    c16[:, 0:128] = 1.0 / 1024
    blk = np.zeros((128, 128), np.float32)
    blk[0:64, 0:64] = 1.0 / 64
    blk[64:128, 64:128] = 1.0 / 64
    c16[:, 128:256] = blk
    c16[:, 256:384] = 1.0 / 128
    c16[:, 384:512] = 1.0
    sw = np.zeros((128, 128), np.float32)
    for m_ in range(128):
        partner = m_ + 32 if (m_ % 64) < 32 else m_ - 32
        sw[partner, m_] = 1.0
    c16[:, 512:640] = sw
    c16[:, 640:1152] = 1.0
    tb["c16"] = _bf(c16)
    return tb


def core_tables(cfg, core):
    SP, SS, NSS = cfg["SP"], cfg["SS"], cfg["NSS"]
    QP = SP // 4
    q = core % 4
    tb = {}
    pos = np.concatenate([np.arange(q * QP, (q + 1) * QP)] + [np.arange(SS)] * NSS).astype(np.float32)
    inv = (1.0 / (np.float32(10000.0) ** (np.arange(0, 64, 2, dtype=np.float32) / np.float32(64)))).astype(np.float32)
    ang = (pos[None, :] * inv[:, None]).astype(np.float32)
    c = np.cos(ang).astype(np.float32)
    s = np.sin(ang).astype(np.float32)
    tb["ropeC"] = np.ascontiguousarray(np.tile(c, (4, 1)))
    tb["ropeS"] = np.ascontiguousarray(np.concatenate([-s, s, -s, s], 0))
    n1 = np.arange(128)[:, None]
    k1 = (32 * q + np.arange(32))[None, :]
    a2 = 2 * np.pi * ((n1 * k1) % 128) / 128
    tb["tabP2"] = _bf(np.concatenate([np.cos(a2), np.sin(a2)], 1))
    sel = np.zeros((128, 12), np.float32)
    if q > 0:
        sel[:, q - 1] = 1.0
        sel[:, 8] = 1.0
    if q < 3:
        sel[:, 4 + q + 1] = 1.0
        sel[:, 9] = 1.0
    tb["sel"] = sel
    return tb


def fm(v):
    return np.ascontiguousarray(np.asarray(v).reshape(8, 128).T)


def prep_inputs(cfg, inp):
    SP, SS, NSS = cfg["SP"], cfg["SS"], cfg["NSS"]
    QP = SP // 4
    shared = dict(make_tables(cfg))
    NLW = cfg.get("NLW", 4)
    for k in ("ada_w", "fnet_w", "fnet_b", "swa_w_qkv", "swa_w_o", "diff_w_qkv", "diff_w_o", "ffn_w_gate", "ffn_w_up", "ffn_w_down"):
        a = np.asarray(inp[k], dtype=np.float32)
        if k in ("ada_w", "ffn_w_gate", "ffn_w_up", "ffn_w_down"):
            a = a[:NLW]
        shared[k] = np.ascontiguousarray(a)
    ab = np.asarray(inp["ada_b"], np.float32)
    shared["ada_bT"] = np.ascontiguousarray(ab.reshape(4, 48, 128).transpose(2, 0, 1).reshape(128, 192))
    ng = np.zeros((128, 64), np.float32)
    for l in range(4):
        ng[:, l * 8:(l + 1) * 8] = fm(inp["norm1_g"][l])
        ng[:, 32 + l * 8:32 + (l + 1) * 8] = fm(inp["norm2_g"][l])
    shared["ng"] = ng
    shared["swa_g"] = np.ascontiguousarray(np.stack([np.tile(inp["swa_q_g"][0], 2), np.tile(inp["swa_k_g"][0], 2)], 1), dtype=np.float32)
    shared["swa_sink"] = np.ascontiguousarray(np.tile(np.asarray(inp["swa_sink"][0], np.float32)[None, :], (128, 1)))
    shared["diff_g"] = np.ascontiguousarray(np.stack([np.tile(inp["diff_q_g"][0], 2), np.tile(inp["diff_k_g"][0], 2)], 1), dtype=np.float32)
    shared["diff_l"] = np.ascontiguousarray(np.stack([inp["diff_lq1"][0], inp["diff_lk1"][0], inp["diff_lq2"][0], inp["diff_lk2"][0]], 1),
                                            dtype=np.float32)
    shared["diff_subln"] = np.ascontiguousarray(np.asarray(inp["diff_subln_g"][0], np.float32).reshape(128, 1))
    cv = np.zeros((128, 4 * NFC * 4), np.float32)
    for l in range(4):
        for j in range(3):
            cv[:, (l * NFC * 4 + j)::4][:, 0:NFC] = np.asarray(inp["ffn_conv_w"][l][j], np.float32).reshape(NFC, 128).T
        cv[:, (l * NFC * 4 + 3)::4][:, 0:NFC] = np.asarray(inp["ffn_conv_b"][l], np.float32).reshape(NFC, 128).T
    shared["ffn_convT"] = cv
    maps = []
    for core in range(8):
        b, q = core // 4, core % 4
        m = dict(shared)
        m.update(core_tables(cfg, core))
        xs = [inp["x_prompt"][b, q * QP:(q + 1) * QP]] + [inp["x_sample"][core * NSS + i] for i in range(NSS)]
        m["xin"] = np.ascontiguousarray(np.concatenate(xs, 0), dtype=np.float32)
        cs = np.stack([inp["c_prompt"][b]] + [inp["c_sample"][core * NSS + i] for i in range(NSS)], 0)
        m["cT"] = np.ascontiguousarray(cs.reshape(1 + NSS, 8, 128).transpose(2, 1, 0).reshape(128, 8 * (1 + NSS)), dtype=np.float32)
        maps.append(m)
    return maps


def assemble(cfg, results, B_, SB_):
    SP, SS, NSS = cfg["SP"], cfg["SS"], cfg["NSS"]
    QP = SP // 4
    yp = np.zeros((B_, SP, D), np.float32)
    ys = np.zeros((SB_, SS, D), np.float32)
    for core in range(8):
        y = results[core]["yout"]
        b, q = core // 4, core % 4
        yp[b, q * QP:(q + 1) * QP] = y[0:QP]
        for i in range(NSS):
            ys[core * NSS + i] = y[QP + i * SS:QP + (i + 1) * SS]
    return yp, ys


FULL = dict(SP=16384, SS=2048, NSS=4, NL=4)


def kernel(**inputs):
    cfg = FULL
    nc = build(cfg)
    maps = prep_inputs(cfg, inputs)
    res = run_bass_kernel_spmd(nc, maps, core_ids=list(range(8)))
    return assemble(cfg, res.results, 2, 32)
```

```python
import math
from contextlib import ExitStack
import numpy as np
import ml_dtypes
import concourse.bass as bass
import concourse.mybir as mybir
from concourse.bass_utils import run_bass_kernel_spmd

F32 = mybir.dt.float32
BF16 = mybir.dt.bfloat16
AF = mybir.ActivationFunctionType
ALU = mybir.AluOpType
D = 1024
DFF = 2816
NFC = 22
EPS = 1e-6
EPOCH = 30000
RING = 12
SLAB32 = 49152


class Buf:
    __slots__ = ("name", "w", "r")

    def __init__(self, name=""):
        self.name = name
        self.w = None
        self.r = {}


class T:
    __slots__ = ("ap", "b")

    def __init__(self, ap, b):
        self.ap = ap
        self.b = b


class Sched:
    ENG = ("pe", "act", "dve", "pool", "sp")

    def __init__(self, nc):
        self.nc = nc
        self.ops = {e: [] for e in self.ENG}
        self.cnt = {e: 0 for e in self.ENG}
        self.seen = {e: {} for e in self.ENG}
        self.ring_pos = {}
        self.ring_tok = {}
        self.keys = set()
        self.stack = ExitStack()

    def _waits(self, eng, reads, writes, extra=()):
        seen = self.seen[eng]
        waits = {}

        def need(tok):
            if tok is None:
                return
            k, v = tok
            if eng == "pe" and k[0] == "pe":
                return
            if seen.get(k, 0) >= v:
                return
            if waits.get(k, 0) < v:
                waits[k] = v

        for b in reads:
            need(b.w)
        for b in writes:
            need(b.w)
            for tok in b.r.values():
                need(tok)
        for tok in extra:
            need(tok)
        for k, v in waits.items():
            seen[k] = v
        return list(waits.items())

    def op(self, eng, fn, reads=(), writes=()):
        waits = self._waits(eng, reads, writes)
        cnt = self.cnt[eng]
        self.cnt[eng] = cnt + 1
        epoch, idx = divmod(cnt, EPOCH)
        key = (eng, epoch)
        self.keys.add(key)
        tok = (key, idx + 1)
        self.ops[eng].append((fn, waits, key, 1))
        for b in reads:
            b.r[eng] = tok
        for b in writes:
            b.w = tok
            b.r = {}
        return tok

    def dma(self, q, fn, reads=(), writes=()):
        pos = self.ring_pos.get(q, 0)
        self.ring_pos[q] = (pos + 1) % RING
        key = ("dma", q, pos)
        self.keys.add(key)
        prev = self.ring_tok.get(key)
        waits = self._waits(q, reads, writes, extra=(prev,) if prev else ())
        tok = (key, (prev[1] if prev else 0) + 16)
        self.ring_tok[key] = tok
        self.ops[q].append((fn, waits, key, 16))
        for b in reads:
            b.r[key] = tok
        for b in writes:
            b.w = tok
            b.r = {}
        return tok

    def last_tokens(self):
        toks = list(self.ring_tok.values())
        for e in self.ENG:
            c = self.cnt[e]
            if c:
                epoch, idx = divmod(c - 1, EPOCH)
                toks.append(((e, epoch), idx + 1))
        return toks

    def barrier(self):
        toks = self.last_tokens()
        for e in self.ENG:
            waits = self._waits(e, (), (), extra=toks)
            if waits:
                self.ops[e].append((None, waits, None, 0))

    def emit(self):
        nc = self.nc
        sems = {}
        for i, k in enumerate(sorted(self.keys, key=str)):
            sems[k] = self.stack.enter_context(nc.semaphore(f"s{i}"))

        def replay(e, name):
            for fn, waits, key, inc in self.ops[name]:
                for k, v in waits:
                    e.wait_ge(sems[k], v)
                if fn is not None:
                    fn(e).then_inc(sems[key], inc)

        with nc.Block() as block:
            @block.sync
            def _(e):
                replay(e, "sp")

            @block.tensor
            def _(e):
                replay(e, "pe")

            @block.scalar
            def _(e):
                replay(e, "act")

            @block.vector
            def _(e):
                replay(e, "dve")

            @block.gpsimd
            def _(e):
                replay(e, "pool")


def I_mm(out, lhsT, rhs, start, stop):
    return lambda e: e.matmul(out, lhsT=lhsT, rhs=rhs, start=start, stop=stop)


def I_tr(out, in_, ident):
    return lambda e: e.transpose(out, in_, ident)


def I_act(out, in_, func, scale=None, bias=None):
    kw = {}
    if scale is not None:
        kw["scale"] = scale
    if bias is not None:
        kw["bias"] = bias
    return lambda e: e.activation(out=out, in_=in_, func=func, **kw)


def I_ts(out, in0, s1, op0, s2=None, op1=None):
    if op1 is None:
        return lambda e: e.tensor_scalar(out=out, in0=in0, scalar1=s1, scalar2=None, op0=op0)
    return lambda e: e.tensor_scalar(out=out, in0=in0, scalar1=s1, scalar2=s2, op0=op0, op1=op1)


def I_tt(out, in0, in1, op):
    return lambda e: e.tensor_tensor(out=out, in0=in0, in1=in1, op=op)


def I_stt(out, in0, scalar, in1, op0, op1):
    return lambda e: e.scalar_tensor_tensor(out=out, in0=in0, scalar=scalar, in1=in1, op0=op0, op1=op1)


def I_cp(out, in_):
    return lambda e: e.tensor_copy(out=out, in_=in_)


def I_rec(out, in_):
    return lambda e: e.reciprocal(out=out, in_=in_)


def I_ms(ap, v):
    return lambda e: e.memset(ap, v)


def I_dma(out, in_, slow=False):
    if slow:
        return lambda e: e.dma_start(out=out, in_=in_, allow_slow_non_contiguous=True)
    return lambda e: e.dma_start(out=out, in_=in_)


class _Stop(Exception):
    pass


def build(cfg):
    SP, SS, NSS, NL = cfg["SP"], cfg["SS"], cfg["NSS"], cfg["NL"]
    NLW = cfg.get("NLW", 4)
    STOP = cfg.get("STOP", None)
    QP = SP // 4
    N2 = SP // 128
    NT = QP + NSS * SS
    NS = 1 + NSS
    NTS = SS // 128
    NKG = SS // 512
    segs = [(0, QP, 0)] + [(QP + i * SS, SS, 1 + i) for i in range(NSS)]
    groups = [(s0 + g * 512, s0, sl, sq) for (s0, sl, sq) in segs for g in range(sl // 512)]
    nc = bass.Bass("TRN2", target_bir_lowering=False)

    def din(name, shape, dt=F32):
        return nc.dram_tensor(name, list(shape), dt, kind="ExternalInput").ap()

    def dsc(name, shape, dt):
        return nc.dram_tensor(name, list(shape), dt).ap()

    xin = din("xin", [NT, D])
    cT_in = din("cT", [128, 8 * NS])
    ada_w = din("ada_w", [NLW, D, 6 * D])
    ada_bT = din("ada_bT", [128, 4 * 48])
    ng_in = din("ng", [128, 64])
    fnet_w = din("fnet_w", [2, D, D])
    fnet_b = din("fnet_b", [2, D])
    swa_w_qkv = din("swa_w_qkv", [1, D, 1536])
    swa_g = din("swa_g", [128, 2])
    swa_sink = din("swa_sink", [128, 16])
    swa_w_o = din("swa_w_o", [1, D, D])
    diff_w_qkv = din("diff_w_qkv", [1, D, 3072])
    diff_g = din("diff_g", [128, 2])
    diff_l = din("diff_l", [64, 4])
    diff_subln = din("diff_subln", [128, 1])
    diff_w_o = din("diff_w_o", [1, D, D])
    ffn_w_gate = din("ffn_w_gate", [NLW, D, DFF])
    ffn_w_up = din("ffn_w_up", [NLW, D, DFF])
    ffn_convT = din("ffn_convT", [128, 4 * NFC * 4])
    ffn_w_down = din("ffn_w_down", [NLW, DFF, D])
    ropeC = din("ropeC", [128, NT])
    ropeS = din("ropeS", [128, NT])
    chCS = din("chCS", [128, 2 * 512], BF16)
    tabS = din("tabS", [NKG, 128, NTS * 2 * 512], BF16)
    tabP1 = din("tabP1", [128, N2, 3 * N2], BF16)
    tabP2 = din("tabP2", [128, 64], BF16)
    swamask = din("swamask", [128, 2 * 256], BF16)
    sel_in = din("sel", [128, 12])
    ident_in = din("ident", [128, 128])
    c16_in = din("c16", [128, 1152], BF16)
    yout = nc.dram_tensor("yout", [NT, D], F32, kind="ExternalOutput").ap()

    XTa = dsc("XTa", [8, 128, NT], F32)
    XTb = dsc("XTb", [8, 128, NT], F32)
    MTd = dsc("MTd", [8, 128, NT], BF16)
    gu16 = dsc("gu16", [max(NL, 1), NFC, 128, 2048], BF16)
    wd16 = dsc("wd16", [max(NL, 1), DFF, D], BF16)
    MQ = QP // 128
    zin_c = [dsc(f"zin{i}", [8 * MQ, 2048], BF16) for i in range(16)]
    zall_c = [dsc(f"zall{i}", [4 * 8 * MQ, 2048], BF16) for i in range(16)]
    zs = dsc("zs", [max(NSS * SS, 128), 2048], BF16)
    vbuf = dsc("vbuf", [128, N2, 2048], BF16)
    QT = dsc("QT", [8, 128, NT], BF16)
    KT = dsc("KT", [8, 128, NT], BF16)
    kin_h = [dsc(f"kin{i}", [128, QP], BF16) for i in range(8)]
    kall_h = [dsc(f"kall{i}", [4 * 128, QP], BF16) for i in range(8)]
    Vt = dsc("Vt", [NT, 1024], BF16)
    vin_h = [dsc(f"vin{i}", [QP, 128], BF16) for i in range(8)]
    vall_h = [dsc(f"vall{i}", [SP, 128], BF16) for i in range(8)]
    sbin = dsc("sbin", [2 * 128, 1024], BF16)
    sball = dsc("sball", [4 * 2 * 128, 1024], BF16)
    bnd_in = dsc("bnd_in", [128, 16], F32)
    bnd_all = dsc("bnd_all", [4 * 128, 16], F32)
    RG = [[0, 1, 2, 3], [4, 5, 6, 7]]

    S = Sched(nc)
    slab = S.stack.enter_context(nc.sbuf_tensor("slab", [128, SLAB32], F32))
    PSB = [T(S.stack.enter_context(nc.psum_tensor(f"ps{i}", [128, 512], F32))[:], Buf()) for i in range(8)]
    psq = [T(PSB[7].ap[:, i * 128:(i + 1) * 128], Buf()) for i in range(4)]
    st = {"top": 0, "ps": 0, "psq": 0}

    def alloc(free, dt, name=""):
        n = int(np.prod(free))
        n32 = n if dt == F32 else (n + 1) // 2
        o = st["top"]
        st["top"] = o + (n32 + 15) // 16 * 16
        assert st["top"] <= SLAB32, ("SBUF arena overflow", name, st["top"])
        ap = slab[:, o:o + n32]
        if dt == BF16:
            ap = ap.bitcast(BF16)[:, 0:n]
        if len(free) == 2:
            ap = ap.rearrange("p (a b) -> p a b", b=free[1])
        elif len(free) == 3:
            ap = ap.rearrange("p (a b c) -> p a b c", b=free[1], c=free[2])
        return T(ap, Buf(name))

    def nextps():
        i = st["ps"]
        st["ps"] = (i + 1) % 7
        return PSB[i]

    def nextpsq():
        i = st["psq"]
        st["psq"] = (i + 1) % 4
        return psq[i]

    def mm(out, lhsT, rhs, start, stop, reads, wr):
        S.op("pe", I_mm(out, lhsT, rhs, start, stop), reads, [wr])

    C16 = alloc((1152,), BF16)
    IDN = alloc((128,), F32)
    ONE32 = alloc((128,), F32)
    MH = alloc((512,), F32)
    SEL = alloc((12,), F32)
    MASK = alloc((2, 256), BF16)
    MASKE = alloc((2, 256), BF16)
    CONV = alloc((4 * NFC * 4,), F32)
    NG = alloc((64,), F32)
    ADAB = alloc((4 * 48,), F32)
    CHCS = alloc((2, 512), BF16)
    modv = [[alloc((8, NS), F32) for j in range(6)] for l in range(NL)]
    s1 = [alloc((8, NS), F32) for l in range(NL)]
    s2 = [alloc((8, NS), F32) for l in range(NL)]
    BSEL = alloc((8, 2), F32)
    persist_top = st["top"]
    OM1024 = C16.ap[:, 0:128]
    BLK64 = C16.ap[:, 128:256]
    OM128 = C16.ap[:, 256:384]
    ONES = C16.ap[:, 384:512]
    SWAP = C16.ap[:, 512:640]
    ONEROW = C16.ap[0:1, 640:1152]

    for t, src in ((C16, c16_in), (IDN, ident_in), (SEL, sel_in), (CONV, ffn_convT), (NG, ng_in),
                   (ADAB, ada_bT), (CHCS, chCS.rearrange("p (a b) -> p a b", b=512)),
                   (MASK, swamask.rearrange("p (a b) -> p a b", b=256))):
        S.dma("sp", I_dma(t.ap, src), writes=[t.b])
    S.op("pool", I_ms(MH.ap, -0.5), writes=[MH.b])
    S.op("pool", I_ms(ONE32.ap, 1.0), writes=[ONE32.b])
    S.op("dve", I_ts(MASKE.ap[:, 0, :], MASK.ap[:, 0, :], SEL.ap[:, 8:9], ALU.mult), [MASK.b, SEL.b], [MASKE.b])
    S.op("dve", I_ts(MASKE.ap[:, 1, :], MASK.ap[:, 1, :], SEL.ap[:, 9:10], ALU.mult), [MASK.b, SEL.b], [MASKE.b])

    def stage_begin():
        S.barrier()
        st["top"] = persist_top

    def ck(name):
        if STOP == name:
            raise _Stop()

    def xt_view(X, t0, w):
        return X[:, :, t0:t0 + w].rearrange("c p t -> p c t")

    for l in range(NL):
        for (wsrc, j) in ((ffn_w_gate, 0), (ffn_w_up, 1)):
            src = wsrc[l].rearrange("(kc p) (fc j) -> fc p kc j", p=128, j=128)
            dst = gu16[l].rearrange("fc p (two kc j) -> fc p two kc j", two=2, j=128)
            for fc in range(NFC):
                S.dma("pool", I_dma(dst[fc, :, j], src[fc]))
        for h in range(2):
            S.dma("pool", I_dma(wd16[l, h * 1408:(h + 1) * 1408, :], ffn_w_down[l, h * 1408:(h + 1) * 1408, :]))

    def stage_mod():
        stage_begin()
        cTt = alloc((8, NS), F32)
        cact = alloc((8, NS), BF16)
        S.dma("sp", I_dma(cTt.ap, cT_in.rearrange("p (a b) -> p a b", b=NS)), writes=[cTt.b])
        S.op("act", I_act(cact.ap, cTt.ap, AF.Silu), [cTt.b], [cact.b])
        wbuf = [alloc((8, 1024), BF16) for _ in range(2)]
        i = 0
        for l in range(NL):
            for j in range(6):
                w = wbuf[i % 2]
                i += 1
                S.dma("pool", I_dma(w.ap, ada_w[l, :, j * 1024:(j + 1) * 1024].rearrange("(kc p) n -> p kc n", p=128)),
                      writes=[w.b])
                for dc in range(8):
                    ps = nextps()
                    for kc in range(8):
                        mm(ps.ap[:, 0:NS], w.ap[:, kc, dc * 128:(dc + 1) * 128], cact.ap[:, kc, :], kc == 0, kc == 7,
                           [w.b, cact.b], ps.b)
                    c0 = l * 48 + j * 8 + dc
                    S.op("act", I_act(modv[l][j].ap[:, dc, :], ps.ap[:, 0:NS], AF.Identity, bias=ADAB.ap[:, c0:c0 + 1]),
                         [ps.b, ADAB.b], [modv[l][j].b])
            for kc in range(8):
                S.op("dve", I_ts(s1[l].ap[:, kc, :], modv[l][1].ap[:, kc, :], 1.0, ALU.add,
                                 NG.ap[:, l * 8 + kc:l * 8 + kc + 1], ALU.mult), [modv[l][1].b, NG.b], [s1[l].b])
                S.op("dve", I_ts(s2[l].ap[:, kc, :], modv[l][4].ap[:, kc, :], 1.0, ALU.add,
                                 NG.ap[:, 32 + l * 8 + kc:32 + l * 8 + kc + 1], ALU.mult), [modv[l][4].b, NG.b], [s2[l].b])

    def stage_init():
        stage_begin()
        xt = [alloc((1024,), F32) for _ in range(2)]
        xo = [alloc((8, 128), F32) for _ in range(2)]
        for tt in range(NT // 128):
            x = xt[tt % 2]
            o = xo[tt % 2]
            S.dma("sp", I_dma(x.ap, xin[tt * 128:(tt + 1) * 128, :]), writes=[x.b])
            for hh in range(2):
                ps = nextps()
                for k in range(4):
                    kc = hh * 4 + k
                    S.op("pe", I_tr(ps.ap[:, k * 128:(k + 1) * 128], x.ap[:, kc * 128:(kc + 1) * 128], IDN.ap),
                         [x.b, IDN.b], [ps.b])
                S.op("act" if hh else "dve",
                     (I_act(o.ap[:, 4:8, :], ps.ap.rearrange("p (a b) -> p a b", b=128), AF.Copy) if hh else
                      I_cp(o.ap[:, 0:4, :], ps.ap.rearrange("p (a b) -> p a b", b=128))), [ps.b], [o.b])
            S.dma("sp", I_dma(xt_view(XTb, tt * 128, 128), o.ap), reads=[o.b])

    def stage_final(X):
        stage_begin()
        xi = [alloc((8, 128), F32) for _ in range(2)]
        yo = [alloc((1024,), F32) for _ in range(2)]
        for tt in range(NT // 128):
            x = xi[tt % 2]
            o = yo[tt % 2]
            S.dma("sp", I_dma(x.ap, xt_view(X, tt * 128, 128)), writes=[x.b])
            for hh in range(2):
                ps = nextps()
                for k in range(4):
                    S.op("pe", I_tr(ps.ap[:, k * 128:(k + 1) * 128], x.ap[:, hh * 4 + k, :], IDN.ap), [x.b, IDN.b], [ps.b])
                S.op("act" if hh else "dve",
                     (I_act(o.ap[:, 512:1024], ps.ap, AF.Copy) if hh else I_cp(o.ap[:, 0:512], ps.ap)), [ps.b], [o.b])
            S.dma("sp", I_dma(yout[tt * 128:(tt + 1) * 128, :], o.ap), reads=[o.b])

    def norm_bufs(W):
        return dict(sq=alloc((8, W), BF16), t=alloc((W,), F32), R=alloc((W,), F32),
                    tmp=[alloc((W,), F32) for _ in range(2)])

    def norm(x, W, sT, shT, seq, hT, c0, nb, xap=None):
        xap = x.ap[:, :, 0:W] if xap is None else xap
        S.op("act", I_act(nb["sq"].ap[:, :, 0:W], xap, AF.Square), [x.b], [nb["sq"].b])
        ps = nextps()
        for kc in range(8):
            mm(ps.ap[:, 0:W], OM1024, nb["sq"].ap[:, kc, 0:W], kc == 0, kc == 7, [nb["sq"].b, C16.b], ps.b)
        S.op("dve", I_ts(nb["t"].ap[:, 0:W], ps.ap[:, 0:W], EPS, ALU.add), [ps.b], [nb["t"].b])
        S.op("act", I_act(nb["t"].ap[:, 0:W], nb["t"].ap[:, 0:W], AF.Ln), [nb["t"].b], [nb["t"].b])
        S.op("act", I_act(nb["R"].ap[:, 0:W], nb["t"].ap[:, 0:W], AF.Exp, scale=-0.5), [nb["t"].b], [nb["R"].b])
        for kc in range(8):
            tmp = nb["tmp"][kc % 2]
            S.op("dve", I_tt(tmp.ap[:, 0:W], xap[:, kc, :], nb["R"].ap[:, 0:W], ALU.mult), [x.b, nb["R"].b], [tmp.b])
            S.op("act", I_act(hT.ap[:, kc, c0:c0 + W], tmp.ap[:, 0:W], AF.Identity, scale=sT.ap[:, kc, seq:seq + 1],
                              bias=shT.ap[:, kc, seq:seq + 1]), [tmp.b, sT.b, shT.b], [hT.b])

    def stage_tail(l, w_in, b_in, Xs, Xd):
        stage_begin()
        wo = alloc((8, 1024), BF16)
        S.dma("pool", I_dma(wo.ap, w_in.rearrange("(kc p) n -> p kc n", p=128)), writes=[wo.b])
        if b_in is not None:
            brow = alloc((1024,), BF16)
            S.dma("pool", I_dma(brow.ap[0:1, :], b_in.rearrange("(o n) -> o n", o=1)), writes=[brow.b])
        xg = [alloc((8, 512), F32) for _ in range(2)]
        mt = [alloc((8, 512), BF16) for _ in range(2)]
        for gi, (t0, s0, sl, seq) in enumerate(groups):
            x = xg[gi % 2]
            m = mt[gi % 2]
            S.dma("sp", I_dma(x.ap, xt_view(Xs, t0, 512)), writes=[x.b])
            S.dma("sp", I_dma(m.ap, xt_view(MTd, t0, 512)), writes=[m.b])
            for dc in range(8):
                ps = nextps()
                for kc in range(8):
                    mm(ps.ap, wo.ap[:, kc, dc * 128:(dc + 1) * 128], m.ap[:, kc, :], kc == 0, kc == 7 and b_in is None,
                       [wo.b, m.b], ps.b)
                if b_in is not None:
                    mm(ps.ap, brow.ap[0:1, dc * 128:(dc + 1) * 128], ONEROW, False, True, [brow.b, C16.b], ps.b)
                S.op("dve", I_stt(x.ap[:, dc, :], ps.ap, modv[l][2].ap[:, dc, seq:seq + 1], x.ap[:, dc, :], ALU.mult, ALU.add),
                     [ps.b, x.b, modv[l][2].b], [x.b])
            S.dma("sp", I_dma(xt_view(Xd, t0, 512), x.ap), reads=[x.b])

    def stage_bnd(X):
        stage_begin()
        bv = bnd_in.rearrange("p (c two) -> p c two", two=2)
        S.dma("sp", I_dma(bv[:, :, 0:1], xt_view(X, 0, 1), True))
        S.dma("sp", I_dma(bv[:, :, 1:2], xt_view(X, QP - 1, 1), True))
        S.barrier()
        bg = Buf()
        S.op("pool", lambda e: e.collective_compute("AllGather", ALU.bypass, replica_groups=RG, ins=[bnd_in.opt()],
                                                    outs=[bnd_all.opt()]), (), [bg])
        ba = alloc((4, 8, 2), F32)
        S.dma("pool", I_dma(ba.ap, bnd_all.rearrange("(r p) (c two) -> p r c two", p=128, two=2)), reads=[bg], writes=[ba.b])
        for side, (col, so) in enumerate(((1, 0), (0, 4))):
            S.op("dve", I_ts(BSEL.ap[:, :, side:side + 1], ba.ap[:, 0, :, col:col + 1], SEL.ap[:, so:so + 1], ALU.mult),
                 [ba.b, SEL.b], [BSEL.b])
            for r in range(1, 4):
                S.op("dve", I_stt(BSEL.ap[:, :, side:side + 1], ba.ap[:, r, :, col:col + 1], SEL.ap[:, so + r:so + r + 1],
                                  BSEL.ap[:, :, side:side + 1], ALU.mult, ALU.add), [ba.b, SEL.b, BSEL.b], [BSEL.b])

    def stage_ffn(l, Xs, Xd):
        stage_begin()
        xg = [alloc((8, 514), F32) for _ in range(2)]
        hT = [alloc((8, 514), BF16) for _ in range(2)]
        hh = alloc((8, 2), BF16)
        nb = norm_bufs(512)
        nbh = norm_bufs(2)
        gx = [alloc((514,), F32) for _ in range(2)]
        acc = [alloc((512,), F32) for _ in range(2)]
        sg = [alloc((512,), F32) for _ in range(2)]
        actT = alloc((NFC, 512), BF16)
        gu = [alloc((2, 8, 128), BF16) for _ in range(4)]
        wdh = [alloc((11, 1024), BF16) for _ in range(2)]
        k = 0
        for gi, (t0, s0, sl, seq) in enumerate(groups):
            x = xg[gi % 2]
            h = hT[gi % 2]
            lin = t0 > s0
            rin = t0 + 512 < s0 + sl
            c_lo = 0 if lin else 1
            c_hi = 514 if rin else 513
            S.dma("sp", I_dma(x.ap[:, :, c_lo:c_hi], xt_view(Xs, t0 - 1 + c_lo, c_hi - c_lo)), writes=[x.b])
            flags = []
            for side, col, inside in ((0, 0, lin), (1, 513, rin)):
                if inside:
                    flags.append(1.0)
                elif seq == 0:
                    S.op("dve", I_cp(x.ap[:, :, col:col + 1], BSEL.ap[:, :, side:side + 1]), [BSEL.b], [x.b])
                    flags.append(SEL.ap[:, 8 + side:9 + side])
                else:
                    S.op("dve", I_ms(x.ap[:, :, col:col + 1], 0.0), (), [x.b])
                    flags.append(0.0)
            norm(x, 512, s2[l], modv[l][3], seq, h, 1, nb, xap=x.ap[:, :, 1:513])
            norm(x, 2, s2[l], modv[l][3], seq, hh, 0, nbh, xap=x.ap[:, :, 0:514:513])
            for side, col in ((0, 0), (1, 513)):
                S.op("dve", I_ts(h.ap[:, :, col:col + 1], hh.ap[:, :, side:side + 1], flags[side], ALU.mult),
                     [hh.b, SEL.b], [h.b])
            for hf in range(2):
                S.dma("sp", I_dma(wdh[hf].ap, wd16[l, hf * 1408:(hf + 1) * 1408, :].rearrange("(fc p) d -> p fc d", p=128)),
                      writes=[wdh[hf].b])
            for fc in range(NFC):
                w = gu[k % 4]
                k += 1
                S.dma("sp", I_dma(w.ap, gu16[l, fc].rearrange("p (two kc j) -> p two kc j", two=2, j=128)), writes=[w.b])
                psg = nextps()
                for kc in range(8):
                    mm(psg.ap, w.ap[:, 0, kc, :], h.ap[:, kc, 1:513], kc == 0, kc == 7, [w.b, h.b], psg.b)
                psh = nextps()
                for kc in range(8):
                    mm(psh.ap[:, 0:2], w.ap[:, 0, kc, :], h.ap[:, kc, 0:514:513], kc == 0, kc == 7, [w.b, h.b], psh.b)
                psu = nextps()
                for kc in range(8):
                    mm(psu.ap, w.ap[:, 1, kc, :], h.ap[:, kc, 1:513], kc == 0, kc == 7, [w.b, h.b], psu.b)
                g = gx[fc % 2]
                a = acc[fc % 2]
                sgt = sg[fc % 2]
                S.op("act", I_act(g.ap[:, 1:513], psg.ap, AF.Copy), [psg.b], [g.b])
                S.op("act", I_act(g.ap[:, 0:514:513], psh.ap[:, 0:2], AF.Copy), [psh.b], [g.b])
                cb = (l * NFC + fc) * 4
                cw = CONV.ap
                S.op("dve", I_ts(a.ap, g.ap[:, 1:513], cw[:, cb + 1:cb + 2], ALU.mult, cw[:, cb + 3:cb + 4], ALU.add),
                     [g.b, CONV.b], [a.b])
                S.op("dve", I_stt(a.ap, g.ap[:, 0:512], cw[:, cb:cb + 1], a.ap, ALU.mult, ALU.add), [g.b, CONV.b, a.b], [a.b])
                S.op("dve", I_stt(a.ap, g.ap[:, 2:514], cw[:, cb + 2:cb + 3], a.ap, ALU.mult, ALU.add), [g.b, CONV.b, a.b], [a.b])
                S.op("act", I_act(sgt.ap, a.ap, AF.Silu), [a.b], [sgt.b])
                S.op("dve", I_tt(actT.ap[:, fc, :], sgt.ap, psu.ap, ALU.mult), [sgt.b, psu.b], [actT.b])
            for dc in range(8):
                ps = nextps()
                for fc in range(NFC):
                    mm(ps.ap, wdh[fc // 11].ap[:, fc % 11, dc * 128:(dc + 1) * 128], actT.ap[:, fc, :], fc == 0, fc == NFC - 1,
                       [wdh[fc // 11].b, actT.b], ps.b)
                S.op("dve", I_stt(x.ap[:, dc, 1:513], ps.ap, modv[l][5].ap[:, dc, seq:seq + 1], x.ap[:, dc, 1:513], ALU.mult, ALU.add),
                     [ps.b, x.b, modv[l][5].b], [x.b])
            S.dma("sp", I_dma(xt_view(Xd, t0, 512), x.ap[:, :, 1:513]), reads=[x.b])

    def stage_fourier(l, j, Xs):
        stage_begin()
        xg = [alloc((8, 512), F32) for _ in range(2)]
        hT = [alloc((8, 512), BF16) for _ in range(2)]
        nbs = [norm_bufs(512) for _ in range(2)]
        zt = [alloc((4, 2048), BF16) for _ in range(2)]
        for gi, (t0, s0, sl, seq) in enumerate(groups):
            x = xg[gi % 2]
            h = hT[gi % 2]
            z = zt[gi % 2]
            S.dma("sp", I_dma(x.ap, xt_view(Xs, t0, 512)), writes=[x.b])
            norm(x, 512, s1[l], modv[l][0], seq, h, 0, nbs[gi % 2])
            for tt in range(4):
                for grp in range(4):
                    ps = nextps()
                    for jj in range(2):
                        mm(ps.ap, h.ap[:, 2 * grp + jj, tt * 128:(tt + 1) * 128], CHCS.ap[:, jj, :], jj == 0, jj == 1,
                           [h.b, CHCS.b], ps.b)
                    dst = z.ap[:, tt, :].rearrange("p (two g c) -> p two g c", two=2, c=256)[:, :, grp, :]
                    src = ps.ap.rearrange("p (two c) -> p two c", two=2)
                    if (tt * 4 + grp) % 2:
                        S.op("act", I_act(dst, src, AF.Copy), [ps.b], [z.b])
                    else:
                        S.op("dve", I_cp(dst, src), [ps.b], [z.b])
            if seq == 0:
                m0 = t0 // 128
                for i in range(16):
                    S.dma("sp", I_dma(zin_c[i].rearrange("(a m) c -> a m c", a=8)[:, m0:m0 + 4, :], z.ap[8 * i:8 * i + 8, :, :]),
                          reads=[z.b])
            else:
                drows = zs[t0 - QP:t0 - QP + 512, :]
                S.dma("sp", I_dma(drows.rearrange("(tt p) c -> p tt c", p=128), z.ap), reads=[z.b])
        ck("A1")
        S.barrier()
        bg = Buf()
        for i in range(16):
            S.op("pool", (lambda a, b: lambda e: e.collective_compute("AllGather", ALU.bypass, replica_groups=RG, ins=[a.opt()],
                                                                      outs=[b.opt()]))(zin_c[i], zall_c[i]), (), [bg])
        ck("AG")
        stage_begin()
        zsb = alloc((NTS, 2048), BF16)
        tb = [alloc((NTS, 2, 512), BF16) for _ in range(2)]
        mo = [alloc((8, 512), BF16) for _ in range(2)]
        ti = 0
        for si in range(NSS):
            S.dma("sp", I_dma(zsb.ap, zs[si * SS:(si + 1) * SS, :].rearrange("(nt p) c -> p nt c", p=128)), writes=[zsb.b])
            for kg in range(NKG):
                tbt = tb[ti % 2]
                m = mo[ti % 2]
                ti += 1
                S.dma("sp", I_dma(tbt.ap, tabS[kg].rearrange("p (nt two k) -> p nt two k", two=2, k=512)), writes=[tbt.b])
                for cc in range(8):
                    ps = nextps()
                    for nt in range(NTS):
                        mm(ps.ap, zsb.ap[:, nt, cc * 128:(cc + 1) * 128], tbt.ap[:, nt, 0, :], nt == 0, False, [zsb.b, tbt.b], ps.b)
                        mm(ps.ap, zsb.ap[:, nt, 1024 + cc * 128:1024 + (cc + 1) * 128], tbt.ap[:, nt, 1, :], False, nt == NTS - 1,
                           [zsb.b, tbt.b], ps.b)
                    S.op("act", I_act(m.ap[:, cc, :], ps.ap, AF.Copy), [ps.b], [m.b])
                t0 = QP + si * SS + kg * 512
                S.dma("sp", I_dma(xt_view(MTd, t0, 512), m.ap), reads=[m.b])
        ck("A2s")
        stage_begin()
        zt1 = [alloc((2048,), BF16) for _ in range(3)]
        tb1 = [alloc((3, N2), BF16) for _ in range(3)]
        vt = [alloc((2048,), BF16) for _ in range(3)]
        for n1 in range(128):
            z = zt1[n1 % 3]
            tbl = tb1[n1 % 3]
            v = vt[n1 % 3]
            ci, ca = n1 // 8, n1 % 8
            for r in range(4):
                S.dma("sp", I_dma(z.ap[r * MQ:(r + 1) * MQ, :], zall_c[ci][r * 8 * MQ + ca * MQ:r * 8 * MQ + (ca + 1) * MQ, :]),
                      reads=[bg], writes=[z.b])
            S.dma("pool", I_dma(tbl.ap[0:N2], tabP1[n1].rearrange("p (a b) -> p a b", b=N2)), writes=[tbl.b])
            for hf in range(2):
                A_ = z.ap[0:N2, hf * 512:(hf + 1) * 512]
                B_ = z.ap[0:N2, 1024 + hf * 512:1024 + (hf + 1) * 512]
                psr = nextps()
                mm(psr.ap[0:N2, :], tbl.ap[0:N2, 0, :], A_, True, False, [z.b, tbl.b], psr.b)
                mm(psr.ap[0:N2, :], tbl.ap[0:N2, 1, :], B_, False, True, [z.b, tbl.b], psr.b)
                psi = nextps()
                mm(psi.ap[0:N2, :], tbl.ap[0:N2, 2, :], B_, True, False, [z.b, tbl.b], psi.b)
                mm(psi.ap[0:N2, :], tbl.ap[0:N2, 1, :], A_, False, True, [z.b, tbl.b], psi.b)
                S.op("act", I_act(v.ap[0:N2, hf * 512:(hf + 1) * 512], psr.ap[0:N2, :], AF.Copy), [psr.b], [v.b])
                S.op("dve", I_cp(v.ap[0:N2, 1024 + hf * 512:1024 + (hf + 1) * 512], psi.ap[0:N2, :]), [psi.b], [v.b])
            S.dma("sp", I_dma(vbuf[n1], v.ap[0:N2, :]), reads=[v.b])
        ck("P1")
        stage_begin()
        T2 = alloc((64,), BF16)
        S.dma("sp", I_dma(T2.ap, tabP2), writes=[T2.b])
        vt2 = [alloc((2048,), BF16) for _ in range(4)]
        MT = alloc((8, QP), BF16)
        KB = min(8, N2)
        for kb in range(N2 // KB):
            banks = [nextps() for _ in range(4)]
            for kl in range(KB):
                k2 = kb * KB + kl
                v = vt2[k2 % 4]
                S.dma("sp", I_dma(v.ap, vbuf[:, k2, :]), writes=[v.b])
                for cc in range(8):
                    bk = banks[cc // 2]
                    o = bk.ap[:, ((cc % 2) * KB + kl) * 32:((cc % 2) * KB + kl + 1) * 32]
                    mm(o, v.ap[:, cc * 128:(cc + 1) * 128], T2.ap[:, 0:32], True, False, [v.b, T2.b], bk.b)
                    mm(o, v.ap[:, 1024 + cc * 128:1024 + (cc + 1) * 128], T2.ap[:, 32:64], False, True, [v.b, T2.b], bk.b)
            for b4 in range(4):
                src = banks[b4].ap[:, 0:2 * KB * 32].rearrange("p (c k a) -> p c k a", c=2, a=32)
                dst = MT.ap[:, 2 * b4:2 * b4 + 2, :].rearrange("p c (a k) -> p c k a", k=N2)[:, :, kb * KB:(kb + 1) * KB, :]
                for c in range(2):
                    S.op("dve" if c else "act", (I_cp(dst[:, c], src[:, c]) if c else I_act(dst[:, c], src[:, c], AF.Copy)),
                         [banks[b4].b], [MT.b])
        for g in range(QP // 512):
            S.dma("sp", I_dma(xt_view(MTd, g * 512, 512), MT.ap[:, :, g * 512:(g + 1) * 512]), reads=[MT.b])

    def qk_chunk(main_fn, gcol, G_, o, qb, cs, sn, store_fn):
        sq, t_, R_, qn, t1, t2 = qb
        ps = nextps()
        main_fn(ps)
        yield
        S.op("act", I_act(sq.ap, ps.ap, AF.Square), [ps.b], [sq.b])
        ps2 = nextps()
        mm(ps2.ap, BLK64, sq.ap, True, True, [sq.b, C16.b], ps2.b)
        yield
        S.op("dve", I_ts(t_.ap, ps2.ap, EPS, ALU.add), [ps2.b], [t_.b])
        S.op("act", I_act(t_.ap, t_.ap, AF.Ln), [t_.b], [t_.b])
        S.op("act", I_act(R_.ap, t_.ap, AF.Exp, scale=-0.5), [t_.b], [R_.b])
        S.op("dve", I_stt(qn.ap, ps.ap, G_.ap[:, gcol:gcol + 1], R_.ap, ALU.mult, ALU.mult), [ps.b, G_.b, R_.b], [qn.b])
        ps3 = nextps()
        mm(ps3.ap, SWAP, qn.ap, True, True, [qn.b, C16.b], ps3.b)
        yield
        S.op("pool", I_tt(t1.ap, qn.ap, cs.ap, ALU.mult), [qn.b, cs.b], [t1.b])
        S.op("dve", I_tt(t2.ap, ps3.ap, sn.ap, ALU.mult), [ps3.b, sn.b], [t2.b])
        S.op("dve", I_tt(o.ap, t1.ap, t2.ap, ALU.add), [t1.b, t2.b], [o.b])
        store_fn(o)

    def pipeline(gens, depth=3):
        pending = list(gens)
        active = []
        while pending or active:
            if pending and len(active) < depth:
                active.append(pending.pop(0))
            nxt = []
            for g in active:
                try:
                    next(g)
                    nxt.append(g)
                except StopIteration:
                    pass
            active = nxt

    def qk_bufs():
        return (alloc((512,), BF16), alloc((512,), F32), alloc((512,), F32), alloc((512,), BF16),
                alloc((512,), F32), alloc((512,), F32))

    def stage_diff(l, Xs):
        lam_init = 0.8 - 0.6 * math.exp(-0.3 * l)
        stage_begin()
        wq = alloc((8, 3072), BF16)
        S.dma("pool", I_dma(wq.ap, diff_w_qkv[0].rearrange("(kc p) n -> p kc n", p=128)), writes=[wq.b])
        G_ = alloc((2,), F32)
        S.dma("sp", I_dma(G_.ap, diff_g), writes=[G_.b])
        xg = [alloc((8, 512), F32) for _ in range(2)]
        hT = [alloc((8, 512), BF16) for _ in range(2)]
        nb = norm_bufs(512)
        qbs = [qk_bufs() for _ in range(3)]
        cs = [alloc((512,), F32) for _ in range(2)]
        sn = [alloc((512,), F32) for _ in range(2)]
        qo = [alloc((512,), BF16) for _ in range(6)]
        vo = [alloc((4, 1024), BF16) for _ in range(2)]
        qi = 0
        for gi, (t0, s0, sl, seq) in enumerate(groups):
            x = xg[gi % 2]
            h = hT[gi % 2]
            c_ = cs[gi % 2]
            s_ = sn[gi % 2]
            v = vo[gi % 2]
            S.dma("sp", I_dma(x.ap, xt_view(Xs, t0, 512)), writes=[x.b])
            S.dma("sp", I_dma(c_.ap, ropeC[:, t0:t0 + 512]), writes=[c_.b])
            S.dma("sp", I_dma(s_.ap, ropeS[:, t0:t0 + 512]), writes=[s_.b])
            norm(x, 512, s1[l], modv[l][0], seq, h, 0, nb)
            gens = []
            for oc in range(16):
                def main_fn(ps, oc=oc, h=h):
                    for kc in range(8):
                        mm(ps.ap, wq.ap[:, kc, oc * 128:(oc + 1) * 128], h.ap[:, kc, :], kc == 0, kc == 7, [wq.b, h.b], ps.b)

                def store_fn(o, oc=oc, t0=t0, seq=seq):
                    if oc < 8:
                        dst = QT[oc, :, t0:t0 + 512]
                    elif seq == 0:
                        dst = kin_h[oc - 8][:, t0:t0 + 512]
                    else:
                        dst = KT[oc - 8, :, t0:t0 + 512]
                    S.dma("sp", I_dma(dst, o.ap), reads=[o.b])

                gens.append(qk_chunk(main_fn, 0 if oc < 8 else 1, G_, qo[qi % 6], qbs[qi % 3], c_, s_, store_fn))
                qi += 1
            pipeline(gens)
            for tt in range(4):
                for nbk in range(2):
                    ps = nextps()
                    for kc in range(8):
                        mm(ps.ap, h.ap[:, kc, tt * 128:(tt + 1) * 128], wq.ap[:, kc, 2048 + nbk * 512:2048 + (nbk + 1) * 512],
                           kc == 0, kc == 7, [wq.b, h.b], ps.b)
                    S.op("act", I_act(v.ap[:, tt, nbk * 512:(nbk + 1) * 512], ps.ap, AF.Copy), [ps.b], [v.b])
            if seq == 0:
                for hd in range(8):
                    S.dma("sp", I_dma(vin_h[hd][t0:t0 + 512, :].rearrange("(tt p) e -> p tt e", p=128), v.ap[:, :, hd * 128:(hd + 1) * 128]),
                          reads=[v.b])
            else:
                S.dma("sp", I_dma(Vt[t0:t0 + 512, :].rearrange("(tt p) c -> p tt c", p=128), v.ap), reads=[v.b])
        S.barrier()
        bgk = Buf()
        bgv = Buf()
        for hd in range(8):
            S.op("pool", (lambda a, b: lambda e: e.collective_compute("AllGather", ALU.bypass, replica_groups=RG, ins=[a.opt()],
                                                                      outs=[b.opt()]))(kin_h[hd], kall_h[hd]), (), [bgk])
            S.op("pool", (lambda a, b: lambda e: e.collective_compute("AllGather", ALU.bypass, replica_groups=RG, ins=[a.opt()],
                                                                      outs=[b.opt()]))(vin_h[hd], vall_h[hd]), (), [bgv])
        stage_begin()
        LKM = max(SP, SS)
        Khs = [alloc((LKM,), BF16) for _ in range(2)]
        Vhs = [alloc((LKM // 128, 128), BF16) for _ in range(2)]
        Qhs = [alloc((max(QP, SS),), BF16) for _ in range(2)]
        hi_ = 0
        Et = [alloc((512,), BF16) for _ in range(4)]
        r0 = alloc((512,), F32)
        r1 = alloc((512,), F32)
        a0 = alloc((512,), F32)
        a1 = alloc((512,), F32)
        sq = alloc((512,), BF16)
        t_ = alloc((512,), F32)
        R_ = alloc((512,), F32)
        mo = [alloc((512,), BF16) for _ in range(2)]
        L4 = alloc((4,), F32)
        pr = alloc((2,), F32)
        e12 = alloc((2,), F32)
        nlam = alloc((1,), F32)
        gsub = alloc((1,), F32)
        S.dma("sp", I_dma(L4.ap[0:64, :], diff_l), writes=[L4.b])
        S.dma("sp", I_dma(gsub.ap, diff_subln), writes=[gsub.b])
        S.op("dve", I_tt(pr.ap[0:64, :], L4.ap[0:64, 0:4:2], L4.ap[0:64, 1:4:2], ALU.mult), [L4.b], [pr.b])
        psl = PSB[0]
        mm(psl.ap[:, 0:2], ONE32.ap[0:64, :], pr.ap[0:64, :], True, True, [ONE32.b, pr.b], psl.b)
        S.op("act", I_act(e12.ap, psl.ap[:, 0:2], AF.Exp), [psl.b], [e12.b])
        S.op("dve", I_tt(nlam.ap, e12.ap[:, 1:2], e12.ap[:, 0:1], ALU.subtract), [e12.b], [nlam.b])
        S.op("dve", I_ts(nlam.ap, nlam.ap, -lam_init, ALU.add), [nlam.b], [nlam.b])
        S.op("dve", I_ts(gsub.ap, gsub.ap, 1.0 - lam_init, ALU.mult), [gsub.b], [gsub.b])
        O0, O1, D0, D1 = PSB[3], PSB[4], PSB[5], PSB[6]
        sb = [PSB[0], PSB[1], PSB[2], PSB[7]]
        sc = [0]

        def nexts():
            i = sc[0]
            sc[0] = (i + 1) % 4
            return sb[i]

        mi = 0
        for (s0, sl, seq) in segs:
            LK = SP if seq == 0 else SS
            NK = LK // 128
            for hd in range(8):
                Kh, Vh, Qh = Khs[hi_ % 2], Vhs[hi_ % 2], Qhs[hi_ % 2]
                hi_ += 1
                if seq == 0:
                    for r in range(4):
                        S.dma("sp", I_dma(Kh.ap[:, r * QP:(r + 1) * QP], kall_h[hd][r * 128:(r + 1) * 128, :]),
                              reads=[bgk], writes=[Kh.b])
                    vv = vall_h[hd].rearrange("(kt p) e -> p kt e", p=128)
                    for c8 in range(0, NK, 16):
                        S.dma("sp", I_dma(Vh.ap[:, c8:c8 + 16, :], vv[:, c8:c8 + 16, :]), reads=[bgv], writes=[Vh.b])
                else:
                    S.dma("sp", I_dma(Kh.ap[:, 0:LK], KT[hd, :, s0:s0 + sl]), writes=[Kh.b])
                    vv = Vt[s0:s0 + sl, :].rearrange("(kt p) (h e) -> p kt h e", p=128, e=128)
                    S.dma("sp", I_dma(Vh.ap[:, 0:NK, :], vv[:, :, hd, :]), writes=[Vh.b])
                S.dma("sp", I_dma(Qh.ap[:, 0:sl], QT[hd, :, s0:s0 + sl]), writes=[Qh.b])
                for qg in range(sl // 512):
                    qs = slice(qg * 512, (qg + 1) * 512)

                    def smm(kt):
                        pa = nexts()
                        pb = nexts()
                        mm(pa.ap, Kh.ap[0:64, kt * 128:(kt + 1) * 128], Qh.ap[0:64, qs], True, True, [Kh.b, Qh.b], pa.b)
                        mm(pb.ap, Kh.ap[64:128, kt * 128:(kt + 1) * 128], Qh.ap[64:128, qs], True, True, [Kh.b, Qh.b], pb.b)
                        return pa, pb

                    cur = smm(0)
                    for kt in range(NK):
                        nxt = smm(kt + 1) if kt + 1 < NK else None
                        e0 = Et[(2 * kt) % 4]
                        e1 = Et[(2 * kt + 1) % 4]
                        S.op("act", I_act(e0.ap, cur[0].ap, AF.Exp, scale=0.125), [cur[0].b], [e0.b])
                        S.op("act", I_act(e1.ap, cur[1].ap, AF.Exp, scale=0.125), [cur[1].b], [e1.b])
                        mm(O0.ap, Vh.ap[:, kt, :], e0.ap, kt == 0, kt == NK - 1, [Vh.b, e0.b], O0.b)
                        mm(D0.ap, ONES, e0.ap, kt == 0, kt == NK - 1, [C16.b, e0.b], D0.b)
                        mm(O1.ap, Vh.ap[:, kt, :], e1.ap, kt == 0, kt == NK - 1, [Vh.b, e1.b], O1.b)
                        mm(D1.ap, ONES, e1.ap, kt == 0, kt == NK - 1, [C16.b, e1.b], D1.b)
                        cur = nxt
                    S.op("dve", I_rec(r0.ap, D0.ap), [D0.b], [r0.b])
                    S.op("dve", I_rec(r1.ap, D1.ap), [D1.b], [r1.b])
                    S.op("dve", I_tt(a0.ap, O0.ap, r0.ap, ALU.mult), [O0.b, r0.b], [a0.b])
                    S.op("dve", I_tt(a1.ap, O1.ap, r1.ap, ALU.mult), [O1.b, r1.b], [a1.b])
                    S.op("dve", I_stt(a0.ap, a1.ap, nlam.ap[:, 0:1], a0.ap, ALU.mult, ALU.add), [a1.b, nlam.b, a0.b], [a0.b])
                    S.op("act", I_act(sq.ap, a0.ap, AF.Square), [a0.b], [sq.b])
                    pm = nexts()
                    mm(pm.ap, OM128, sq.ap, True, True, [sq.b, C16.b], pm.b)
                    S.op("dve", I_ts(t_.ap, pm.ap, EPS, ALU.add), [pm.b], [t_.b])
                    S.op("act", I_act(t_.ap, t_.ap, AF.Ln), [t_.b], [t_.b])
                    S.op("act", I_act(R_.ap, t_.ap, AF.Exp, scale=-0.5), [t_.b], [R_.b])
                    m = mo[mi % 2]
                    mi += 1
                    S.op("dve", I_stt(m.ap, a0.ap, gsub.ap[:, 0:1], R_.ap, ALU.mult, ALU.mult), [a0.b, gsub.b, R_.b], [m.b])
                    S.dma("sp", I_dma(MTd[hd, :, s0 + qg * 512:s0 + (qg + 1) * 512], m.ap), reads=[m.b])

    def stage_swa(l, Xs):
        stage_begin()
        wq = alloc((8, 1024), BF16)
        wk2 = alloc((8, 4, 128), BF16)
        wv2 = alloc((8, 4, 128), BF16)
        S.dma("pool", I_dma(wq.ap, swa_w_qkv[0, :, 0:1024].rearrange("(kc p) n -> p kc n", p=128)), writes=[wq.b])
        for dup in range(2):
            for hk in range(4):
                S.dma("pool", I_dma(wk2.ap[:, :, hk, dup * 64:(dup + 1) * 64],
                                    swa_w_qkv[0, :, 1024 + hk * 64:1024 + (hk + 1) * 64].rearrange("(kc p) d -> p kc d", p=128)), writes=[wk2.b])
                S.dma("pool", I_dma(wv2.ap[:, :, hk, dup * 64:(dup + 1) * 64],
                                    swa_w_qkv[0, :, 1280 + hk * 64:1280 + (hk + 1) * 64].rearrange("(kc p) d -> p kc d", p=128)), writes=[wv2.b])
        G_ = alloc((2,), F32)
        S.dma("sp", I_dma(G_.ap, swa_g), writes=[G_.b])
        xg = [alloc((8, 512), F32) for _ in range(2)]
        hT = [alloc((8, 512), BF16) for _ in range(2)]
        nbs = [norm_bufs(512) for _ in range(2)]
        qbs = [qk_bufs() for _ in range(3)]
        cs = [alloc((512,), F32) for _ in range(2)]
        sn = [alloc((512,), F32) for _ in range(2)]
        qo = [alloc((512,), BF16) for _ in range(6)]
        vo = [alloc((4, 512), BF16) for _ in range(2)]
        qi = 0
        for gi, (t0, s0, sl, seq) in enumerate(groups):
            x = xg[gi % 2]
            h = hT[gi % 2]
            c_ = cs[gi % 2]
            s_ = sn[gi % 2]
            v = vo[gi % 2]
            S.dma("sp", I_dma(x.ap, xt_view(Xs, t0, 512)), writes=[x.b])
            S.dma("sp", I_dma(c_.ap, ropeC[:, t0:t0 + 512]), writes=[c_.b])
            S.dma("sp", I_dma(s_.ap, ropeS[:, t0:t0 + 512]), writes=[s_.b])
            norm(x, 512, s1[l], modv[l][0], seq, h, 0, nbs[gi % 2])
            gens = []
            for oc in range(12):
                def main_fn(ps, oc=oc, h=h):
                    for kc in range(8):
                        lw = wq.ap[:, kc, oc * 128:(oc + 1) * 128] if oc < 8 else wk2.ap[:, kc, oc - 8, :]
                        mm(ps.ap, lw, h.ap[:, kc, :], kc == 0, kc == 7, [wq.b, wk2.b, h.b], ps.b)

                def store_fn(o, oc=oc, t0=t0, seq=seq):
                    dst = QT[oc, :, t0:t0 + 512] if oc < 8 else KT[oc - 8, :, t0:t0 + 512]
                    S.dma("sp", I_dma(dst, o.ap), reads=[o.b])
                    if seq == 0 and oc >= 8 and t0 == 0:
                        S.dma("sp", I_dma(sbin[0:128, (oc - 8) * 128:(oc - 7) * 128], o.ap[:, 0:128]), reads=[o.b])
                    if seq == 0 and oc >= 8 and t0 == QP - 512:
                        S.dma("sp", I_dma(sbin[128:256, (oc - 8) * 128:(oc - 7) * 128], o.ap[:, 384:512]), reads=[o.b])

                gens.append(qk_chunk(main_fn, 0 if oc < 8 else 1, G_, qo[qi % 6], qbs[qi % 3], c_, s_, store_fn))
                qi += 1
            pipeline(gens)
            for tt in range(4):
                ps = nextps()
                for kc in range(8):
                    mm(ps.ap, h.ap[:, kc, tt * 128:(tt + 1) * 128], wv2.ap[:, kc].rearrange("p a b -> p (a b)"),
                       kc == 0, kc == 7, [wv2.b, h.b], ps.b)
                S.op("act", I_act(v.ap[:, tt, :], ps.ap, AF.Copy), [ps.b], [v.b])
            S.dma("sp", I_dma(Vt[t0:t0 + 512, 0:512].rearrange("(tt p) c -> p tt c", p=128), v.ap), reads=[v.b])
            if seq == 0 and t0 == 0:
                S.dma("sp", I_dma(sbin[0:128, 512:1024], v.ap[:, 0, :]), reads=[v.b])
            if seq == 0 and t0 == QP - 512:
                S.dma("sp", I_dma(sbin[128:256, 512:1024], v.ap[:, 3, :]), reads=[v.b])
        S.barrier()
        bg = Buf()
        S.op("pool", lambda e: e.collective_compute("AllGather", ALU.bypass, replica_groups=RG, ins=[sbin.opt()],
                                                    outs=[sball.opt()]), (), [bg])
        stage_begin()
        sk = alloc((16,), F32)
        se = alloc((16,), F32)
        S.dma("sp", I_dma(sk.ap, swa_sink), writes=[sk.b])
        S.op("act", I_act(se.ap, sk.ap, AF.Exp), [sk.b], [se.b])
        hal = alloc((4, 2, 1024), BF16)
        S.dma("sp", I_dma(hal.ap, sball.rearrange("(r e p) c -> p r e c", p=128, e=2)), reads=[bg], writes=[hal.b])
        hsel = alloc((2, 1024), BF16)
        for side, (e_, so) in enumerate(((1, 0), (0, 4))):
            S.op("dve", I_ts(hsel.ap[:, side, :], hal.ap[:, 0, e_, :], SEL.ap[:, so:so + 1], ALU.mult), [hal.b, SEL.b], [hsel.b])
            for r in range(1, 4):
                S.op("dve", I_stt(hsel.ap[:, side, :], hal.ap[:, r, e_, :], SEL.ap[:, so + r:so + r + 1], hsel.ap[:, side, :],
                                  ALU.mult, ALU.add), [hal.b, SEL.b, hsel.b], [hsel.b])
        Qg = [alloc((8, 512), BF16) for _ in range(2)]
        Kg = [alloc((4, 768), BF16) for _ in range(2)]
        Vg = [alloc((6, 512), BF16) for _ in range(2)]
        Mg = [alloc((8, 512), BF16) for _ in range(2)]
        Et = [alloc((256,), BF16) for _ in range(9)]
        rr = [alloc((256,), F32) for _ in range(3)]
        ODB = [PSB[0], PSB[1], PSB[2]]
        SH = [T(PSB[3 + i].ap[:, 0:256], PSB[3 + i].b) for i in range(5)]
        oi = 0
        si = 0
        ei = 0
        ri = 0
        for gi, (t0, s0, sl, seq) in enumerate(groups):
            q = Qg[gi % 2]
            k_ = Kg[gi % 2]
            v = Vg[gi % 2]
            m = Mg[gi % 2]
            S.dma("sp", I_dma(q.ap, xt_view(QT, t0, 512)), writes=[q.b])
            S.dma("sp", I_dma(k_.ap[:, :, 128:640], KT[0:4, :, t0:t0 + 512].rearrange("c p t -> p c t")), writes=[k_.b])
            S.dma("sp", I_dma(v.ap[:, 1:5, :], Vt[t0:t0 + 512, 0:512].rearrange("(tt p) c -> p tt c", p=128)), writes=[v.b])
            lin = t0 > s0
            rin = t0 + 512 < s0 + sl
            if lin:
                S.dma("sp", I_dma(k_.ap[:, :, 0:128], KT[0:4, :, t0 - 128:t0].rearrange("c p t -> p c t")), writes=[k_.b])
                S.dma("sp", I_dma(v.ap[:, 0, :], Vt[t0 - 128:t0, 0:512]), writes=[v.b])
            elif seq == 0:
                S.op("dve", I_cp(k_.ap[:, :, 0:128], hsel.ap[:, 0, 0:512].rearrange("p (a b) -> p a b", b=128)), [hsel.b], [k_.b])
                S.op("dve", I_cp(v.ap[:, 0, :], hsel.ap[:, 0, 512:1024]), [hsel.b], [v.b])
            if rin:
                S.dma("sp", I_dma(k_.ap[:, :, 640:768], KT[0:4, :, t0 + 512:t0 + 640].rearrange("c p t -> p c t")), writes=[k_.b])
                S.dma("sp", I_dma(v.ap[:, 5, :], Vt[t0 + 512:t0 + 640, 0:512]), writes=[v.b])
            elif seq == 0:
                S.op("dve", I_cp(k_.ap[:, :, 640:768], hsel.ap[:, 1, 0:512].rearrange("p (a b) -> p a b", b=128)), [hsel.b], [k_.b])
                S.op("dve", I_cp(v.ap[:, 5, :], hsel.ap[:, 1, 512:1024]), [hsel.b], [v.b])
            for nbk in range(4):
                offs = []
                for o in (-1, 0, 1):
                    if (o == -1 and nbk == 0 and not lin and seq != 0) or (o == 1 and nbk == 3 and not rin and seq != 0):
                        continue
                    edge = (o == -1 and nbk == 0 and not lin) or (o == 1 and nbk == 3 and not rin)
                    offs.append((o, edge))
                for hk in range(4):
                    for half in range(2):
                        pr_ = slice(half * 64, half * 64 + 64)
                        pOD = ODB[oi % 3]
                        oi += 1
                        es = []
                        for ii, (o, edge) in enumerate(offs):
                            pS = SH[si % 5]
                            si += 1
                            kc0 = 128 + (nbk + o) * 128
                            mm(pS.ap, k_.ap[pr_, hk, kc0:kc0 + 128], q.ap[pr_, 2 * hk:2 * hk + 2, nbk * 128:(nbk + 1) * 128],
                               True, True, [k_.b, q.b], pS.b)
                            e = Et[ei % 9]
                            ei += 1
                            S.op("act", I_act(e.ap, pS.ap, AF.Exp, scale=0.125), [pS.b], [e.b])
                            if o != 0:
                                mk = (MASKE if edge else MASK).ap[:, 0 if o == -1 else 1, :]
                                S.op("dve", I_tt(e.ap, e.ap, mk, ALU.mult), [e.b, MASK.b, MASKE.b], [e.b])
                            es.append((e, o))
                        for ii, (e, o) in enumerate(es):
                            mm(pOD.ap[:, 0:256], v.ap[:, 1 + nbk + o, hk * 128:(hk + 1) * 128], e.ap, ii == 0, ii == len(es) - 1,
                               [v.b, e.b], pOD.b)
                        for ii, (e, o) in enumerate(es):
                            mm(pOD.ap[:, 256:512], ONES, e.ap, ii == 0, ii == len(es) - 1, [C16.b, e.b], pOD.b)
                        r_ = rr[ri % 3]
                        ri += 1
                        for jj in range(2):
                            qh = hk * 4 + half + 2 * jj
                            S.op("dve", I_ts(r_.ap[:, jj * 128:(jj + 1) * 128], pOD.ap[:, 256 + jj * 128:256 + (jj + 1) * 128], se.ap[:, qh:qh + 1], ALU.add),
                                 [pOD.b, se.b], [r_.b])
                        S.op("dve", I_rec(r_.ap, r_.ap), [r_.b], [r_.b])
                        S.op("dve", I_tt(m.ap[pr_, 2 * hk:2 * hk + 2, nbk * 128:(nbk + 1) * 128],
                                         pOD.ap[pr_, 0:256].rearrange("p (a b) -> p a b", b=128),
                                         r_.ap[pr_, :].rearrange("p (a b) -> p a b", b=128), ALU.mult), [pOD.b, r_.b], [m.b])
            S.dma("sp", I_dma(xt_view(MTd, t0, 512), m.ap), reads=[m.b])

    stage_mod()
    stage_init()
    try:
        for l in range(NL):
            kind, j = l % 3, l // 3
            if kind == 0:
                stage_fourier(l, j, XTb)
                ck("mix")
                stage_tail(l, fnet_w[j], fnet_b[j], XTb, XTa)
            elif kind == 1:
                stage_swa(l, XTb)
                ck("mix")
                stage_tail(l, swa_w_o[0], None, XTb, XTa)
            else:
                stage_diff(l, XTb)
                ck("mix")
                stage_tail(l, diff_w_o[0], None, XTb, XTa)
            ck("tail")
            stage_bnd(XTa)
            ck("bnd")
            stage_ffn(l, XTa, XTb)
    except _Stop:
        pass
    stage_final(XTa if STOP in ("tail", "bnd") else XTb)
    S.barrier()
    S.emit()
    S.stack.close()
    return nc


def _bf(a):
    return np.ascontiguousarray(a).astype(ml_dtypes.bfloat16)


def make_tables(cfg):
    SP, SS, NSS = cfg["SP"], cfg["SS"], cfg["NSS"]
    QP = SP // 4
    N2 = SP // 128
    NTS = SS // 128
    NKG = SS // 512
    tb = {}
    a = np.arange(256)
    ang = 2 * np.pi * np.outer(a, a) / 256
    cg = np.cos(ang) / 16.0
    sg = np.sin(ang) / 16.0
    ch = np.zeros((128, 2, 512))
    for jj in range(2):
        ch[:, jj, 0:256] = cg[jj * 128:(jj + 1) * 128]
        ch[:, jj, 256:512] = sg[jj * 128:(jj + 1) * 128]
    tb["chCS"] = _bf(ch.reshape(128, 1024))
    n = np.arange(SS)
    angS = 2 * np.pi * (np.outer(n, n) % SS) / SS
    CS_ = np.cos(angS) / np.sqrt(SS)
    SS_ = -np.sin(angS) / np.sqrt(SS)
    t = np.zeros((NKG, 128, NTS, 2, 512))
    for kg in range(NKG):
        for nt in range(NTS):
            t[kg, :, nt, 0, :] = CS_[nt * 128:(nt + 1) * 128, kg * 512:(kg + 1) * 512]
            t[kg, :, nt, 1, :] = SS_[nt * 128:(nt + 1) * 128, kg * 512:(kg + 1) * 512]
    tb["tabS"] = _bf(t.reshape(NKG, 128, NTS * 2 * 512))
    n1 = np.arange(128)[:, None, None]
    n2 = np.arange(N2)[None, :, None]
    k2 = np.arange(N2)[None, None, :]
    angE = 2 * np.pi * (((n1 + 128 * n2) * k2) % SP) / SP
    Ec = np.cos(angE) / np.sqrt(SP)
    Es = np.sin(angE) / np.sqrt(SP)
    t1 = np.stack([Ec, -Es, -Ec], axis=2)
    tb["tabP1"] = _bf(t1.reshape(128, N2, 3 * N2))
    kk = np.arange(128)[:, None]
    qq = np.arange(128)[None, :]
    m0 = (qq <= kk).astype(np.float32)
    m1 = (kk <= qq).astype(np.float32)
    mk = np.stack([np.concatenate([m0, m0], 1), np.concatenate([m1, m1], 1)], 1)
    tb["swamask"] = _bf(mk.reshape(128, 512))
    tb["ident"] = np.eye(128, dtype=np.float32)
    c16 = np.zeros((128, 1152), np.float32)
    c16[:, 0:128] = 1.0 / 1024
    blk = np.zeros((128, 128), np.float32)
    blk[0:64, 0:64] = 1.0 / 64
    blk[64:128, 64:128] = 1.0 / 64
    c16[:, 128:256] = blk
    c16[:, 256:384] = 1.0 / 128
    c16[:, 384:512] = 1.0
    sw = np.zeros((128, 128), np.float32)
    for m_ in range(128):
        partner = m_ + 32 if (m_ % 64) < 32 else m_ - 32
        sw[partner, m_] = 1.0
    c16[:, 512:640] = sw
    c16[:, 640:1152] = 1.0
    tb["c16"] = _bf(c16)
    return tb


def core_tables(cfg, core):
    SP, SS, NSS = cfg["SP"], cfg["SS"], cfg["NSS"]
    QP = SP // 4
    q = core % 4
    tb = {}
    pos = np.concatenate([np.arange(q * QP, (q + 1) * QP)] + [np.arange(SS)] * NSS).astype(np.float32)
    inv = (1.0 / (np.float32(10000.0) ** (np.arange(0, 64, 2, dtype=np.float32) / np.float32(64)))).astype(np.float32)
    ang = (pos[None, :] * inv[:, None]).astype(np.float32)
    c = np.cos(ang).astype(np.float32)
    s = np.sin(ang).astype(np.float32)
    tb["ropeC"] = np.ascontiguousarray(np.tile(c, (4, 1)))
    tb["ropeS"] = np.ascontiguousarray(np.concatenate([-s, s, -s, s], 0))
    n1 = np.arange(128)[:, None]
    k1 = (32 * q + np.arange(32))[None, :]
    a2 = 2 * np.pi * ((n1 * k1) % 128) / 128
    tb["tabP2"] = _bf(np.concatenate([np.cos(a2), np.sin(a2)], 1))
    sel = np.zeros((128, 12), np.float32)
    if q > 0:
        sel[:, q - 1] = 1.0
        sel[:, 8] = 1.0
    if q < 3:
        sel[:, 4 + q + 1] = 1.0
        sel[:, 9] = 1.0
    tb["sel"] = sel
    return tb


def fm(v):
    return np.ascontiguousarray(np.asarray(v).reshape(8, 128).T)


def prep_inputs(cfg, inp):
    SP, SS, NSS = cfg["SP"], cfg["SS"], cfg["NSS"]
    QP = SP // 4
    shared = dict(make_tables(cfg))
    NLW = cfg.get("NLW", 4)
    for k in ("ada_w", "fnet_w", "fnet_b", "swa_w_qkv", "swa_w_o", "diff_w_qkv", "diff_w_o", "ffn_w_gate", "ffn_w_up", "ffn_w_down"):
        a = np.asarray(inp[k], dtype=np.float32)
        if k in ("ada_w", "ffn_w_gate", "ffn_w_up", "ffn_w_down"):
            a = a[:NLW]
        shared[k] = np.ascontiguousarray(a)
    ab = np.asarray(inp["ada_b"], np.float32)
    shared["ada_bT"] = np.ascontiguousarray(ab.reshape(4, 48, 128).transpose(2, 0, 1).reshape(128, 192))
    ng = np.zeros((128, 64), np.float32)
    for l in range(4):
        ng[:, l * 8:(l + 1) * 8] = fm(inp["norm1_g"][l])
        ng[:, 32 + l * 8:32 + (l + 1) * 8] = fm(inp["norm2_g"][l])
    shared["ng"] = ng
    shared["swa_g"] = np.ascontiguousarray(np.stack([np.tile(inp["swa_q_g"][0], 2), np.tile(inp["swa_k_g"][0], 2)], 1), dtype=np.float32)
    shared["swa_sink"] = np.ascontiguousarray(np.tile(np.asarray(inp["swa_sink"][0], np.float32)[None, :], (128, 1)))
    shared["diff_g"] = np.ascontiguousarray(np.stack([np.tile(inp["diff_q_g"][0], 2), np.tile(inp["diff_k_g"][0], 2)], 1), dtype=np.float32)
    shared["diff_l"] = np.ascontiguousarray(np.stack([inp["diff_lq1"][0], inp["diff_lk1"][0], inp["diff_lq2"][0], inp["diff_lk2"][0]], 1),
                                            dtype=np.float32)
    shared["diff_subln"] = np.ascontiguousarray(np.asarray(inp["diff_subln_g"][0], np.float32).reshape(128, 1))
    cv = np.zeros((128, 4 * NFC * 4), np.float32)
    for l in range(4):
        for j in range(3):
            cv[:, (l * NFC * 4 + j)::4][:, 0:NFC] = np.asarray(inp["ffn_conv_w"][l][j], np.float32).reshape(NFC, 128).T
        cv[:, (l * NFC * 4 + 3)::4][:, 0:NFC] = np.asarray(inp["ffn_conv_b"][l], np.float32).reshape(NFC, 128).T
    shared["ffn_convT"] = cv
    maps = []
    for core in range(8):
        b, q = core // 4, core % 4
        m = dict(shared)
        m.update(core_tables(cfg, core))
        xs = [inp["x_prompt"][b, q * QP:(q + 1) * QP]] + [inp["x_sample"][core * NSS + i] for i in range(NSS)]
        m["xin"] = np.ascontiguousarray(np.concatenate(xs, 0), dtype=np.float32)
        cs = np.stack([inp["c_prompt"][b]] + [inp["c_sample"][core * NSS + i] for i in range(NSS)], 0)
        m["cT"] = np.ascontiguousarray(cs.reshape(1 + NSS, 8, 128).transpose(2, 1, 0).reshape(128, 8 * (1 + NSS)), dtype=np.float32)
        maps.append(m)
    return maps


def assemble(cfg, results, B_, SB_):
    SP, SS, NSS = cfg["SP"], cfg["SS"], cfg["NSS"]
    QP = SP // 4
    yp = np.zeros((B_, SP, D), np.float32)
    ys = np.zeros((SB_, SS, D), np.float32)
    for core in range(8):
        y = results[core]["yout"]
        b, q = core // 4, core % 4
        yp[b, q * QP:(q + 1) * QP] = y[0:QP]
        for i in range(NSS):
            ys[core * NSS + i] = y[QP + i * SS:QP + (i + 1) * SS]
    return yp, ys


FULL = dict(SP=16384, SS=2048, NSS=4, NL=4)


def kernel(**inputs):
    cfg = FULL
    nc = build(cfg)
    maps = prep_inputs(cfg, inputs)
    res = run_bass_kernel_spmd(nc, maps, core_ids=list(range(8)))
    return assemble(cfg, res.results, 2, 32)
```
